# Optimizing a Trainium2 kernel written in Bass

```python
import math, functools
import jax, jax.numpy as jnp
from jax import lax
import numpy as np

D_MODEL = 1024
BATCH = 1
SEQ = 16384
DEPTH = 2

GRID_W = 64
CTX_LEN = 256
Q_BLOCK = 128
RET_CHUNK = 128
ROPE_THETA = 10000.0
EPS = 1e-6

GQA_HEADS = 8
GQA_KV_HEADS = 2
GQA_HEAD_DIM = 64
RET_HEADS = 4
RET_QK_DIM = 64
RET_V_DIM = 128
MLA_HEADS = 8
MLA_Q_RANK = 384
MLA_KV_RANK = 256
MLA_NOPE_DIM = 64
MLA_ROPE_DIM = 32
MLA_V_DIM = 64
DIFF_HEADS = 4
DIFF_HEAD_DIM = 64
N_BRANCHES = 4
BRANCH_WIDTH = 512
FFN_HIDDEN = -(-8 * D_MODEL // (3 * 256)) * 256

IN_SPLITS = (
    GQA_HEADS * GQA_HEAD_DIM, GQA_KV_HEADS * GQA_HEAD_DIM, GQA_KV_HEADS * GQA_HEAD_DIM,
    RET_HEADS * RET_QK_DIM, RET_HEADS * RET_QK_DIM, RET_HEADS * RET_V_DIM, RET_HEADS * RET_V_DIM,
    MLA_Q_RANK, MLA_KV_RANK, MLA_ROPE_DIM,
    DIFF_HEADS * 2 * DIFF_HEAD_DIM, DIFF_HEADS * 2 * DIFF_HEAD_DIM, DIFF_HEADS * 2 * DIFF_HEAD_DIM,
    N_BRANCHES * D_MODEL,
)
IN_WIDTH = sum(IN_SPLITS)

kernel_name = 'hybrid_gqa_retention_mla_diffattn_prefix_ctx'


def rms_norm(x, gain=None):
    x32 = x.astype(jnp.float32)
    y = x32 * lax.rsqrt(jnp.mean(x32 * x32, axis=-1, keepdims=True) + EPS)
    if gain is not None:
        y = y * gain.astype(jnp.float32)
    return y.astype(x.dtype)


def axial_rope(n_rows, dim):
    quarter = dim // 4
    freqs = ROPE_THETA ** (-jnp.arange(quarter, dtype=jnp.float32) / quarter)
    row = jnp.repeat(jnp.arange(n_rows, dtype=jnp.float32), GRID_W)
    col = jnp.tile(jnp.arange(GRID_W, dtype=jnp.float32), n_rows)
    ang = jnp.concatenate([row[:, None] * freqs, col[:, None] * freqs], axis=-1)
    return jnp.cos(ang), jnp.sin(ang)


def apply_rope(x, rope):
    cos, sin = rope
    S, half = cos.shape
    shape = (S,) + (1,) * (x.ndim - 3) + (half,)
    cos = cos.reshape(shape).astype(x.dtype)
    sin = sin.reshape(shape).astype(x.dtype)
    xp = x.reshape(x.shape[:-1] + (half, 2))
    x0, x1 = xp[..., 0], xp[..., 1]
    return jnp.stack([x0 * cos - x1 * sin, x0 * sin + x1 * cos], axis=-1).reshape(x.shape)


def split_cols(a):
    out, off = [], 0
    for w in IN_SPLITS:
        out.append(a[..., off:off + w])
        off += w
    return out


def sweep_queries(block_fn, q):
    B, S = q.shape[:2]
    nb = S // Q_BLOCK
    qs = jnp.moveaxis(q.reshape((B, nb, Q_BLOCK) + q.shape[2:]), 1, 0)
    out = lax.map(block_fn, qs)
    return jnp.moveaxis(out, 0, 1).reshape((B, S) + out.shape[3:])


def softmax_block(qb, k, v, scale):
    s = jnp.einsum('bqngd,bsnd->bngqs', qb, k).astype(jnp.float32) * scale
    p = jax.nn.softmax(s, axis=-1).astype(v.dtype)
    return jnp.einsum('bngqs,bsne->bqnge', p, v)


def diff_block(qb, k, v, lam, scale):
    s = jnp.einsum('bqhmd,bshmd->bhmqs', qb, k).astype(jnp.float32) * scale
    p = jax.nn.softmax(s, axis=-1)
    a = p[:, :, 0] - lam * p[:, :, 1]
    return jnp.einsum('bhqs,bshe->bqhe', a.astype(v.dtype), v)


def two_stream_attention(block_fn, q_l, k_l, v_l, q_c, k_c, v_c, need_ctx):
    k_all = jnp.concatenate([k_l, k_c], axis=1)
    v_all = jnp.concatenate([v_l, v_c], axis=1)
    o_l = sweep_queries(lambda qb: block_fn(qb, k_all, v_all), q_l)
    o_c = sweep_queries(lambda qb: block_fn(qb, k_c, v_c), q_c) if need_ctx else None
    return o_l, o_c


def flat_heads(o):
    return o.reshape(o.shape[:2] + (-1,))


def gqa_branch(pl, pc, qk_gain, rope, need_ctx):
    def prep(q, k, v, rope_tab):
        B, S = q.shape[:2]
        q = rms_norm(q.reshape(B, S, GQA_KV_HEADS, GQA_HEADS // GQA_KV_HEADS, GQA_HEAD_DIM), qk_gain[0])
        k = rms_norm(k.reshape(B, S, GQA_KV_HEADS, GQA_HEAD_DIM), qk_gain[1])
        v = v.reshape(B, S, GQA_KV_HEADS, GQA_HEAD_DIM)
        if rope_tab is not None:
            q = apply_rope(q, rope_tab)
            k = apply_rope(k, rope_tab)
        return q, k, v
    block = functools.partial(softmax_block, scale=GQA_HEAD_DIM ** -0.5)
    o_l, o_c = two_stream_attention(block, *prep(*pl, rope), *prep(*pc, None), need_ctx)
    return flat_heads(o_l), (flat_heads(o_c) if need_ctx else None)


def retention_chunkwise(q, k, v, log_gamma, state0):
    out_dtype = v.dtype
    q, k, v = (a.astype(jnp.float32) for a in (q, k, v))
    log_gamma = log_gamma.astype(jnp.float32)
    B, N, H, _ = q.shape
    nc = N // RET_CHUNK
    idx = jnp.arange(RET_CHUNK, dtype=jnp.float32)
    rel = idx[:, None] - idx[None, :]
    intra = jnp.where(rel >= 0, jnp.exp(log_gamma[:, None, None] * jnp.maximum(rel, 0.0)), 0.0)
    q_decay = jnp.exp(log_gamma[None, :] * (idx[:, None] + 1.0))
    k_decay = jnp.exp(log_gamma[None, :] * (RET_CHUNK - 1.0 - idx[:, None]))
    chunk_decay = jnp.exp(log_gamma * RET_CHUNK)
    to_chunks = lambda a: jnp.moveaxis(a.reshape((B, nc, RET_CHUNK) + a.shape[2:]), 1, 0)

    def step(state, qkv):
        qc, kc, vc = qkv
        scores = jnp.einsum('bihd,bjhd->bhij', qc, kc) * intra
        o = jnp.einsum('bhij,bjhe->bihe', scores, vc)
        o = o + jnp.einsum('bihd,bhde->bihe', qc, state) * q_decay[None, :, :, None]
        state = state * chunk_decay[None, :, None, None] + jnp.einsum(
            'bjhd,bjhe->bhde', kc * k_decay[None, :, :, None], vc)
        return state, o

    state, out = lax.scan(step, state0, (to_chunks(q), to_chunks(k), to_chunks(v)))
    out = jnp.moveaxis(out, 0, 1).reshape(B, N, H, -1)
    return out.astype(out_dtype), state


def retention_branch(pl, pc, log_decay, rope, need_ctx):
    def prep(q, k, v, rope_tab):
        B, S = q.shape[:2]
        q = q.reshape(B, S, RET_HEADS, RET_QK_DIM)
        k = k.reshape(B, S, RET_HEADS, RET_QK_DIM) * (RET_QK_DIM ** -0.5)
        v = v.reshape(B, S, RET_HEADS, RET_V_DIM)
        if rope_tab is not None:
            q = apply_rope(q, rope_tab)
            k = apply_rope(k, rope_tab)
        return q, k, v
    ql, kl, vl = prep(*pl[:3], rope)
    qc, kc, vc = prep(*pc[:3], None)
    flip = lambda a: jnp.flip(a, axis=1)
    zero = jnp.zeros((ql.shape[0], RET_HEADS, RET_QK_DIM, RET_V_DIM), jnp.float32)
    oc_f, s_f = retention_chunkwise(qc, kc, vc, log_decay[0], zero)
    oc_b, s_b = retention_chunkwise(flip(qc), flip(kc), flip(vc), log_decay[1], zero)
    ol_f, _ = retention_chunkwise(ql, kl, vl, log_decay[0], s_f)
    ol_b, _ = retention_chunkwise(flip(ql), flip(kl), flip(vl), log_decay[1], s_b)

    def finish(o, g):
        return flat_heads(rms_norm(o)) * jax.nn.silu(g)
    o_l = finish(ol_f + flip(ol_b), pl[3])
    o_c = finish(oc_f + flip(oc_b), pc[3]) if need_ctx else None
    return o_l, o_c


def mla_branch(pl, pc, cq_gain, ckv_gain, w_uq, w_ukv, qk_gain, rope, need_ctx):
    nope = MLA_NOPE_DIM

    def prep(c_q, c_kv, k_rope, rope_tab):
        B, S = c_q.shape[:2]
        q = (rms_norm(c_q, cq_gain) @ w_uq).reshape(B, S, MLA_HEADS, MLA_NOPE_DIM + MLA_ROPE_DIM)
        kv = (rms_norm(c_kv, ckv_gain) @ w_ukv).reshape(B, S, MLA_HEADS, MLA_NOPE_DIM + MLA_V_DIM)
        q_nope = rms_norm(q[..., :nope], qk_gain[0, :nope])
        q_rope = rms_norm(q[..., nope:], qk_gain[0, nope:])
        k_nope = rms_norm(kv[..., :nope], qk_gain[1, :nope])
        v = kv[..., nope:]
        k_rope = rms_norm(k_rope, qk_gain[1, nope:])
        if rope_tab is not None:
            q_rope = apply_rope(q_rope, rope_tab)
            k_rope = apply_rope(k_rope, rope_tab)
        k_rope = jnp.broadcast_to(k_rope[:, :, None], (B, S, MLA_HEADS, MLA_ROPE_DIM))
        q = jnp.concatenate([q_nope, q_rope], axis=-1)[:, :, :, None]
        k = jnp.concatenate([k_nope, k_rope], axis=-1)
        return q, k, v
    block = functools.partial(softmax_block, scale=(MLA_NOPE_DIM + MLA_ROPE_DIM) ** -0.5)
    o_l, o_c = two_stream_attention(block, *prep(*pl, rope), *prep(*pc, None), need_ctx)
    return flat_heads(o_l), (flat_heads(o_c) if need_ctx else None)


def diff_branch(pl, pc, qk_gain, lam_vecs, subln_gain, lam_init, rope, need_ctx):
    def prep(q, k, v, rope_tab):
        B, S = q.shape[:2]
        q = rms_norm(q.reshape(B, S, DIFF_HEADS, 2, DIFF_HEAD_DIM), qk_gain[0])
        k = rms_norm(k.reshape(B, S, DIFF_HEADS, 2, DIFF_HEAD_DIM), qk_gain[1])
        v = v.reshape(B, S, DIFF_HEADS, 2 * DIFF_HEAD_DIM)
        if rope_tab is not None:
            q = apply_rope(q, rope_tab)
            k = apply_rope(k, rope_tab)
        return q, k, v
    lv = lam_vecs.astype(jnp.float32)
    lam = jnp.exp(jnp.sum(lv[0] * lv[1])) - jnp.exp(jnp.sum(lv[2] * lv[3])) + lam_init
    block = functools.partial(diff_block, lam=lam, scale=DIFF_HEAD_DIM ** -0.5)
    o_l, o_c = two_stream_attention(block, *prep(*pl, rope), *prep(*pc, None), need_ctx)

    def finish(o):
        return flat_heads(rms_norm(o, subln_gain) * (1.0 - lam_init))
    return finish(o_l), (finish(o_c) if need_ctx else None)


def gated_merge(branches, gate_logits, w_branch, w_out):
    stacked = jnp.stack(branches, axis=-2)
    proj = jnp.einsum('bsnw,nwd->bsnd', stacked, w_branch)
    gates = jax.nn.sigmoid(gate_logits.reshape(proj.shape))
    return jnp.sum(gates * proj, axis=-2) @ w_out


def adaln(cond, w_mod, b_mod):
    return (jax.nn.silu(cond) @ w_mod + b_mod).reshape(cond.shape[0], 6, -1)


def modulate(x, shift, scale):
    return x * (1.0 + scale[:, None]) + shift[:, None]


def swiglu(h, w_in, w_out):
    gate, up = jnp.split(h @ w_in, 2, axis=-1)
    return (jax.nn.silu(gate) * up) @ w_out


def setup_inputs(seed: int = 0) -> dict:
    key = jax.random.key(seed)
    ks = jax.random.split(key, 24)
    f32 = jnp.float32
    nrm = lambda k, shape: jax.random.normal(k, shape, f32)
    dense = lambda k, shape, fan_in: nrm(k, shape) * (fan_in ** -0.5)
    gain = lambda k, shape: 1.0 + 0.01 * nrm(k, shape)
    base_decay = jnp.log(2.0 ** (5.0 + jnp.arange(RET_HEADS, dtype=f32)) - 1.0)
    return {
        'x': nrm(ks[0], (BATCH, SEQ, D_MODEL)),
        'c': nrm(ks[1], (BATCH, D_MODEL)),
        'ctx': nrm(ks[2], (BATCH, CTX_LEN, D_MODEL)),
        'c_ctx': nrm(ks[3], (D_MODEL,)),
        'w_mod': dense(ks[4], (DEPTH, D_MODEL, 6 * D_MODEL), D_MODEL),
        'b_mod': 0.01 * nrm(ks[5], (DEPTH, 6 * D_MODEL)),
        'norm_gain': gain(ks[6], (DEPTH, 2, D_MODEL)),
        'w_in': dense(ks[7], (DEPTH, D_MODEL, IN_WIDTH), D_MODEL),
        'gqa_qk_gain': gain(ks[8], (DEPTH, 2, GQA_HEAD_DIM)),
        'ret_decay': base_decay + 0.01 * nrm(ks[9], (DEPTH, 2, RET_HEADS)),
        'mla_cq_gain': gain(ks[10], (DEPTH, MLA_Q_RANK)),
        'mla_ckv_gain': gain(ks[11], (DEPTH, MLA_KV_RANK)),
        'mla_w_uq': dense(ks[12], (DEPTH, MLA_Q_RANK, MLA_HEADS * (MLA_NOPE_DIM + MLA_ROPE_DIM)), MLA_Q_RANK),
        'mla_w_ukv': dense(ks[13], (DEPTH, MLA_KV_RANK, MLA_HEADS * (MLA_NOPE_DIM + MLA_V_DIM)), MLA_KV_RANK),
        'mla_qk_gain': gain(ks[14], (DEPTH, 2, MLA_NOPE_DIM + MLA_ROPE_DIM)),
        'diff_qk_gain': gain(ks[15], (DEPTH, 2, DIFF_HEAD_DIM)),
        'diff_lambda': 0.1 * nrm(ks[16], (DEPTH, 4, DIFF_HEAD_DIM)),
        'diff_subln_gain': gain(ks[17], (DEPTH, 2 * DIFF_HEAD_DIM)),
        'w_branch': dense(ks[18], (DEPTH, N_BRANCHES, BRANCH_WIDTH, D_MODEL), BRANCH_WIDTH),
        'w_out': dense(ks[19], (DEPTH, D_MODEL, D_MODEL), D_MODEL),
        'w_ffn_in': dense(ks[20], (DEPTH, D_MODEL, 2 * FFN_HIDDEN), D_MODEL),
        'w_ffn_out': dense(ks[21], (DEPTH, FFN_HIDDEN, D_MODEL), FFN_HIDDEN),
    }


def reference(x, c, ctx, c_ctx, w_mod, b_mod, norm_gain, w_in, gqa_qk_gain, ret_decay,
              mla_cq_gain, mla_ckv_gain, mla_w_uq, mla_w_ukv, mla_qk_gain, diff_qk_gain,
              diff_lambda, diff_subln_gain, w_branch, w_out, w_ffn_in, w_ffn_out):
    S = x.shape[1]
    ROWS = S // GRID_W
    rope64 = axial_rope(ROWS, 64)
    rope32 = axial_rope(ROWS, MLA_ROPE_DIM)
    for l in range(DEPTH):
        need_ctx = l < DEPTH - 1
        lam_init = 0.8 - 0.6 * math.exp(-0.3 * l)
        mod = adaln(c, w_mod[l], b_mod[l])
        mod_c = adaln(c_ctx[None], w_mod[l], b_mod[l])

        h = modulate(rms_norm(x, norm_gain[l, 0]), mod[:, 0], mod[:, 1])
        hc = modulate(rms_norm(ctx, norm_gain[l, 0]), mod_c[:, 0], mod_c[:, 1])
        pl = split_cols(h @ w_in[l])
        pc = split_cols(hc @ w_in[l])
        a_l, a_c = gqa_branch(pl[0:3], pc[0:3], gqa_qk_gain[l], rope64, need_ctx)
        b_l, b_c = retention_branch(pl[3:7], pc[3:7], jax.nn.log_sigmoid(ret_decay[l].astype(jnp.float32)),
                                    rope64, need_ctx)
        c_l, c_c = mla_branch(pl[7:10], pc[7:10], mla_cq_gain[l], mla_ckv_gain[l], mla_w_uq[l],
                              mla_w_ukv[l], mla_qk_gain[l], rope32, need_ctx)
        d_l, d_c = diff_branch(pl[10:13], pc[10:13], diff_qk_gain[l], diff_lambda[l], diff_subln_gain[l],
                               lam_init, rope64, need_ctx)
        y = gated_merge([a_l, b_l, c_l, d_l], pl[13], w_branch[l], w_out[l])
        x = x + mod[:, 2][:, None] * y

        h2 = modulate(rms_norm(x, norm_gain[l, 1]), mod[:, 3], mod[:, 4])
        x = x + mod[:, 5][:, None] * swiglu(h2, w_ffn_in[l], w_ffn_out[l])

        if need_ctx:
            yc = gated_merge([a_c, b_c, c_c, d_c], pc[13], w_branch[l], w_out[l])
            ctx = ctx + mod_c[:, 2][:, None] * yc
            hc2 = modulate(rms_norm(ctx, norm_gain[l, 1]), mod_c[:, 3], mod_c[:, 4])
            ctx = ctx + mod_c[:, 5][:, None] * swiglu(hc2, w_ffn_in[l], w_ffn_out[l])
    return x
```

```python
import math
import numpy as np
import ml_dtypes
import concourse.bass as bass
import concourse.mybir as mybir
from concourse.bass_utils import run_bass_kernel_spmd
from contextlib import ExitStack

F32 = mybir.dt.float32
BF16 = mybir.dt.bfloat16
AF = mybir.ActivationFunctionType
ALU = mybir.AluOpType

NCORES = 8
D = 1024
SEQ = 16384
NLAT = SEQ // NCORES
NCTX = 256
NT = NLAT + NCTX
GROUPS = [(0, 512), (512, 512), (1024, 512), (1536, 512), (2048, 256)]
DEPTH = 2
FFN = 2816
EPS = 1e-6

O_AQ, O_AK, O_AV = 0, 512, 640
O_BQ, O_BK, O_BV, O_BG = 768, 1024, 1280, 1792
O_CQ, O_CKV, O_CKR = 2304, 2688, 2944
O_DQ, O_DK, O_DV = 2976, 3488, 4000
O_G = 4512
IN_W = 8608
QA, QB, QC, QD = 0, 512, 768, 1536
NQ = 2048
KA, KCN, KCR, KD, KB = 0, 128, 640, 672, 1184
NK = 1440
VA, VC, VD, VB, VBK = 0, 128, 640, 1152, 1664
NV = 1920
NG = 4608
GV_AQ, GV_AK, GV_DQ, GV_DK, GV_CQ, GV_CKV, GV_MQ, GV_MKN, GV_MKR, GV_SC96, GV_EPS, GV_SUBLN = 0, 1, 2, 3, 4, 7, 9, 10, 11, 12, 13, 14
NGV = 16
C_ONES, C_BD64, C_BD96, C_PSW, C_ID = 0, 128, 256, 384, 512
NCONST = 640


class Buf:
    __slots__ = ("t", "lw", "rd", "name")

    def __init__(self, t, name=""):
        self.t = t
        self.lw = {}
        self.rd = {}
        self.name = name

    def __getitem__(self, idx):
        return self.t[idx]


class Ring:
    def __init__(self, bufs):
        self.bufs = bufs
        self.i = 0

    def next(self):
        b = self.bufs[self.i % len(self.bufs)]
        self.i += 1
        return b


class Sync:
    def __init__(self, nc, es, n_dma_sems=24):
        self.nc = nc
        self.engs = {"pe": nc.tensor, "act": nc.scalar, "dve": nc.vector,
                     "pool": nc.gpsimd, "sp": nc.sync}
        self.sems = {}
        self.cnt = {}
        for k in self.engs:
            self.sems[k] = es.enter_context(nc.semaphore("s_" + k))
            self.cnt[k] = 0
        self.dsems = []
        for i in range(n_dma_sems):
            key = "d%d" % i
            self.sems[key] = es.enter_context(nc.semaphore("s_" + key))
            self.cnt[key] = 0
            self.dsems.append(key)
        self.dnext = 0
        self.seen = {k: {} for k in self.engs}
        self.nwaits = 0
        self.nins = 0

    def _wait(self, e, tok):
        if tok is None:
            return
        key, val = tok
        if key == e:
            return
        if self.seen[e].get(key, 0) >= val:
            return
        self.engs[e].wait_ge(self.sems[key], val)
        self.seen[e][key] = val
        self.nwaits += 1

    def _deps(self, e, reads, writes, par=False):
        for b in reads:
            for k, v in b.lw.items():
                self._wait(e, (k, v))
        for b in writes:
            for k, v in b.lw.items():
                if par and k[0] == "d":
                    continue
                self._wait(e, (k, v))
            for k, v in b.rd.items():
                self._wait(e, (k, v))

    def _mark(self, tok, reads, writes, par=False):
        for b in reads:
            b.rd[tok[0]] = tok[1]
        for b in writes:
            if par:
                b.lw[tok[0]] = tok[1]
            else:
                b.lw = {tok[0]: tok[1]}
                b.rd = {}

    def op(self, e, fn, reads=(), writes=()):
        self._deps(e, reads, writes)
        ins = fn()
        self.cnt[e] += 1
        ins.then_inc(self.sems[e], 1)
        tok = (e, self.cnt[e])
        self._mark(tok, reads, writes)
        self.nins += 1
        return tok

    def dma(self, q, out, in_, reads=(), writes=(), par=False, **kw):
        key = self.dsems[self.dnext]
        self.dnext = (self.dnext + 1) % len(self.dsems)
        self._wait(q, (key, self.cnt[key]))
        self._deps(q, reads, writes, par)
        ins = self.engs[q].dma_start(out=out, in_=in_, **kw)
        self.cnt[key] += 16
        ins.then_inc(self.sems[key], 16)
        tok = (key, self.cnt[key])
        self._mark(tok, reads, writes, par)
        self.nins += 1
        return tok

    def barrier(self):
        for e in self.engs:
            for k, v in self.cnt.items():
                if k != e and v > 0:
                    self._wait(e, (k, v))

    def finish(self, e="sp"):
        for k, v in self.cnt.items():
            if k != e and v > 0:
                self._wait(e, (k, v))


class Ctx:
    def __init__(self, nc, es):
        self.nc = nc
        self.es = es
        self.S = Sync(nc, es)
        self.uid = 0

    def sb(self, es, name, shape, dt):
        self.uid += 1
        return Buf(es.enter_context(self.nc.sbuf_tensor("sb%d_%s" % (self.uid, name), shape, dt)), name)

    def ps(self, es, name, shape, dt=F32):
        self.uid += 1
        return Buf(es.enter_context(self.nc.psum_tensor("ps%d_%s" % (self.uid, name), shape, dt)), name)

    def ring(self, es, name, shape, dt, n, psum=False):
        mk = self.ps if psum else self.sb
        return Ring([mk(es, "%s%d" % (name, i), shape, dt) for i in range(n)])


def emit_modvec(X, esp, es, cT, w_mod, bmodT, modT_d):
    nc, S = X.nc, X.S
    modT = X.sb(esp, "modT_sb", [128, 48, 2], F32)
    csb = X.sb(es, "csb", [128, 8, 2], F32)
    sg = X.sb(es, "csg", [128, 8, 2], F32)
    bm = X.sb(es, "bmod", [128, 48], F32)
    pm = X.ps(es, "pmod", [128, 48, 2], F32)
    wr = X.ring(es, "wmod", [128, 8, 512], F32, 2)
    S.dma("sp", csb[:], cT.rearrange("(c p) j -> p c j", p=128), writes=[csb])
    S.dma("sp", bm[:], bmodT, writes=[bm])
    S.op("act", lambda: nc.scalar.activation(out=sg[:], in_=csb[:], func=AF.Sigmoid), reads=[csb], writes=[sg])
    S.op("dve", lambda: nc.vector.tensor_tensor(out=sg[:], in0=sg[:], in1=csb[:], op=ALU.mult),
         reads=[sg, csb], writes=[sg])
    for ch in range(12):
        w = wr.next()
        S.dma("sp", w[:], w_mod[:, ch * 512:(ch + 1) * 512].rearrange("(c p) n -> p c n", p=128), writes=[w])
        for nb in range(4):
            for kc in range(8):
                S.op("pe", lambda: nc.tensor.matmul(pm[:, ch * 4 + nb, :], lhsT=w[:, kc, nb * 128:(nb + 1) * 128],
                                                    rhs=sg[:, kc, :], start=(kc == 0), stop=(kc == 7)),
                     reads=[w, sg], writes=[pm])
    for j in range(2):
        S.op("dve", lambda: nc.vector.tensor_tensor(out=modT[:, :, j], in0=pm[:, :, j], in1=bm[:], op=ALU.add),
             reads=[pm, bm], writes=[modT])
    S.dma("sp", modT_d.rearrange("p (a j) -> p a j", j=2), modT[:], reads=[modT])
    return modT


def emit_norm_mod(X, nc, S, xg, tn, jcol, Gm, Sft, ones, epsc, sqr, pms_r, tmp_r, hT_out):
    pms = pms_r.next()
    for c in range(8):
        sq = sqr.next()
        S.op("act", lambda: nc.scalar.activation(out=sq[:, :tn], in_=xg[:, c, :tn], func=AF.Square),
             reads=[xg], writes=[sq])
        S.op("pe", lambda: nc.tensor.matmul(pms[:, :tn], lhsT=ones, rhs=sq[:, :tn], start=(c == 0), stop=(c == 7)),
             reads=[sq], writes=[pms])
    rstd = getattr(tmp_r, "rstd", None) or tmp_r.next()
    S.op("act", lambda: nc.scalar.activation(out=rstd[:, :tn], in_=pms[:, :tn], func=AF.Ln, scale=1.0 / D, bias=epsc),
         reads=[pms], writes=[rstd])
    S.op("act", lambda: nc.scalar.activation(out=rstd[:, :tn], in_=rstd[:, :tn], func=AF.Exp, scale=-0.5),
         reads=[rstd], writes=[rstd])
    for c in range(8):
        t = tmp_r.next()
        S.op("dve", lambda: nc.vector.scalar_tensor_tensor(out=t[:, :tn], in0=xg[:, c, :tn], scalar=Gm[:, c, jcol:jcol + 1],
                                                           in1=rstd[:, :tn], op0=ALU.mult, op1=ALU.mult),
             reads=[xg, Gm, rstd], writes=[t])
        S.op("act", lambda: nc.scalar.activation(out=hT_out[:, c, :tn], in_=t[:, :tn], func=AF.Identity,
                                                 bias=Sft[:, c, jcol:jcol + 1]),
             reads=[t, Sft], writes=[hT_out])


def emit_A(X, T, l, v, with_ctx, kv_only):
    nc, S = X.nc, X.S
    groups = GROUPS if with_ctx else GROUPS[:4]
    xin = T.xin[l]

    def xc(t0, tn):
        c0 = (8 * NLAT + t0 - NLAT) if t0 >= NLAT else (v * NLAT + t0)
        return xin[:, c0:c0 + tn]
    w_in, w_uq, w_ukv, g0T, gvec, consts, rope = T.w_in[l], T.w_uq[l], T.w_ukv[l], T.g0T[l], T.gvec[l], T.consts, T.rope[v]
    qT_d, kT_d, v_d, gT_d = T.qT_s[v], T.kT_s[v], T.v_s[v], T.gT_s[v]
    modT_i = T.modT_s[l]
    hT_d = T.hT_s
    hT_db = Buf(hT_d, "hT_scr")
    with ExitStack() as es0:
        S.barrier()
        cst = X.sb(es0, "cst", [128, NCONST], F32)
        gv = X.sb(es0, "gv", [128, NGV], F32)
        S.dma("sp", cst[:], consts, writes=[cst])
        S.dma("sp", gv[:], gvec, writes=[gv])
        identb = X.sb(es0, "identb", [128, 128], BF16)
        S.op("dve", lambda: nc.vector.tensor_copy(out=identb[:], in_=cst[:, C_ID:C_ID + 128]), reads=[cst], writes=[identb])
        ones = cst[:, C_ONES:C_ONES + 128]
        epsc = gv[:, GV_EPS:GV_EPS + 1]
        modT = X.sb(es0, "modTa", [128, 48, 2], F32)
        S.dma("sp", modT[:], modT_i.rearrange("p (a j) -> p a j", j=2), writes=[modT])
        g0 = X.sb(es0, "g0", [128, 8], F32)
        S.dma("sp", g0[:], g0T, writes=[g0])
        Gm = X.sb(es0, "Gm", [128, 8, 2], F32)
        for j in range(2):
            S.op("dve", lambda: nc.vector.scalar_tensor_tensor(out=Gm[:, :, j], in0=modT[:, 8:16, j], scalar=1.0,
                                                               in1=g0[:], op0=ALU.add, op1=ALU.mult),
                 reads=[modT, g0], writes=[Gm])
        Sft = modT
        S.barrier()

        with ExitStack() as es:
            NW1 = O_G
            w1 = X.sb(es, "w1", [128, 8, NW1], BF16)
            for c in range(8):
                S.dma("pool", w1[:, c, :], w_in[c * 128:(c + 1) * 128, 0:NW1], writes=[w1], par=True)
            wuq = X.sb(es, "wuq", [128, 3, 768], BF16)
            S.dma("pool", wuq[:], w_uq.rearrange("(c p) n -> p c n", p=128), writes=[wuq])
            wukv = X.sb(es, "wukv", [128, 2, 2, 8, 64], BF16)
            for c in range(2):
                for t in range(2):
                    S.dma("pool", wukv[:, c, t], w_ukv[c * 128:(c + 1) * 128, :].rearrange("p (h t e) -> p t h e", h=8, t=2)[:, t],
                          writes=[wukv], par=True)
            xg = X.sb(es, "xg", [128, 8, 512], F32)
            hTg = X.sb(es, "hTg", [128, 8, 512], BF16)
            tab = X.sb(es, "tab", [128, 6, 512], F32)
            sqr = X.ring(es, "sq", [128, 512], F32, 2)
            tmp = X.ring(es, "tmp", [128, 512], F32, 8)
            tmp.rstd = X.sb(es, "rstd1", [128, 512], F32)
            xsr = X.ring(es, "xs", [128, 512], F32, 4)
            outr = X.ring(es, "ob", [128, 512], BF16, 4)
            cqn = X.sb(es, "cqn", [128, 3, 512], BF16)
            ckvn = X.sb(es, "ckvn", [128, 2, 512], BF16)
            pmm = X.ring(es, "pmm", [128, 512], F32, 3, psum=True)
            pms_r = X.ring(es, "pms", [128, 512], F32, 2, psum=True)
            prx = X.ring(es, "prx", [128, 512], F32, 2, psum=True)
            ptr = X.ps(es, "ptr", [128, 4, 128], BF16)

            def mm_fm(pm, M, tn, lhs_fn, rhs_fn, nk, rd):
                for c in range(nk):
                    S.op("pe", lambda: nc.tensor.matmul(pm[0:M, :tn], lhsT=lhs_fn(c), rhs=rhs_fn(c),
                                                        start=(c == 0), stop=(c == nk - 1)),
                         reads=rd, writes=[pm])

            def rstd_from(xsq_list, M, tn, bd, scale):
                pst = pms_r.next()
                n = len(xsq_list)
                for i, sq in enumerate(xsq_list):
                    S.op("pe", lambda: nc.tensor.matmul(pst[0:M, :tn], lhsT=bd, rhs=sq[0:M, :tn],
                                                        start=(i == 0), stop=(i == n - 1)),
                         reads=[sq, cst], writes=[pst])
                r = tmp.next()
                S.op("act", lambda: nc.scalar.activation(out=r[0:M, :tn], in_=pst[0:M, :tn], func=AF.Ln, scale=scale,
                                                         bias=epsc[0:M, :]),
                     reads=[pst, gv], writes=[r])
                S.op("act", lambda: nc.scalar.activation(out=r[0:M, :tn], in_=r[0:M, :tn], func=AF.Exp, scale=-0.5),
                     reads=[r], writes=[r])
                return r

            def rope_store(xn, M, tn, ti, dst_ap, dst_buf=None):
                pr = prx.next()
                S.op("pe", lambda: nc.tensor.matmul(pr[0:M, :tn], lhsT=cst[0:M, C_PSW:C_PSW + M], rhs=xn[0:M, :tn],
                                                    start=True, stop=True), reads=[xn, cst], writes=[pr])
                t1 = tmp.next()
                S.op("pool", lambda: nc.gpsimd.tensor_tensor(out=t1[0:M, :tn], in0=xn[0:M, :tn], in1=tab[0:M, 2 * ti, :tn],
                                                             op=ALU.mult), reads=[xn, tab], writes=[t1])
                t2 = tmp.next()
                S.op("dve", lambda: nc.vector.tensor_tensor(out=t2[0:M, :tn], in0=pr[0:M, :tn], in1=tab[0:M, 2 * ti + 1, :tn],
                                                            op=ALU.mult), reads=[pr, tab], writes=[t2])
                ob = outr.next()
                S.op("dve", lambda: nc.vector.tensor_tensor(out=ob[0:M, :tn], in0=t1[0:M, :tn], in1=t2[0:M, :tn], op=ALU.add),
                     reads=[t1, t2], writes=[ob])
                S.dma("sp", dst_ap, ob[0:M, :tn], reads=[ob])
                return ob

            def job_norm(pm, M, tn, bd, scale, gcol, ti, dst_ap):
                xs = xsr.next()
                S.op("act", lambda: nc.scalar.copy(out=xs[0:M, :tn], in_=pm[0:M, :tn]), reads=[pm], writes=[xs])
                sq = sqr.next()
                S.op("act", lambda: nc.scalar.activation(out=sq[0:M, :tn], in_=pm[0:M, :tn], func=AF.Square),
                     reads=[pm], writes=[sq])
                r = rstd_from([sq], M, tn, bd, scale)
                if ti is None:
                    ob = outr.next()
                    S.op("dve", lambda: nc.vector.scalar_tensor_tensor(out=ob[0:M, :tn], in0=xs[0:M, :tn],
                                                                       scalar=gv[0:M, gcol:gcol + 1], in1=r[0:M, :tn],
                                                                       op0=ALU.mult, op1=ALU.mult),
                         reads=[xs, gv, r], writes=[ob])
                    S.dma("sp", dst_ap, ob[0:M, :tn], reads=[ob])
                    return ob
                xn = tmp.next()
                S.op("dve", lambda: nc.vector.scalar_tensor_tensor(out=xn[0:M, :tn], in0=xs[0:M, :tn],
                                                                   scalar=gv[0:M, gcol:gcol + 1], in1=r[0:M, :tn],
                                                                   op0=ALU.mult, op1=ALU.mult),
                     reads=[xs, gv, r], writes=[xn])
                return rope_store(xn, M, tn, ti, dst_ap)

            bd64 = cst[:, C_BD64:C_BD64 + 128]
            for gi, (t0, tn) in enumerate(groups):
                jcol = 1 if gi == 4 else 0
                S.dma("sp", xg[:, :, :tn], xc(t0, tn).rearrange("(c p) t -> p c t", p=128), writes=[xg])
                S.dma("sp", tab[:, :, :tn], rope[:, :, t0:t0 + tn].rearrange("s p t -> p s t"), writes=[tab])
                emit_norm_mod(X, nc, S, xg, tn, jcol, Gm, Sft, ones, epsc, sqr, pms_r, tmp, hTg)
                S.dma("sp", hT_d[:, t0:t0 + tn].rearrange("(c p) t -> p c t", p=128), hTg[:, :, :tn], reads=[hTg],
                      writes=[hT_db], par=True)
                hrhs = lambda c: hTg[:, c, :tn]
                fam = [(O_AQ, 4, GV_AQ, qT_d, QA), (O_AK, 1, GV_AK, kT_d, KA),
                       (O_DQ, 4, GV_DQ, qT_d, QD), (O_DK, 4, GV_DK, kT_d, KD)]
                if kv_only:
                    fam = [f_ for f_ in fam if f_[3] is kT_d]
                for (co, nb, gcol, dst, r0) in fam:
                    for b in range(nb):
                        pm = pmm.next()
                        mm_fm(pm, 128, tn, lambda c: w1[:, c, co + b * 128:co + (b + 1) * 128], hrhs, 8, [w1, hTg])
                        job_norm(pm, 128, tn, bd64, 1.0 / 64, gcol, 0, dst[r0 + b * 128:r0 + (b + 1) * 128, t0:t0 + tn])
                for (co, nb, scl, dst, r0, tokmaj) in ([] if kv_only else [(O_BQ, 2, 1.0, qT_d, QB, False)]) + [(O_BK, 2, 0.125, kT_d, KB, True)]:
                    for b in range(nb):
                        pm = pmm.next()
                        mm_fm(pm, 128, tn, lambda c: w1[:, c, co + b * 128:co + (b + 1) * 128], hrhs, 8, [w1, hTg])
                        xn = tmp.next()
                        S.op("act", lambda: nc.scalar.mul(out=xn[:, :tn], in_=pm[:, :tn], mul=scl),
                             reads=[pm], writes=[xn])
                        ob = rope_store(xn, 128, tn, 0, dst[r0 + b * 128:r0 + (b + 1) * 128, t0:t0 + tn])
                        if tokmaj:
                            nsub = tn // 128
                            for s in range(nsub):
                                S.op("pe", lambda: nc.tensor.transpose(out=ptr[:, s, :], in_=ob[:, s * 128:(s + 1) * 128],
                                                                       identity=identb[:]),
                                     reads=[ob, identb], writes=[ptr])
                            kb = outr.next()
                            S.op("dve", lambda: nc.vector.tensor_copy(out=kb[:, :tn], in_=ptr[:, 0:nsub, :].rearrange("p s e -> p (s e)")),
                                 reads=[ptr], writes=[kb])
                            S.dma("sp", v_d[t0:t0 + tn, VBK + b * 128:VBK + (b + 1) * 128].rearrange("(s p) e -> p s e", p=128),
                                  kb[:, :tn].rearrange("p (s e) -> p s e", e=128), reads=[kb])
                for b in range(0 if kv_only else 4):
                    pm = pmm.next()
                    mm_fm(pm, 128, tn, lambda c: w1[:, c, O_BG + b * 128:O_BG + (b + 1) * 128], hrhs, 8, [w1, hTg])
                    ob = outr.next()
                    S.op("act", lambda: nc.scalar.activation(out=ob[:, :tn], in_=pm[:, :tn], func=AF.Silu),
                         reads=[pm], writes=[ob])
                    S.dma("sp", gT_d[b * 128:(b + 1) * 128, t0:t0 + tn], ob[:, :tn], reads=[ob])
                for (co, nb, gc0, dstt, nfeat) in ([] if kv_only else [(O_CQ, 3, GV_CQ, cqn, 384.0)]) + [(O_CKV, 2, GV_CKV, ckvn, 256.0)]:
                    xss, sqs = [], []
                    for b in range(nb):
                        pm = pmm.next()
                        mm_fm(pm, 128, tn, lambda c: w1[:, c, co + b * 128:co + (b + 1) * 128], hrhs, 8, [w1, hTg])
                        xs = xsr.next()
                        S.op("act", lambda: nc.scalar.copy(out=xs[:, :tn], in_=pm[:, :tn]), reads=[pm], writes=[xs])
                        sq = tmp.next()
                        S.op("act", lambda: nc.scalar.activation(out=sq[:, :tn], in_=pm[:, :tn], func=AF.Square),
                             reads=[pm], writes=[sq])
                        xss.append(xs)
                        sqs.append(sq)
                    r = rstd_from(sqs, 128, tn, ones, 1.0 / nfeat)
                    for b in range(nb):
                        S.op("dve", lambda: nc.vector.scalar_tensor_tensor(out=dstt[:, b, :tn], in0=xss[b][:, :tn],
                                                                           scalar=gv[:, gc0 + b:gc0 + b + 1], in1=r[:, :tn],
                                                                           op0=ALU.mult, op1=ALU.mult),
                             reads=[xss[b], gv, r], writes=[dstt])
                pm = pmm.next()
                mm_fm(pm, 32, tn, lambda c: w1[:, c, O_CKR:O_CKR + 32], hrhs, 8, [w1, hTg])
                job_norm(pm, 32, tn, cst[0:32, C_BD64:C_BD64 + 32], 1.0 / 32, GV_MKR, 2, kT_d[KCR:KCR + 32, t0:t0 + tn])
                for h in range(0 if kv_only else 8):
                    pm = pmm.next()
                    mm_fm(pm, 96, tn, lambda c: wuq[:, c, h * 96:(h + 1) * 96], lambda c: cqn[:, c, :tn], 3, [wuq, cqn])
                    job_norm(pm, 96, tn, cst[0:96, C_BD96:C_BD96 + 96], gv[0:96, GV_SC96:GV_SC96 + 1], GV_MQ, 1,
                             qT_d[QC + h * 96:QC + (h + 1) * 96, t0:t0 + tn])
                for hp in range(4):
                    pm = pmm.next()
                    mm_fm(pm, 128, tn, lambda c: wukv[:, c, 0, 2 * hp:2 * hp + 2, :].rearrange("p h e -> p (h e)"),
                          lambda c: ckvn[:, c, :tn], 2, [wukv, ckvn])
                    job_norm(pm, 128, tn, bd64, 1.0 / 64, GV_MKN, None, kT_d[KCN + hp * 128:KCN + (hp + 1) * 128, t0:t0 + tn])
                for s in range(tn // 128):
                    ts_ = slice(s * 128, (s + 1) * 128)
                    vjobs = [(lambda c: hTg[:, c, ts_], lambda c: w1[:, c, O_AV:O_AV + 128], 8, 128, VA, [hTg, w1]),
                             (lambda c: hTg[:, c, ts_], lambda c: w1[:, c, O_BV:O_BV + 512], 8, 512, VB, [hTg, w1]),
                             (lambda c: hTg[:, c, ts_], lambda c: w1[:, c, O_DV:O_DV + 512], 8, 512, VD, [hTg, w1]),
                             (lambda c: ckvn[:, c, ts_], lambda c: wukv[:, c, 1].rearrange("p h e -> p (h e)"), 2, 512, VC,
                              [ckvn, wukv])]
                    for (lf, rf, nk, ncol, vo, rd) in vjobs:
                        pm = pmm.next()
                        for c in range(nk):
                            S.op("pe", lambda: nc.tensor.matmul(pm[:, :ncol], lhsT=lf(c), rhs=rf(c), start=(c == 0),
                                                                stop=(c == nk - 1)), reads=rd, writes=[pm])
                        ob = outr.next()
                        S.op("act", lambda: nc.scalar.copy(out=ob[:, :ncol], in_=pm[:, :ncol]), reads=[pm], writes=[ob])
                        S.dma("sp", v_d[t0 + s * 128:t0 + (s + 1) * 128, vo:vo + ncol], ob[:, :ncol], reads=[ob])
            S.barrier()
        with ExitStack() as es:
          if not kv_only:
              w2 = X.sb(es, "w2", [128, 8, 4096], BF16)
              for c in range(8):
                  S.dma("pool", w2[:, c, :], w_in[c * 128:(c + 1) * 128, O_G:O_G + 4096], writes=[w2], par=True)
              hr = X.ring(es, "hT2", [128, 8, 512], BF16, 2)
              outr = X.ring(es, "ob2", [128, 512], BF16, 4)
              pmm = X.ring(es, "pmm2", [128, 512], F32, 4, psum=True)
              for gi, (t0, tn) in enumerate(groups):
                  hTg = hr.next()
                  S.dma("sp", hTg[:, :, :tn], hT_d[:, t0:t0 + tn].rearrange("(c p) t -> p c t", p=128), reads=[hT_db],
                        writes=[hTg])
                  for b in range(32):
                      pm = pmm.next()
                      for c in range(8):
                          S.op("pe", lambda: nc.tensor.matmul(pm[:, :tn], lhsT=w2[:, c, b * 128:(b + 1) * 128], rhs=hTg[:, c, :tn],
                                                              start=(c == 0), stop=(c == 7)), reads=[w2, hTg], writes=[pm])
                      ob = outr.next()
                      S.op("act", lambda: nc.scalar.activation(out=ob[:, :tn], in_=pm[:, :tn], func=AF.Sigmoid),
                           reads=[pm], writes=[ob])
                      S.dma("sp", gT_d[512 + b * 128:512 + (b + 1) * 128, t0:t0 + tn], ob[:, :tn], reads=[ob])
        S.barrier()


NKT = 130
NKEY = NKT * 128
BR_A, BR_B, BR_C, BR_D = 0, 512, 1024, 1536
RT_RELF, RT_MSKF, RT_RELB, RT_MSKB, RT_QDF, RT_QDB, RT_KDF, RT_KDB, RT_SEL = 0, 128, 256, 384, 512, 640, 768, 769, 770
NRT = 770 + 64
AX = mybir.AxisListType


def emit_B(X, T, l, v, need_ctx):
    nc, S = X.nc, X.S
    layer = l
    lam_init = 0.8 - 0.6 * math.exp(-0.3 * layer)
    groups = GROUPS if need_ctx else GROUPS[:4]
    xin = T.xin[l]
    xout = T.xout[l]

    def xci(t0, tn):
        c0 = (8 * NLAT + t0 - NLAT) if t0 >= NLAT else (v * NLAT + t0)
        return xin[:, c0:c0 + tn]

    def xco(t0, tn):
        if l == DEPTH - 1:
            return xout[:, t0:t0 + tn]
        c0 = (8 * NLAT + t0 - NLAT) if t0 >= NLAT else (v * NLAT + t0)
        return xout[:, c0:c0 + tn]
    qT, kTl, vl, kTg, vg, gT = T.qT_s[v], T.kT_s[v], T.v_s[v], T.kTg, T.vg, T.gT_s[v]
    modT_i, g1T = T.modT_s[l], T.g1T[l]
    w_branch, w_out, w_fi, w_fo = T.w_branch[l], T.w_out[l], T.w_fi[l], T.w_fo[l]
    gvec, consts, rt_i, retE, retd_i, dlam_i = T.gvec[l], T.consts, T.rt, T.retE[v], T.retd[l], T.dlam[l]
    brT_d, x1_d, h2_d = T.brT_s, T.x1T_s, T.h2T_s
    brT_b, x1_b, h2_b = Buf(brT_d, "brT"), Buf(x1_d, "x1"), Buf(h2_d, "h2")
    with ExitStack() as es0:
        S.barrier()
        cst = X.sb(es0, "cst", [128, NCONST], F32)
        gv = X.sb(es0, "gv", [128, NGV], F32)
        modT = X.sb(es0, "modTs", [128, 48, 2], F32)
        rt = X.sb(es0, "rt", [128, NRT], F32)
        g1 = X.sb(es0, "g1", [128, 8], F32)
        Gm2 = X.sb(es0, "Gm2", [128, 8, 2], F32)
        small = X.sb(es0, "small", [128, 64], F32)
        onesb = X.sb(es0, "onesb", [128, 128], BF16)
        S.dma("sp", cst[:], consts, writes=[cst])
        S.dma("sp", gv[:], gvec, writes=[gv])
        S.dma("sp", modT[:], modT_i.rearrange("p (a j) -> p a j", j=2), writes=[modT])
        S.dma("sp", rt[:], rt_i, writes=[rt])
        S.dma("sp", g1[:], g1T, writes=[g1])
        S.barrier()
        S.op("dve", lambda: nc.vector.tensor_copy(out=onesb[:], in_=cst[:, C_ONES:C_ONES + 128]), reads=[cst], writes=[onesb])
        S.op("pool", lambda: nc.gpsimd.memset(small[:], 0.0), writes=[small])
        for j in range(2):
            S.op("dve", lambda: nc.vector.scalar_tensor_tensor(out=Gm2[:, :, j], in0=modT[:, 32:40, j], scalar=1.0,
                                                               in1=g1[:], op0=ALU.add, op1=ALU.mult),
                 reads=[modT, g1], writes=[Gm2])
        ones = cst[:, C_ONES:C_ONES + 128]
        one_col = cst[:, C_ONES:C_ONES + 1]
        epsc = gv[:, GV_EPS:GV_EPS + 1]
        sel = rt[:, RT_SEL:RT_SEL + 64]
        with ExitStack() as es:
            dl = X.sb(es, "dl", [128, 4, 64], F32)
            rd = X.sb(es, "rd", [128, 8], F32)
            pr = X.sb(es, "prd", [128, 2, 64], F32)
            S.dma("sp", dl[:], dlam_i.rearrange("p (a e) -> p a e", e=64), writes=[dl])
            S.dma("sp", rd[:], retd_i, writes=[rd])
            S.op("dve", lambda: nc.vector.tensor_tensor(out=pr[:, 0, :], in0=dl[:, 0, :], in1=dl[:, 1, :], op=ALU.mult),
                 reads=[dl], writes=[pr])
            S.op("dve", lambda: nc.vector.tensor_tensor(out=pr[:, 1, :], in0=dl[:, 2, :], in1=dl[:, 3, :], op=ALU.mult),
                 reads=[dl], writes=[pr])
            for m in range(2):
                S.op("act", lambda: nc.scalar.activation(out=dl[:, m, :], in_=pr[:, m, :], func=AF.Identity,
                                                         accum_out=small[:, 2 + m:3 + m]), reads=[pr], writes=[small, dl])
            S.op("act", lambda: nc.scalar.activation(out=small[:, 2:4], in_=small[:, 2:4], func=AF.Exp), reads=[small], writes=[small])
            S.op("dve", lambda: nc.vector.tensor_tensor(out=small[:, 0:1], in0=small[:, 3:4], in1=small[:, 2:3], op=ALU.subtract),
                 reads=[small], writes=[small])
            S.op("dve", lambda: nc.vector.tensor_scalar(out=small[:, 0:1], in0=small[:, 0:1], scalar1=-lam_init, scalar2=None,
                                                        op0=ALU.add), reads=[small], writes=[small])
            S.op("dve", lambda: nc.vector.tensor_scalar(out=small[:, 1:2], in0=gv[:, GV_SUBLN:GV_SUBLN + 1],
                                                        scalar1=1.0 - lam_init, scalar2=None, op0=ALU.mult),
                 reads=[gv], writes=[small])
            S.op("act", lambda: nc.scalar.activation(out=small[:, 8:16], in_=rd[:], func=AF.Exp, scale=-1.0), reads=[rd], writes=[small])
            S.op("act", lambda: nc.scalar.activation(out=small[:, 8:16], in_=small[:, 8:16], func=AF.Ln, bias=one_col),
                 reads=[small], writes=[small])
            S.op("dve", lambda: nc.vector.tensor_scalar(out=small[:, 8:16], in0=small[:, 8:16], scalar1=-1.0, scalar2=None,
                                                        op0=ALU.mult), reads=[small], writes=[small])
            S.op("act", lambda: nc.scalar.activation(out=small[:, 16:24], in_=small[:, 8:16], func=AF.Exp, scale=128.0),
                 reads=[small], writes=[small])
            for dr in range(2):
                for h in range(4):
                    cc = dr * 4 + h
                    S.op("act", lambda: nc.scalar.activation(out=small[:, 24 + cc:25 + cc], in_=rt[:, RT_KDF + dr:RT_KDF + dr + 1],
                                                             func=AF.Exp, scale=small[:, 8 + cc:9 + cc]),
                         reads=[small, rt], writes=[small])
            S.barrier()
        neg_lam = small[:, 0:1]
        gsub = small[:, 1:2]

        with ExitStack() as es:
            kbuf = X.ring(es, "kTu", [128, NKEY], BF16, 2)
            vbuf = X.ring(es, "vau", [128, NKT, 128], BF16, 2)
            qr = X.ring(es, "qh", [128, NT], BF16, 3)
            ptr_ = X.ring(es, "pt", [128, 512], BF16, 4)
            fin = X.ring(es, "fin", [128, 512], F32, 6)
            obr = X.ring(es, "obf", [128, 512], BF16, 3)
            pst = X.ring(es, "pst", [128, 512], F32, 3, psum=True)
            pacc = X.ring(es, "pacc", [128, 512], F32, 4, psum=True)
            pn_r = X.ring(es, "pn", [128, 512], F32, 1, psum=True)
            for vb_ in vbuf.bufs:
                S.op("pool", lambda: nc.gpsimd.memset(vb_[:, :, 64:128], 1.0), writes=[vb_])

            def attn_run(kt_list, kT_ap_fn, q_ap, tn, scale, pv_list):
                n = len(kt_list)
                LA = 2
                pts = [None] * n
                for i in range(n + LA):
                    if i < n:
                        ps_ = pst.next()
                        S.op("pe", lambda: nc.tensor.matmul(ps_[:, :tn], lhsT=kT_ap_fn(kt_list[i]), rhs=q_ap, start=True, stop=True),
                             reads=kq_reads, writes=[ps_])
                        pt = ptr_.next()
                        S.op("act", lambda: nc.scalar.activation(out=pt[:, :tn], in_=ps_[:, :tn], func=AF.Exp, scale=scale),
                             reads=[ps_], writes=[pt])
                        pts[i] = pt
                    if i >= LA:
                        j = i - LA
                        for (acc, M, lf, rdl) in pv_list:
                            S.op("pe", lambda: nc.tensor.matmul(acc[0:M, :tn], lhsT=lf(kt_list[j]), rhs=pts[j][:, :tn],
                                                                start=(j == 0), stop=(j == n - 1)),
                                 reads=[pts[j]] + rdl, writes=[acc])

            def finalize_AC(acc, tn, dst_ap):
                a = fin.next()
                S.op("dve", lambda: nc.vector.tensor_copy(out=a[:, :tn], in_=acc[:, :tn]), reads=[acc], writes=[a])
                pn = pn_r.next()
                S.op("pe", lambda: nc.tensor.matmul(pn[0:64, :tn], lhsT=sel, rhs=a[:, :tn], start=True, stop=True),
                     reads=[a, rt], writes=[pn])
                rc = fin.next()
                S.op("dve", lambda: nc.vector.reciprocal(out=rc[0:64, :tn], in_=pn[0:64, :tn]), reads=[pn], writes=[rc])
                ob = obr.next()
                S.op("dve", lambda: nc.vector.tensor_tensor(out=ob[0:64, :tn], in0=a[0:64, :tn], in1=rc[0:64, :tn], op=ALU.mult),
                     reads=[a, rc], writes=[ob])
                S.dma("sp", dst_ap, ob[0:64, :tn], reads=[ob], writes=[brT_b], par=True)

            units = [("A", g) for g in range(2)] + [("C", h) for h in range(8)] + [("D", h) for h in range(4)]
            all_kt = list(range(NKT))
            ctx_kt = [128, 129]
            d_started = False
            for (kind, u) in units:
                kb_ = kbuf.next()
                vb_ = vbuf.next()
                if kind == "A":
                    S.dma("sp", kb_[0:64, :], kTg[KA + u * 64:KA + (u + 1) * 64, :], writes=[kb_])
                    vsrc = vg[:, VA + u * 64:VA + (u + 1) * 64]
                    heads = [(QA + (4 * u + i) * 64, 64, BR_A + (4 * u + i) * 64) for i in range(4)]
                    kr, vw, scale = 64, 64, 0.125
                elif kind == "C":
                    S.dma("sp", kb_[0:64, :], kTg[KCN + u * 64:KCN + (u + 1) * 64, :], writes=[kb_], par=True)
                    S.dma("sp", kb_[64:96, :], kTg[KCR:KCR + 32, :], writes=[kb_], par=True)
                    vsrc = vg[:, VC + u * 64:VC + (u + 1) * 64]
                    heads = [(QC + u * 96, 96, BR_C + u * 64)]
                    kr, vw, scale = 96, 64, 96.0 ** -0.5
                else:
                    S.dma("sp", kb_[:, :], kTg[KD + u * 128:KD + (u + 1) * 128, :], writes=[kb_])
                    vsrc = vg[:, VD + u * 128:VD + (u + 1) * 128]
                    heads = [(QD + u * 128, 128, BR_D + u * 128)]
                    kr, vw, scale = 128, 128, 0.125
                for pc in range(5):
                    S.dma("sp", vb_[:, pc * 26:(pc + 1) * 26, 0:vw],
                          vsrc[pc * 26 * 128:(pc + 1) * 26 * 128, :].rearrange("(t p) e -> p t e", p=128), writes=[vb_], par=True)
                for (qrow, qrows, brow) in heads:
                    qh = qr.next()
                    S.dma("sp", qh[0:qrows, :], qT[qrow:qrow + qrows, :], writes=[qh])
                    kq_reads = [kb_, qh]
                    for gi, (t0, tn) in enumerate(groups):
                        kts = ctx_kt if gi == 4 else all_kt
                        if kind != "D":
                            acc = pacc.next()
                            attn_run(kts, lambda kt: kb_[0:kr, kt * 128:(kt + 1) * 128], qh[0:kr, t0:t0 + tn], tn, scale,
                                     [(acc, 128, lambda kt: vb_[:, kt, :], [vb_])])
                            finalize_AC(acc, tn, brT_d[brow:brow + 64, t0:t0 + tn])
                        else:
                            accs = []
                            for m in range(2):
                                ao, asum = pacc.next(), pacc.next()
                                attn_run(kts, lambda kt: kb_[m * 64:(m + 1) * 64, kt * 128:(kt + 1) * 128],
                                         qh[m * 64:(m + 1) * 64, t0:t0 + tn], tn, scale,
                                         [(ao, 128, lambda kt: vb_[:, kt, :], [vb_]), (asum, 128, lambda kt: onesb[:], [onesb])])
                                accs.append((ao, asum))
                            r0, r1, t0_, t1_ = fin.next(), fin.next(), fin.next(), fin.next()
                            S.op("dve", lambda: nc.vector.reciprocal(out=r0[:, :tn], in_=accs[0][1][:, :tn]), reads=[accs[0][1]], writes=[r0])
                            S.op("dve", lambda: nc.vector.reciprocal(out=r1[:, :tn], in_=accs[1][1][:, :tn]), reads=[accs[1][1]], writes=[r1])
                            S.op("dve", lambda: nc.vector.tensor_tensor(out=t0_[:, :tn], in0=accs[0][0][:, :tn], in1=r0[:, :tn], op=ALU.mult),
                                 reads=[accs[0][0], r0], writes=[t0_])
                            S.op("dve", lambda: nc.vector.scalar_tensor_tensor(out=t1_[:, :tn], in0=accs[1][0][:, :tn], scalar=neg_lam,
                                                                               in1=r1[:, :tn], op0=ALU.mult, op1=ALU.mult),
                                 reads=[accs[1][0], r1, small], writes=[t1_])
                            S.op("dve", lambda: nc.vector.tensor_tensor(out=t0_[:, :tn], in0=t0_[:, :tn], in1=t1_[:, :tn], op=ALU.add),
                                 reads=[t0_, t1_], writes=[t0_])
                            S.op("act", lambda: nc.scalar.activation(out=r0[:, :tn], in_=t0_[:, :tn], func=AF.Square), reads=[t0_], writes=[r0])
                            pn = pn_r.next()
                            S.op("pe", lambda: nc.tensor.matmul(pn[:, :tn], lhsT=ones, rhs=r0[:, :tn], start=True, stop=True),
                                 reads=[r0], writes=[pn])
                            S.op("act", lambda: nc.scalar.activation(out=r1[:, :tn], in_=pn[:, :tn], func=AF.Ln, scale=1.0 / 128, bias=epsc),
                                 reads=[pn], writes=[r1])
                            S.op("act", lambda: nc.scalar.activation(out=r1[:, :tn], in_=r1[:, :tn], func=AF.Exp, scale=-0.5),
                                 reads=[r1], writes=[r1])
                            ob = obr.next()
                            S.op("dve", lambda: nc.vector.scalar_tensor_tensor(out=ob[:, :tn], in0=t0_[:, :tn], scalar=gsub, in1=r1[:, :tn],
                                                                               op0=ALU.mult, op1=ALU.mult),
                                 reads=[t0_, r1, small], writes=[ob])
                            S.dma("sp", brT_d[brow:brow + 128, t0:t0 + tn], ob[:, :tn], reads=[ob], writes=[brT_b], par=True)
            S.barrier()

        ntl = 18 if need_ctx else 16
        with ExitStack() as es:
            mask = X.sb(es, "rmask", [128, 4, 128], F32)
            qdec = X.sb(es, "qdec", [64, 8, 128], F32)
            coef = X.sb(es, "coef", [128, 2, NKT], F32)
            Eb = X.sb(es, "retEs", [128, 4, NKT], F32)
            kbg = X.sb(es, "kbg", [128, NKT, 64], BF16)
            vbg = X.sb(es, "vbg", [128, NKT, 128], BF16)
            kw = X.ring(es, "kw", [128, NKT, 64], BF16, 2)
            kbl = X.sb(es, "kbl", [128, 18, 64], BF16)
            vbl = X.sb(es, "vbl", [128, 18, 128], BF16)
            kdl = X.ring(es, "kdl", [128, 18, 64], BF16, 2)
            qh = X.sb(es, "rqh", [64, NT], BF16)
            kh = X.sb(es, "rkh", [64, NT], BF16)
            gh = X.sb(es, "rgh", [128, NT], BF16)
            qd = X.ring(es, "rqd", [64, 2, 128], BF16, 3)
            st = X.ring(es, "rst", [64, 128], F32, 2)
            snap = X.sb(es, "snap", [64, 2, 18, 128], BF16)
            sm = X.ring(es, "rsm", [128, 128], BF16, 3)
            tmpf = X.ring(es, "rtmp", [128, 512], F32, 4)
            obr = X.ring(es, "rob", [128, 512], BF16, 2)
            pU = X.ring(es, "pU", [64, 128], F32, 2, psum=True)
            pS = X.ring(es, "pS", [128, 128], F32, 2, psum=True)
            pO = X.ring(es, "pO", [128, 512], F32, 2, psum=True)
            pN = X.ring(es, "pN2", [128, 512], F32, 1, psum=True)
            S.dma("sp", Eb[:], retE, writes=[Eb])
            for h in range(4):
                for dr in range(2):
                    cc = dr * 4 + h
                    t = tmpf.next()
                    rel = rt[:, RT_RELF + 256 * dr:RT_RELF + 256 * dr + 128]
                    msk = rt[:, RT_MSKF + 256 * dr:RT_MSKF + 256 * dr + 128]
                    S.op("act", lambda: nc.scalar.activation(out=t[:, 0:128], in_=rel, func=AF.Exp, scale=small[:, 8 + cc:9 + cc]),
                         reads=[small], writes=[t])
                    if dr == 0:
                        S.op("dve", lambda: nc.vector.tensor_tensor(out=mask[:, h, :], in0=t[:, 0:128], in1=msk, op=ALU.mult),
                             reads=[t], writes=[mask])
                    else:
                        S.op("dve", lambda: nc.vector.tensor_tensor(out=t[:, 0:128], in0=t[:, 0:128], in1=msk, op=ALU.mult),
                             reads=[t], writes=[t])
                        S.op("dve", lambda: nc.vector.tensor_tensor(out=mask[:, h, :], in0=mask[:, h, :], in1=t[:, 0:128], op=ALU.add),
                             reads=[t, mask], writes=[mask])
                    S.op("act", lambda: nc.scalar.activation(out=qdec[:, cc, :], in_=rt[0:64, RT_QDF + 128 * dr:RT_QDF + 128 * dr + 128],
                                                             func=AF.Exp, scale=small[0:64, 8 + cc:9 + cc]),
                         reads=[small], writes=[qdec])
            for h in range(4):
                for pc in range(5):
                    S.dma("sp", kbg[:, pc * 26:(pc + 1) * 26, :],
                          vg[pc * 3328:(pc + 1) * 3328, VBK + h * 64:VBK + (h + 1) * 64].rearrange("(t p) e -> p t e", p=128),
                          writes=[kbg], par=True)
                    S.dma("sp", vbg[:, pc * 26:(pc + 1) * 26, :],
                          vg[pc * 3328:(pc + 1) * 3328, VB + h * 128:VB + (h + 1) * 128].rearrange("(t p) e -> p t e", p=128),
                          writes=[vbg], par=True)
                S.dma("sp", kbl[:], vl[:, VBK + h * 64:VBK + (h + 1) * 64].rearrange("(t p) e -> p t e", p=128), writes=[kbl])
                S.dma("sp", vbl[:], vl[:, VB + h * 128:VB + (h + 1) * 128].rearrange("(t p) e -> p t e", p=128), writes=[vbl])
                S.dma("sp", qh[:], qT[QB + h * 64:QB + (h + 1) * 64, :], writes=[qh])
                S.dma("sp", kh[:], kTl[KB + h * 64:KB + (h + 1) * 64, :], writes=[kh])
                S.dma("sp", gh[:], gT[h * 128:(h + 1) * 128, :], writes=[gh])
                for dr in range(2):
                    cc = dr * 4 + h
                    S.op("act", lambda: nc.scalar.activation(out=coef[:, dr, :], in_=Eb[:, 2 * dr, :], func=AF.Exp,
                                                             scale=small[:, 8 + cc:9 + cc]), reads=[Eb, small], writes=[coef])
                    S.op("dve", lambda: nc.vector.tensor_tensor(out=coef[:, dr, :], in0=coef[:, dr, :], in1=Eb[:, 2 * dr + 1, :], op=ALU.mult),
                         reads=[Eb, coef], writes=[coef])
                    kw_ = kw.next()
                    S.op("dve", lambda: nc.vector.tensor_tensor(out=kw_[:], in0=kbg[:], in1=coef[:, dr, :].unsqueeze(2).to_broadcast([128, NKT, 64]),
                                                                op=ALU.mult), reads=[kbg, coef], writes=[kw_])
                    pu = pU.next()
                    for t in range(NKT):
                        S.op("pe", lambda: nc.tensor.matmul(pu[:], lhsT=kw_[:, t, :], rhs=vbg[:, t, :], start=(t == 0), stop=(t == NKT - 1)),
                             reads=[kw_, vbg], writes=[pu])
                    s_ = st.next()
                    S.op("dve", lambda: nc.vector.tensor_copy(out=s_[:], in_=pu[:]), reads=[pu], writes=[s_])
                    kd_ = kdl.next()
                    S.op("dve", lambda: nc.vector.tensor_scalar(out=kd_[:], in0=kbl[:], scalar1=small[:, 24 + cc:25 + cc], scalar2=None,
                                                                op0=ALU.mult), reads=[kbl, small], writes=[kd_])
                    order = list(range(16)) if dr == 0 else list(range(15, -1, -1))
                    cdc = small[0:64, 16 + cc:17 + cc]
                    for i in order:
                        S.op("act", lambda: nc.scalar.copy(out=snap[:, dr, i, :], in_=s_[:]), reads=[s_], writes=[snap])
                        pu = pU.next()
                        S.op("pe", lambda: nc.tensor.matmul(pu[:], lhsT=kd_[:, i, :], rhs=vbl[:, i, :], start=True, stop=True),
                             reads=[kd_, vbl], writes=[pu])
                        S.op("dve", lambda: nc.vector.scalar_tensor_tensor(out=s_[:], in0=s_[:], scalar=cdc, in1=pu[:],
                                                                           op0=ALU.mult, op1=ALU.add), reads=[s_, pu, small], writes=[s_])
                    if need_ctx:
                        first, second = (16, 17) if dr == 0 else (17, 16)
                        S.op("pool", lambda: nc.gpsimd.memset(snap[:, dr, first, :], 0.0), writes=[snap])
                        pu = pU.next()
                        S.op("pe", lambda: nc.tensor.matmul(pu[:], lhsT=kd_[:, first, :], rhs=vbl[:, first, :], start=True, stop=True),
                             reads=[kd_, vbl], writes=[pu])
                        S.op("act", lambda: nc.scalar.copy(out=snap[:, dr, second, :], in_=pu[:]), reads=[pu], writes=[snap])
                for gi, (t0, tn) in enumerate(groups):
                    po = pO.next()
                    for s in range(tn // 128):
                        i = t0 // 128 + s
                        ts_ = slice(i * 128, (i + 1) * 128)
                        psc = pS.next()
                        S.op("pe", lambda: nc.tensor.matmul(psc[:], lhsT=kh[:, ts_], rhs=qh[:, ts_], start=True, stop=True),
                             reads=[kh, qh], writes=[psc])
                        sm_ = sm.next()
                        S.op("dve", lambda: nc.vector.tensor_tensor(out=sm_[:], in0=psc[:], in1=mask[:, h, :], op=ALU.mult),
                             reads=[psc, mask], writes=[sm_])
                        qd_ = qd.next()
                        for dr in range(2):
                            S.op("pool", lambda: nc.gpsimd.tensor_tensor(out=qd_[:, dr, :], in0=qh[:, ts_], in1=qdec[:, dr * 4 + h, :], op=ALU.mult),
                                 reads=[qh, qdec], writes=[qd_])
                        S.op("pe", lambda: nc.tensor.matmul(po[:, s * 128:(s + 1) * 128], lhsT=vbl[:, i, :], rhs=sm_[:], start=True, stop=False),
                             reads=[vbl, sm_], writes=[po])
                        for dr in range(2):
                            S.op("pe", lambda: nc.tensor.matmul(po[:, s * 128:(s + 1) * 128], lhsT=snap[:, dr, i, :], rhs=qd_[:, dr, :],
                                                                start=False, stop=(dr == 1)), reads=[snap, qd_], writes=[po])
                    o_ = tmpf.next()
                    S.op("act", lambda: nc.scalar.copy(out=o_[:, :tn], in_=po[:, :tn]), reads=[po], writes=[o_])
                    q_ = tmpf.next()
                    S.op("act", lambda: nc.scalar.activation(out=q_[:, :tn], in_=po[:, :tn], func=AF.Square), reads=[po], writes=[q_])
                    pn = pN.next()
                    S.op("pe", lambda: nc.tensor.matmul(pn[:, :tn], lhsT=ones, rhs=q_[:, :tn], start=True, stop=True), reads=[q_], writes=[pn])
                    S.op("act", lambda: nc.scalar.activation(out=q_[:, :tn], in_=pn[:, :tn], func=AF.Ln, scale=1.0 / 128, bias=epsc),
                         reads=[pn], writes=[q_])
                    S.op("act", lambda: nc.scalar.activation(out=q_[:, :tn], in_=q_[:, :tn], func=AF.Exp, scale=-0.5), reads=[q_], writes=[q_])
                    S.op("dve", lambda: nc.vector.tensor_tensor(out=o_[:, :tn], in0=o_[:, :tn], in1=q_[:, :tn], op=ALU.mult),
                         reads=[o_, q_], writes=[o_])
                    ob = obr.next()
                    S.op("dve", lambda: nc.vector.tensor_tensor(out=ob[:, :tn], in0=o_[:, :tn], in1=gh[:, t0:t0 + tn], op=ALU.mult),
                         reads=[o_, gh], writes=[ob])
                    S.dma("sp", brT_d[BR_B + h * 128:BR_B + (h + 1) * 128, t0:t0 + tn], ob[:, :tn], reads=[ob], writes=[brT_b], par=True)
            S.barrier()

        with ExitStack() as es:
            wb = X.sb(es, "wbr", [128, 16, D], BF16)
            wo = X.sb(es, "wo", [128, 8, D], BF16)
            S.dma("pool", wb[:], w_branch.rearrange("(c p) n -> p c n", p=128), writes=[wb])
            S.dma("pool", wo[:], w_out.rearrange("(c p) n -> p c n", p=128), writes=[wo])
            brg = X.sb(es, "brg", [128, 16, 512], BF16)
            gtg = X.sb(es, "gtg", [128, 32, 512], BF16)
            xg = X.sb(es, "xg3", [128, 8, 512], F32)
            x1g = X.sb(es, "x1g", [128, 8, 512], F32)
            mg = X.sb(es, "mg", [128, 8, 512], BF16)
            h2g = X.sb(es, "h2g", [128, 8, 512], BF16)
            macc = X.ring(es, "macc", [128, 512], F32, 2)
            tmp = X.ring(es, "tmp3", [128, 512], F32, 4)
            tmp.rstd = X.sb(es, "rstd3", [128, 512], F32)
            sqr = X.ring(es, "sq3", [128, 512], F32, 2)
            pp = X.ring(es, "pp3", [128, 512], F32, 4, psum=True)
            pms_r = X.ring(es, "pms3", [128, 512], F32, 2, psum=True)
            for gi, (t0, tn) in enumerate(groups):
                jcol = 1 if gi == 4 else 0
                S.dma("sp", brg[:, :, :tn], brT_d[:, t0:t0 + tn].rearrange("(c p) t -> p c t", p=128), reads=[brT_b], writes=[brg])
                S.dma("sp", gtg[:, :, :tn], gT[512:, t0:t0 + tn].rearrange("(c p) t -> p c t", p=128), writes=[gtg])
                S.dma("sp", xg[:, :, :tn], xci(t0, tn).rearrange("(c p) t -> p c t", p=128), writes=[xg])
                for ob in range(8):
                    m_ = macc.next()
                    for n in range(4):
                        p_ = pp.next()
                        for kc in range(4):
                            S.op("pe", lambda: nc.tensor.matmul(p_[:, :tn], lhsT=wb[:, n * 4 + kc, ob * 128:(ob + 1) * 128],
                                                                rhs=brg[:, n * 4 + kc, :tn], start=(kc == 0), stop=(kc == 3)),
                                 reads=[wb, brg], writes=[p_])
                        if n == 0:
                            S.op("dve", lambda: nc.vector.tensor_tensor(out=m_[:, :tn], in0=p_[:, :tn], in1=gtg[:, n * 8 + ob, :tn], op=ALU.mult),
                                 reads=[p_, gtg], writes=[m_])
                        else:
                            t_ = tmp.next()
                            S.op("dve", lambda: nc.vector.tensor_tensor(out=t_[:, :tn], in0=p_[:, :tn], in1=gtg[:, n * 8 + ob, :tn], op=ALU.mult),
                                 reads=[p_, gtg], writes=[t_])
                            if n < 3:
                                S.op("pool", lambda: nc.gpsimd.tensor_tensor(out=m_[:, :tn], in0=m_[:, :tn], in1=t_[:, :tn], op=ALU.add),
                                     reads=[m_, t_], writes=[m_])
                            else:
                                S.op("pool", lambda: nc.gpsimd.tensor_tensor(out=mg[:, ob, :tn], in0=m_[:, :tn], in1=t_[:, :tn], op=ALU.add),
                                     reads=[m_, t_], writes=[mg])
                for ob in range(8):
                    p_ = pp.next()
                    for kc in range(8):
                        S.op("pe", lambda: nc.tensor.matmul(p_[:, :tn], lhsT=wo[:, kc, ob * 128:(ob + 1) * 128], rhs=mg[:, kc, :tn],
                                                            start=(kc == 0), stop=(kc == 7)), reads=[wo, mg], writes=[p_])
                    S.op("dve", lambda: nc.vector.scalar_tensor_tensor(out=x1g[:, ob, :tn], in0=p_[:, :tn], scalar=modT[:, 16 + ob, jcol:jcol + 1],
                                                                       in1=xg[:, ob, :tn], op0=ALU.mult, op1=ALU.add),
                         reads=[p_, modT, xg], writes=[x1g])
                S.dma("sp", x1_d[:, t0:t0 + tn].rearrange("(c p) t -> p c t", p=128), x1g[:, :, :tn], reads=[x1g], writes=[x1_b], par=True)
                emit_norm_mod(X, nc, S, x1g, tn, jcol, Gm2, _Shift(modT, 24), ones, epsc,
                              sqr, pms_r, tmp, h2g)
                S.dma("sp", h2_d[:, t0:t0 + tn].rearrange("(c p) t -> p c t", p=128), h2g[:, :, :tn], reads=[h2g], writes=[h2_b], par=True)
            S.barrier()
        with ExitStack() as es:
            wfi = X.sb(es, "wfi", [128, 8, 2 * FFN], BF16)
            wfo = X.sb(es, "wfo", [128, 22, D], BF16)
            for c in range(8):
                S.dma("pool", wfi[:, c, :], w_fi[c * 128:(c + 1) * 128, :], writes=[wfi], par=True)
            S.dma("pool", wfo[:], w_fo.rearrange("(c p) n -> p c n", p=128), writes=[wfo])
            TG = 256
            h2r = X.ring(es, "h2r", [128, 8, TG], BF16, 2)
            x1r = X.ring(es, "x1r", [128, 8, TG], F32, 2)
            x2r = X.ring(es, "x2r", [128, 8, TG], F32, 2)
            act = X.sb(es, "actT", [128, 22, TG], BF16)
            sgr = X.ring(es, "sgr", [128, TG], F32, 3)
            pg_r = X.ring(es, "pg", [128, 512], F32, 3, psum=True)
            pu_r = X.ring(es, "pu", [128, 512], F32, 3, psum=True)
            po_r = X.ring(es, "po4", [128, 512], F32, 2, psum=True)
            ntok = NT if need_ctx else NLAT
            for t0 in range(0, ntok, TG):
                jcol = 1 if t0 >= NLAT else 0
                h2 = h2r.next()
                x1 = x1r.next()
                x2 = x2r.next()
                S.dma("sp", h2[:], h2_d[:, t0:t0 + TG].rearrange("(c p) t -> p c t", p=128), reads=[h2_b], writes=[h2])
                S.dma("sp", x1[:], x1_d[:, t0:t0 + TG].rearrange("(c p) t -> p c t", p=128), reads=[x1_b], writes=[x1])
                for fb in range(22):
                    pg, pu = pg_r.next(), pu_r.next()
                    for kc in range(8):
                        S.op("pe", lambda: nc.tensor.matmul(pg[:, :TG], lhsT=wfi[:, kc, fb * 128:(fb + 1) * 128], rhs=h2[:, kc, :],
                                                            start=(kc == 0), stop=(kc == 7)), reads=[wfi, h2], writes=[pg])
                    for kc in range(8):
                        S.op("pe", lambda: nc.tensor.matmul(pu[:, :TG], lhsT=wfi[:, kc, FFN + fb * 128:FFN + (fb + 1) * 128], rhs=h2[:, kc, :],
                                                            start=(kc == 0), stop=(kc == 7)), reads=[wfi, h2], writes=[pu])
                    sg = sgr.next()
                    S.op("act", lambda: nc.scalar.activation(out=sg[:], in_=pg[:, :TG], func=AF.Silu), reads=[pg], writes=[sg])
                    S.op("dve", lambda: nc.vector.tensor_tensor(out=act[:, fb, :], in0=pu[:, :TG], in1=sg[:], op=ALU.mult),
                         reads=[pu, sg], writes=[act])
                for ob in range(8):
                    po = po_r.next()
                    for fb in range(22):
                        S.op("pe", lambda: nc.tensor.matmul(po[:, :TG], lhsT=wfo[:, fb, ob * 128:(ob + 1) * 128], rhs=act[:, fb, :],
                                                            start=(fb == 0), stop=(fb == 21)), reads=[wfo, act], writes=[po])
                    S.op("dve", lambda: nc.vector.scalar_tensor_tensor(out=x2[:, ob, :], in0=po[:, :TG], scalar=modT[:, 40 + ob, jcol:jcol + 1],
                                                                       in1=x1[:, ob, :], op0=ALU.mult, op1=ALU.add),
                         reads=[po, modT, x1], writes=[x2])
                S.dma("sp", xco(t0, TG).rearrange("(c p) t -> p c t", p=128), x2[:], reads=[x2])
        S.barrier()


def make_consts():
    c = np.zeros((128, NCONST), np.float32)
    c[:, C_ONES:C_ONES + 128] = 1.0
    for b in range(2):
        c[b * 64:(b + 1) * 64, C_BD64 + b * 64:C_BD64 + (b + 1) * 64] = 1.0
    c[0:64, C_BD96:C_BD96 + 64] = 1.0
    c[64:96, C_BD96 + 64:C_BD96 + 96] = 1.0
    for p in range(128):
        c[p, C_PSW + (p ^ 1)] = 1.0
        c[p, C_ID + p] = 1.0
    return c


def make_rope_tables(core):
    s = core * NLAT + np.arange(NLAT)
    row = (s // 64).astype(np.float64)
    col = (s % 64).astype(np.float64)

    def ang(dim):
        q = dim // 4
        f = 10000.0 ** (-(np.arange(q, dtype=np.float32) / np.float32(q))).astype(np.float32)
        f = f.astype(np.float32)
        a = np.concatenate([row[:, None].astype(np.float32) * f[None, :], col[:, None].astype(np.float32) * f[None, :]], -1)
        return a.astype(np.float32)
    out = np.zeros((6, 128, NT), np.float32)
    out[0::2, :, :] = 1.0
    a64 = ang(64)
    a32 = ang(32)
    sign = np.where(np.arange(128) % 2 == 0, -1.0, 1.0).astype(np.float32)
    p = np.arange(128)
    idx64 = (p % 64) // 2
    out[0, :, :NLAT] = np.cos(a64)[:, idx64].T
    out[1, :, :NLAT] = (np.sin(a64)[:, idx64] * sign[None, :]).T
    p32 = np.arange(32)
    idx32 = p32 // 2
    out[2, 64:96, :NLAT] = np.cos(a32)[:, idx32].T
    out[3, 64:96, :NLAT] = (np.sin(a32)[:, idx32] * sign[None, :32]).T
    out[4, 0:32, :NLAT] = np.cos(a32)[:, idx32].T
    out[5, 0:32, :NLAT] = (np.sin(a32)[:, idx32] * sign[None, :32]).T
    return out


def tile_col(v, n=128):
    v = np.asarray(v, np.float32).reshape(-1)
    reps = -(-n // v.size)
    return np.tile(v, reps)[:n] if v.size <= n and n % v.size == 0 else np.pad(v, (0, n - v.size))


def make_gvec(inp, l):
    g = np.zeros((128, NGV), np.float32)
    g[:, GV_AQ] = tile_col(inp["gqa_qk_gain"][l, 0])
    g[:, GV_AK] = tile_col(inp["gqa_qk_gain"][l, 1])
    g[:, GV_DQ] = tile_col(inp["diff_qk_gain"][l, 0])
    g[:, GV_DK] = tile_col(inp["diff_qk_gain"][l, 1])
    g[:, GV_CQ:GV_CQ + 3] = inp["mla_cq_gain"][l].reshape(3, 128).T
    g[:, GV_CKV:GV_CKV + 2] = inp["mla_ckv_gain"][l].reshape(2, 128).T
    g[:96, GV_MQ] = inp["mla_qk_gain"][l, 0]
    g[:, GV_MKN] = tile_col(inp["mla_qk_gain"][l, 1, :64])
    g[:32, GV_MKR] = inp["mla_qk_gain"][l, 1, 64:]
    g[:64, GV_SC96] = 1.0 / 64
    g[64:96, GV_SC96] = 1.0 / 32
    g[:, GV_EPS] = EPS
    g[:, GV_SUBLN] = inp["diff_subln_gain"][l]
    return g


class _Shift:
    def __init__(self, buf, base):
        self.buf = buf
        self.base = base
        self.lw = buf.lw
        self.rd = buf.rd

    def __getitem__(self, idx):
        p, c, j = idx
        return self.buf.t[p, self.base + c, j]


def make_rt():
    r = np.zeros((128, NRT), np.float32)
    j = np.arange(128)[:, None].astype(np.float32)
    i = np.arange(128)[None, :].astype(np.float32)
    r[:, RT_RELF:RT_RELF + 128] = np.maximum(i - j, 0)
    r[:, RT_MSKF:RT_MSKF + 128] = (i >= j)
    r[:, RT_RELB:RT_RELB + 128] = np.maximum(j - i, 0)
    r[:, RT_MSKB:RT_MSKB + 128] = (j >= i)
    r[:, RT_QDF:RT_QDF + 128] = i + 1.0
    r[:, RT_QDB:RT_QDB + 128] = 128.0 - i
    r[:, RT_KDF] = 127.0 - np.arange(128)
    r[:, RT_KDB] = np.arange(128)
    for k in range(64):
        r[64 + k, RT_SEL + k] = 1.0
    return r


def make_retE(core, rot=0):
    Eg = _make_retE_global(core)
    perm = [((rot + u // 16) % NCORES) * 16 + u % 16 for u in range(128)] + [128, 129]
    return np.ascontiguousarray(Eg[:, :, perm])


def _make_retE_global(core):
    E = np.zeros((128, 4, NKT), np.float32)
    j = np.arange(128).astype(np.float32)
    T0 = core * 16
    pos = np.zeros(NKT)
    pos[128], pos[129] = 0, 1
    pos[:128] = 2 + np.arange(128)
    P0 = 2 + T0
    for u in range(NKT):
        if pos[u] < P0:
            E[:, 0, u] = 127.0 - j + 128.0 * (P0 - 1 - pos[u])
            E[:, 1, u] = 1.0
    pos[129], pos[128] = 0, 1
    pos[:128] = 2 + 127 - np.arange(128)
    P0 = 2 + 127 - (T0 + 15)
    for u in range(NKT):
        if pos[u] < P0:
            E[:, 2, u] = j + 128.0 * (P0 - 1 - pos[u])
            E[:, 3, u] = 1.0
    return E


class _NS:
    pass


def build_fused():
    nc = bass.Bass("TRN2", target_bir_lowering=False)
    di = lambda name, shape, dt=F32: nc.dram_tensor(name, shape, dt, kind="ExternalInput").ap()
    scr = lambda name, shape, dt=BF16: nc.dram_tensor(name, shape, dt, kind="Internal").ap()
    T = _NS()
    xT_all = di("xT_all", [D, NKEY])
    cT = di("cT", [D, 2])
    w_mod = di("w_mod", [DEPTH, D, 6 * D])
    bmodT = di("bmodT", [DEPTH, 128, 48])
    T.g0T = di("g0T", [DEPTH, 128, 8])
    T.g1T = di("g1T", [DEPTH, 128, 8])
    T.w_in = di("w_in", [DEPTH, D, IN_W])
    T.w_uq = di("w_uq", [DEPTH, 384, 768])
    T.w_ukv = di("w_ukv", [DEPTH, 256, 1024])
    T.gvec = di("gvec", [DEPTH, 128, NGV])
    T.consts = di("consts", [128, NCONST])
    T.rope = di("rope", [NCORES, 6, 128, NT])
    T.w_branch = di("w_branch", [DEPTH, 2048, D])
    T.w_out = di("w_out", [DEPTH, D, D])
    T.w_fi = di("w_fi", [DEPTH, D, 2 * FFN])
    T.w_fo = di("w_fo", [DEPTH, FFN, D])
    T.rt = di("rt", [128, NRT])
    T.retE = di("retE", [NCORES, 128, 4, NKT])
    T.retd = di("retd", [DEPTH, 128, 8])
    T.dlam = di("dlam", [DEPTH, 128, 256])
    out = nc.dram_tensor("xTo", [D, NLAT], F32, kind="ExternalOutput").ap()
    x1_all = scr("x1_all", [D, NKEY], F32)
    T.xin = [xT_all, x1_all]
    T.xout = [x1_all, out]
    T.modT_s = [scr("modT_s%d" % l, [128, 96], F32) for l in range(DEPTH)]
    T.qT_s = [scr("qT_s%d" % v, [NQ, NT]) for v in range(NCORES)]
    T.kT_s = [scr("kT_s%d" % v, [NK, NT]) for v in range(NCORES)]
    T.v_s = [scr("v_s%d" % v, [NT, NV]) for v in range(NCORES)]
    T.gT_s = [scr("gT_s%d" % v, [NG, NT]) for v in range(NCORES)]
    T.hT_s = scr("hT_s", [D, NT])
    T.kTg = scr("kTg", [NK, NKEY])
    T.vg = scr("vg", [NKEY, NV])
    T.brT_s = scr("brT_s", [2048, NT])
    T.x1T_s = scr("x1T_s", [D, NT], F32)
    T.h2T_s = scr("h2T_s", [D, NT])
    with ExitStack() as esr:
        X = Ctx(nc, esr)
        S = X.S
        for l in range(DEPTH):
            with ExitStack() as esm:
                dummy = X.sb(esm, "modkeep", [128, 1], F32)
                with ExitStack() as est:
                    emit_modvec(X, esm, est, cT, w_mod[l], bmodT[l], T.modT_s[l])
                    S.barrier()
            for v in range(NCORES):
                emit_A(X, T, l, v, with_ctx=(v == 0), kv_only=(l == DEPTH - 1 and v > 0))
                S.dma("sp", T.kTg[:, v * NLAT:(v + 1) * NLAT], T.kT_s[v][:, 0:NLAT])
                S.dma("sp", T.vg[v * NLAT:(v + 1) * NLAT, :], T.v_s[v][0:NLAT, :])
                if v == 0:
                    S.dma("sp", T.kTg[:, 8 * NLAT:], T.kT_s[v][:, NLAT:])
                    S.dma("sp", T.vg[8 * NLAT:, :], T.v_s[v][NLAT:, :])
            S.barrier()
            for v in (range(NCORES) if l < DEPTH - 1 else [0]):
                emit_B(X, T, l, v, need_ctx=(l < DEPTH - 1 and v == 0))
        S.finish("sp")
        print("fused: instructions", S.nins, "waits", S.nwaits)
    return nc


def inputs_fused(inp):
    consts = make_consts()
    rtab = make_rt()
    x = inp["x"][0]
    ctx = inp["ctx"][0]
    c_cols = np.ascontiguousarray(np.stack([inp["c"][0], inp["c_ctx"]], 1))
    shared = {
        "cT": c_cols,
        "w_mod": np.ascontiguousarray(inp["w_mod"]),
        "bmodT": np.ascontiguousarray(inp["b_mod"].reshape(DEPTH, 48, 128).transpose(0, 2, 1)),
        "g0T": np.ascontiguousarray(inp["norm_gain"][:, 0].reshape(DEPTH, 8, 128).transpose(0, 2, 1)),
        "g1T": np.ascontiguousarray(inp["norm_gain"][:, 1].reshape(DEPTH, 8, 128).transpose(0, 2, 1)),
        "w_in": np.ascontiguousarray(inp["w_in"]),
        "w_uq": np.ascontiguousarray(inp["mla_w_uq"]),
        "w_ukv": np.ascontiguousarray(inp["mla_w_ukv"]),
        "gvec": np.stack([make_gvec(inp, l) for l in range(DEPTH)]),
        "consts": consts,
        "w_branch": np.ascontiguousarray(inp["w_branch"].reshape(DEPTH, 2048, D)),
        "w_out": np.ascontiguousarray(inp["w_out"]),
        "w_fi": np.ascontiguousarray(inp["w_ffn_in"]),
        "w_fo": np.ascontiguousarray(inp["w_ffn_out"]),
        "rt": rtab,
        "retd": np.ascontiguousarray(np.tile(inp["ret_decay"].reshape(DEPTH, 1, 8), (1, 128, 1))),
        "dlam": np.ascontiguousarray(np.tile(inp["diff_lambda"].reshape(DEPTH, 1, 256), (1, 128, 1))),
    }
    maps = []
    for r in range(NCORES):
        order = [(r + v) % NCORES for v in range(NCORES)]
        xrot = np.concatenate([x[g * NLAT:(g + 1) * NLAT] for g in order] + [ctx], 0)
        m = dict(shared)
        m["xT_all"] = np.ascontiguousarray(xrot.T)
        m["rope"] = np.stack([make_rope_tables(g) for g in order])
        m["retE"] = np.stack([make_retE(g, r) for g in order])
        maps.append(m)
    return maps


_PROG = {}


def kernel(**inp):
    inp = {k: np.asarray(v) for k, v in inp.items()}
    if "f" not in _PROG:
        _PROG["f"] = build_fused()
    res = run_bass_kernel_spmd(_PROG["f"], inputs_fused(inp), core_ids=list(range(NCORES)))
    out = np.concatenate([np.asarray(res.results[r]["xTo"]).T for r in range(NCORES)], 0)[None]
    return np.ascontiguousarray(out.astype(np.float32))
```

```python
import math
import numpy as np
import ml_dtypes
import concourse.bass as bass
import concourse.mybir as mybir
from concourse.bass_utils import run_bass_kernel_spmd
from contextlib import ExitStack

F32 = mybir.dt.float32
BF16 = mybir.dt.bfloat16
AF = mybir.ActivationFunctionType
ALU = mybir.AluOpType

NCORES = 8
D = 1024
SEQ = 16384
NLAT = SEQ // NCORES
NCTX = 256
NT = NLAT + NCTX
GROUPS = [(0, 512), (512, 512), (1024, 512), (1536, 512), (2048, 256)]
DEPTH = 2
FFN = 2816
EPS = 1e-6

O_AQ, O_AK, O_AV = 0, 512, 640
O_BQ, O_BK, O_BV, O_BG = 768, 1024, 1280, 1792
O_CQ, O_CKV, O_CKR = 2304, 2688, 2944
O_DQ, O_DK, O_DV = 2976, 3488, 4000
O_G = 4512
IN_W = 8608
QA, QB, QC, QD = 0, 512, 768, 1536
NQ = 2048
KA, KCN, KCR, KD, KB = 0, 128, 640, 672, 1184
NK = 1440
VA, VC, VD, VB, VBK = 0, 128, 640, 1152, 1664
NV = 1920
NG = 4608
GV_AQ, GV_AK, GV_DQ, GV_DK, GV_CQ, GV_CKV, GV_MQ, GV_MKN, GV_MKR, GV_SC96, GV_EPS, GV_SUBLN = 0, 1, 2, 3, 4, 7, 9, 10, 11, 12, 13, 14
NGV = 16
C_ONES, C_BD64, C_BD96, C_PSW, C_ID = 0, 128, 256, 384, 512
NCONST = 640


class Buf:
    __slots__ = ("t", "lw", "rd", "name")

    def __init__(self, t, name=""):
        self.t = t
        self.lw = {}
        self.rd = {}
        self.name = name

    def __getitem__(self, idx):
        return self.t[idx]


class Ring:
    def __init__(self, bufs):
        self.bufs = bufs
        self.i = 0

    def next(self):
        b = self.bufs[self.i % len(self.bufs)]
        self.i += 1
        return b


class Sync:
    def __init__(self, nc, es, n_dma_sems=24):
        self.nc = nc
        self.engs = {"pe": nc.tensor, "act": nc.scalar, "dve": nc.vector,
                     "pool": nc.gpsimd, "sp": nc.sync}
        self.sems = {}
        self.cnt = {}
        for k in self.engs:
            self.sems[k] = es.enter_context(nc.semaphore("s_" + k))
            self.cnt[k] = 0
        self.dsems = []
        for i in range(n_dma_sems):
            key = "d%d" % i
            self.sems[key] = es.enter_context(nc.semaphore("s_" + key))
            self.cnt[key] = 0
            self.dsems.append(key)
        self.dnext = 0
        self.seen = {k: {} for k in self.engs}
        self.nwaits = 0
        self.nins = 0

    def _wait(self, e, tok):
        if tok is None:
            return
        key, val = tok
        if key == e:
            return
        if self.seen[e].get(key, 0) >= val:
            return
        self.engs[e].wait_ge(self.sems[key], val)
        self.seen[e][key] = val
        self.nwaits += 1

    def _deps(self, e, reads, writes, par=False):
        for b in reads:
            for k, v in b.lw.items():
                self._wait(e, (k, v))
        for b in writes:
            for k, v in b.lw.items():
                if par and k[0] == "d":
                    continue
                self._wait(e, (k, v))
            for k, v in b.rd.items():
                self._wait(e, (k, v))

    def _mark(self, tok, reads, writes, par=False):
        for b in reads:
            b.rd[tok[0]] = tok[1]
        for b in writes:
            if par:
                b.lw[tok[0]] = tok[1]
            else:
                b.lw = {tok[0]: tok[1]}
                b.rd = {}

    def op(self, e, fn, reads=(), writes=()):
        self._deps(e, reads, writes)
        ins = fn()
        self.cnt[e] += 1
        ins.then_inc(self.sems[e], 1)
        tok = (e, self.cnt[e])
        self._mark(tok, reads, writes)
        self.nins += 1
        return tok

    def dma(self, q, out, in_, reads=(), writes=(), par=False, **kw):
        key = self.dsems[self.dnext]
        self.dnext = (self.dnext + 1) % len(self.dsems)
        self._wait(q, (key, self.cnt[key]))
        self._deps(q, reads, writes, par)
        ins = self.engs[q].dma_start(out=out, in_=in_, **kw)
        self.cnt[key] += 16
        ins.then_inc(self.sems[key], 16)
        tok = (key, self.cnt[key])
        self._mark(tok, reads, writes, par)
        self.nins += 1
        return tok

    def barrier(self):
        for e in self.engs:
            for k, v in self.cnt.items():
                if k != e and v > 0:
                    self._wait(e, (k, v))

    def finish(self, e="sp"):
        for k, v in self.cnt.items():
            if k != e and v > 0:
                self._wait(e, (k, v))


class Ctx:
    def __init__(self, nc, es):
        self.nc = nc
        self.es = es
        self.S = Sync(nc, es)
        self.uid = 0

    def sb(self, es, name, shape, dt):
        self.uid += 1
        return Buf(es.enter_context(self.nc.sbuf_tensor("sb%d_%s" % (self.uid, name), shape, dt)), name)

    def ps(self, es, name, shape, dt=F32):
        self.uid += 1
        return Buf(es.enter_context(self.nc.psum_tensor("ps%d_%s" % (self.uid, name), shape, dt)), name)

    def ring(self, es, name, shape, dt, n, psum=False):
        mk = self.ps if psum else self.sb
        return Ring([mk(es, "%s%d" % (name, i), shape, dt) for i in range(n)])


def emit_modvec(X, esp, es, cT, w_mod, bmodT, modT_d):
    nc, S = X.nc, X.S
    modT = X.sb(esp, "modT_sb", [128, 48, 2], F32)
    csb = X.sb(es, "csb", [128, 8, 2], F32)
    sg = X.sb(es, "csg", [128, 8, 2], F32)
    bm = X.sb(es, "bmod", [128, 48], F32)
    pm = X.ps(es, "pmod", [128, 48, 2], F32)
    wr = X.ring(es, "wmod", [128, 8, 512], F32, 2)
    S.dma("sp", csb[:], cT.rearrange("(c p) j -> p c j", p=128), writes=[csb])
    S.dma("sp", bm[:], bmodT, writes=[bm])
    S.op("act", lambda: nc.scalar.activation(out=sg[:], in_=csb[:], func=AF.Sigmoid), reads=[csb], writes=[sg])
    S.op("dve", lambda: nc.vector.tensor_tensor(out=sg[:], in0=sg[:], in1=csb[:], op=ALU.mult),
         reads=[sg, csb], writes=[sg])
    for ch in range(12):
        w = wr.next()
        S.dma("sp", w[:], w_mod[:, ch * 512:(ch + 1) * 512].rearrange("(c p) n -> p c n", p=128), writes=[w])
        for nb in range(4):
            for kc in range(8):
                S.op("pe", lambda: nc.tensor.matmul(pm[:, ch * 4 + nb, :], lhsT=w[:, kc, nb * 128:(nb + 1) * 128],
                                                    rhs=sg[:, kc, :], start=(kc == 0), stop=(kc == 7)),
                     reads=[w, sg], writes=[pm])
    for j in range(2):
        S.op("dve", lambda: nc.vector.tensor_tensor(out=modT[:, :, j], in0=pm[:, :, j], in1=bm[:], op=ALU.add),
             reads=[pm, bm], writes=[modT])
    S.dma("sp", modT_d.rearrange("p (a j) -> p a j", j=2), modT[:], reads=[modT])
    return modT


def emit_norm_mod(X, nc, S, xg, tn, jcol, Gm, Sft, ones, epsc, sqr, pms_r, tmp_r, hT_out):
    pms = pms_r.next()
    for c in range(8):
        sq = sqr.next()
        S.op("act", lambda: nc.scalar.activation(out=sq[:, :tn], in_=xg[:, c, :tn], func=AF.Square),
             reads=[xg], writes=[sq])
        S.op("pe", lambda: nc.tensor.matmul(pms[:, :tn], lhsT=ones, rhs=sq[:, :tn], start=(c == 0), stop=(c == 7)),
             reads=[sq], writes=[pms])
    rstd = getattr(tmp_r, "rstd", None) or tmp_r.next()
    S.op("act", lambda: nc.scalar.activation(out=rstd[:, :tn], in_=pms[:, :tn], func=AF.Ln, scale=1.0 / D, bias=epsc),
         reads=[pms], writes=[rstd])
    S.op("act", lambda: nc.scalar.activation(out=rstd[:, :tn], in_=rstd[:, :tn], func=AF.Exp, scale=-0.5),
         reads=[rstd], writes=[rstd])
    for c in range(8):
        t = tmp_r.next()
        S.op("dve", lambda: nc.vector.scalar_tensor_tensor(out=t[:, :tn], in0=xg[:, c, :tn], scalar=Gm[:, c, jcol:jcol + 1],
                                                           in1=rstd[:, :tn], op0=ALU.mult, op1=ALU.mult),
             reads=[xg, Gm, rstd], writes=[t])
        S.op("act", lambda: nc.scalar.activation(out=hT_out[:, c, :tn], in_=t[:, :tn], func=AF.Identity,
                                                 bias=Sft[:, c, jcol:jcol + 1]),
             reads=[t, Sft], writes=[hT_out])


def emit_A(X, T, l, v, with_ctx, kv_only):
    nc, S = X.nc, X.S
    groups = GROUPS if with_ctx else GROUPS[:4]
    xin = T.xin[l]

    def xc(t0, tn):
        c0 = (8 * NLAT + t0 - NLAT) if t0 >= NLAT else (v * NLAT + t0)
        return xin[:, c0:c0 + tn]
    w_in, w_uq, w_ukv, g0T, gvec, consts, rope = T.w_in[l], T.w_uq[l], T.w_ukv[l], T.g0T[l], T.gvec[l], T.consts, T.rope[v]
    qT_d, kT_d, v_d, gT_d = T.qT_s[v], T.kT_s[v], T.v_s[v], T.gT_s[v]
    modT_i = T.modT_s[l]
    hT_d = T.hT_s
    hT_db = Buf(hT_d, "hT_scr")
    with ExitStack() as es0:
        S.barrier()
        cst = X.sb(es0, "cst", [128, NCONST], F32)
        gv = X.sb(es0, "gv", [128, NGV], F32)
        S.dma("sp", cst[:], consts, writes=[cst])
        S.dma("sp", gv[:], gvec, writes=[gv])
        identb = X.sb(es0, "identb", [128, 128], BF16)
        S.op("dve", lambda: nc.vector.tensor_copy(out=identb[:], in_=cst[:, C_ID:C_ID + 128]), reads=[cst], writes=[identb])
        ones = cst[:, C_ONES:C_ONES + 128]
        epsc = gv[:, GV_EPS:GV_EPS + 1]
        modT = X.sb(es0, "modTa", [128, 48, 2], F32)
        S.dma("sp", modT[:], modT_i.rearrange("p (a j) -> p a j", j=2), writes=[modT])
        g0 = X.sb(es0, "g0", [128, 8], F32)
        S.dma("sp", g0[:], g0T, writes=[g0])
        Gm = X.sb(es0, "Gm", [128, 8, 2], F32)
        for j in range(2):
            S.op("dve", lambda: nc.vector.scalar_tensor_tensor(out=Gm[:, :, j], in0=modT[:, 8:16, j], scalar=1.0,
                                                               in1=g0[:], op0=ALU.add, op1=ALU.mult),
                 reads=[modT, g0], writes=[Gm])
        Sft = modT
        S.barrier()

        with ExitStack() as es:
            NW1 = O_G
            w1 = X.sb(es, "w1", [128, 8, NW1], BF16)
            for c in range(8):
                S.dma("pool", w1[:, c, :], w_in[c * 128:(c + 1) * 128, 0:NW1], writes=[w1], par=True)
            wuq = X.sb(es, "wuq", [128, 3, 768], BF16)
            S.dma("pool", wuq[:], w_uq.rearrange("(c p) n -> p c n", p=128), writes=[wuq])
            wukv = X.sb(es, "wukv", [128, 2, 2, 8, 64], BF16)
            for c in range(2):
                for t in range(2):
                    S.dma("pool", wukv[:, c, t], w_ukv[c * 128:(c + 1) * 128, :].rearrange("p (h t e) -> p t h e", h=8, t=2)[:, t],
                          writes=[wukv], par=True)
            xg = X.sb(es, "xg", [128, 8, 512], F32)
            hTg = X.sb(es, "hTg", [128, 8, 512], BF16)
            tab = X.sb(es, "tab", [128, 6, 512], F32)
            sqr = X.ring(es, "sq", [128, 512], F32, 2)
            tmp = X.ring(es, "tmp", [128, 512], F32, 8)
            tmp.rstd = X.sb(es, "rstd1", [128, 512], F32)
            xsr = X.ring(es, "xs", [128, 512], F32, 4)
            outr = X.ring(es, "ob", [128, 512], BF16, 4)
            cqn = X.sb(es, "cqn", [128, 3, 512], BF16)
            ckvn = X.sb(es, "ckvn", [128, 2, 512], BF16)
            pmm = X.ring(es, "pmm", [128, 512], F32, 3, psum=True)
            pms_r = X.ring(es, "pms", [128, 512], F32, 2, psum=True)
            prx = X.ring(es, "prx", [128, 512], F32, 2, psum=True)
            ptr = X.ps(es, "ptr", [128, 4, 128], BF16)

            def mm_fm(pm, M, tn, lhs_fn, rhs_fn, nk, rd):
                for c in range(nk):
                    S.op("pe", lambda: nc.tensor.matmul(pm[0:M, :tn], lhsT=lhs_fn(c), rhs=rhs_fn(c),
                                                        start=(c == 0), stop=(c == nk - 1)),
                         reads=rd, writes=[pm])

            def rstd_from(xsq_list, M, tn, bd, scale):
                pst = pms_r.next()
                n = len(xsq_list)
                for i, sq in enumerate(xsq_list):
                    S.op("pe", lambda: nc.tensor.matmul(pst[0:M, :tn], lhsT=bd, rhs=sq[0:M, :tn],
                                                        start=(i == 0), stop=(i == n - 1)),
                         reads=[sq, cst], writes=[pst])
                r = tmp.next()
                S.op("act", lambda: nc.scalar.activation(out=r[0:M, :tn], in_=pst[0:M, :tn], func=AF.Ln, scale=scale,
                                                         bias=epsc[0:M, :]),
                     reads=[pst, gv], writes=[r])
                S.op("act", lambda: nc.scalar.activation(out=r[0:M, :tn], in_=r[0:M, :tn], func=AF.Exp, scale=-0.5),
                     reads=[r], writes=[r])
                return r

            def rope_store(xn, M, tn, ti, dst_ap, dst_buf=None):
                pr = prx.next()
                S.op("pe", lambda: nc.tensor.matmul(pr[0:M, :tn], lhsT=cst[0:M, C_PSW:C_PSW + M], rhs=xn[0:M, :tn],
                                                    start=True, stop=True), reads=[xn, cst], writes=[pr])
                t1 = tmp.next()
                S.op("pool", lambda: nc.gpsimd.tensor_tensor(out=t1[0:M, :tn], in0=xn[0:M, :tn], in1=tab[0:M, 2 * ti, :tn],
                                                             op=ALU.mult), reads=[xn, tab], writes=[t1])
                t2 = tmp.next()
                S.op("dve", lambda: nc.vector.tensor_tensor(out=t2[0:M, :tn], in0=pr[0:M, :tn], in1=tab[0:M, 2 * ti + 1, :tn],
                                                            op=ALU.mult), reads=[pr, tab], writes=[t2])
                ob = outr.next()
                S.op("dve", lambda: nc.vector.tensor_tensor(out=ob[0:M, :tn], in0=t1[0:M, :tn], in1=t2[0:M, :tn], op=ALU.add),
                     reads=[t1, t2], writes=[ob])
                S.dma("sp", dst_ap, ob[0:M, :tn], reads=[ob])
                return ob

            def job_norm(pm, M, tn, bd, scale, gcol, ti, dst_ap):
                xs = xsr.next()
                S.op("act", lambda: nc.scalar.copy(out=xs[0:M, :tn], in_=pm[0:M, :tn]), reads=[pm], writes=[xs])
                sq = sqr.next()
                S.op("act", lambda: nc.scalar.activation(out=sq[0:M, :tn], in_=pm[0:M, :tn], func=AF.Square),
                     reads=[pm], writes=[sq])
                r = rstd_from([sq], M, tn, bd, scale)
                if ti is None:
                    ob = outr.next()
                    S.op("dve", lambda: nc.vector.scalar_tensor_tensor(out=ob[0:M, :tn], in0=xs[0:M, :tn],
                                                                       scalar=gv[0:M, gcol:gcol + 1], in1=r[0:M, :tn],
                                                                       op0=ALU.mult, op1=ALU.mult),
                         reads=[xs, gv, r], writes=[ob])
                    S.dma("sp", dst_ap, ob[0:M, :tn], reads=[ob])
                    return ob
                xn = tmp.next()
                S.op("dve", lambda: nc.vector.scalar_tensor_tensor(out=xn[0:M, :tn], in0=xs[0:M, :tn],
                                                                   scalar=gv[0:M, gcol:gcol + 1], in1=r[0:M, :tn],
                                                                   op0=ALU.mult, op1=ALU.mult),
                     reads=[xs, gv, r], writes=[xn])
                return rope_store(xn, M, tn, ti, dst_ap)

            bd64 = cst[:, C_BD64:C_BD64 + 128]
            for gi, (t0, tn) in enumerate(groups):
                jcol = 1 if gi == 4 else 0
                S.dma("sp", xg[:, :, :tn], xc(t0, tn).rearrange("(c p) t -> p c t", p=128), writes=[xg])
                S.dma("sp", tab[:, :, :tn], rope[:, :, t0:t0 + tn].rearrange("s p t -> p s t"), writes=[tab])
                emit_norm_mod(X, nc, S, xg, tn, jcol, Gm, Sft, ones, epsc, sqr, pms_r, tmp, hTg)
                S.dma("sp", hT_d[:, t0:t0 + tn].rearrange("(c p) t -> p c t", p=128), hTg[:, :, :tn], reads=[hTg],
                      writes=[hT_db], par=True)
                hrhs = lambda c: hTg[:, c, :tn]
                fam = [(O_AQ, 4, GV_AQ, qT_d, QA), (O_AK, 1, GV_AK, kT_d, KA),
                       (O_DQ, 4, GV_DQ, qT_d, QD), (O_DK, 4, GV_DK, kT_d, KD)]
                if kv_only:
                    fam = [f_ for f_ in fam if f_[3] is kT_d]
                for (co, nb, gcol, dst, r0) in fam:
                    for b in range(nb):
                        pm = pmm.next()
                        mm_fm(pm, 128, tn, lambda c: w1[:, c, co + b * 128:co + (b + 1) * 128], hrhs, 8, [w1, hTg])
                        job_norm(pm, 128, tn, bd64, 1.0 / 64, gcol, 0, dst[r0 + b * 128:r0 + (b + 1) * 128, t0:t0 + tn])
                for (co, nb, scl, dst, r0, tokmaj) in ([] if kv_only else [(O_BQ, 2, 1.0, qT_d, QB, False)]) + [(O_BK, 2, 0.125, kT_d, KB, True)]:
                    for b in range(nb):
                        pm = pmm.next()
                        mm_fm(pm, 128, tn, lambda c: w1[:, c, co + b * 128:co + (b + 1) * 128], hrhs, 8, [w1, hTg])
                        xn = tmp.next()
                        S.op("act", lambda: nc.scalar.mul(out=xn[:, :tn], in_=pm[:, :tn], mul=scl),
                             reads=[pm], writes=[xn])
                        ob = rope_store(xn, 128, tn, 0, dst[r0 + b * 128:r0 + (b + 1) * 128, t0:t0 + tn])
                        if tokmaj:
                            nsub = tn // 128
                            for s in range(nsub):
                                S.op("pe", lambda: nc.tensor.transpose(out=ptr[:, s, :], in_=ob[:, s * 128:(s + 1) * 128],
                                                                       identity=identb[:]),
                                     reads=[ob, identb], writes=[ptr])
                            kb = outr.next()
                            S.op("dve", lambda: nc.vector.tensor_copy(out=kb[:, :tn], in_=ptr[:, 0:nsub, :].rearrange("p s e -> p (s e)")),
                                 reads=[ptr], writes=[kb])
                            S.dma("sp", v_d[t0:t0 + tn, VBK + b * 128:VBK + (b + 1) * 128].rearrange("(s p) e -> p s e", p=128),
                                  kb[:, :tn].rearrange("p (s e) -> p s e", e=128), reads=[kb])
                for b in range(0 if kv_only else 4):
                    pm = pmm.next()
                    mm_fm(pm, 128, tn, lambda c: w1[:, c, O_BG + b * 128:O_BG + (b + 1) * 128], hrhs, 8, [w1, hTg])
                    ob = outr.next()
                    S.op("act", lambda: nc.scalar.activation(out=ob[:, :tn], in_=pm[:, :tn], func=AF.Silu),
                         reads=[pm], writes=[ob])
                    S.dma("sp", gT_d[b * 128:(b + 1) * 128, t0:t0 + tn], ob[:, :tn], reads=[ob])
                for (co, nb, gc0, dstt, nfeat) in ([] if kv_only else [(O_CQ, 3, GV_CQ, cqn, 384.0)]) + [(O_CKV, 2, GV_CKV, ckvn, 256.0)]:
                    xss, sqs = [], []
                    for b in range(nb):
                        pm = pmm.next()
                        mm_fm(pm, 128, tn, lambda c: w1[:, c, co + b * 128:co + (b + 1) * 128], hrhs, 8, [w1, hTg])
                        xs = xsr.next()
                        S.op("act", lambda: nc.scalar.copy(out=xs[:, :tn], in_=pm[:, :tn]), reads=[pm], writes=[xs])
                        sq = tmp.next()
                        S.op("act", lambda: nc.scalar.activation(out=sq[:, :tn], in_=pm[:, :tn], func=AF.Square),
                             reads=[pm], writes=[sq])
                        xss.append(xs)
                        sqs.append(sq)
                    r = rstd_from(sqs, 128, tn, ones, 1.0 / nfeat)
                    for b in range(nb):
                        S.op("dve", lambda: nc.vector.scalar_tensor_tensor(out=dstt[:, b, :tn], in0=xss[b][:, :tn],
                                                                           scalar=gv[:, gc0 + b:gc0 + b + 1], in1=r[:, :tn],
                                                                           op0=ALU.mult, op1=ALU.mult),
                             reads=[xss[b], gv, r], writes=[dstt])
                pm = pmm.next()
                mm_fm(pm, 32, tn, lambda c: w1[:, c, O_CKR:O_CKR + 32], hrhs, 8, [w1, hTg])
                job_norm(pm, 32, tn, cst[0:32, C_BD64:C_BD64 + 32], 1.0 / 32, GV_MKR, 2, kT_d[KCR:KCR + 32, t0:t0 + tn])
                for h in range(0 if kv_only else 8):
                    pm = pmm.next()
                    mm_fm(pm, 96, tn, lambda c: wuq[:, c, h * 96:(h + 1) * 96], lambda c: cqn[:, c, :tn], 3, [wuq, cqn])
                    job_norm(pm, 96, tn, cst[0:96, C_BD96:C_BD96 + 96], gv[0:96, GV_SC96:GV_SC96 + 1], GV_MQ, 1,
                             qT_d[QC + h * 96:QC + (h + 1) * 96, t0:t0 + tn])
                for hp in range(4):
                    pm = pmm.next()
                    mm_fm(pm, 128, tn, lambda c: wukv[:, c, 0, 2 * hp:2 * hp + 2, :].rearrange("p h e -> p (h e)"),
                          lambda c: ckvn[:, c, :tn], 2, [wukv, ckvn])
                    job_norm(pm, 128, tn, bd64, 1.0 / 64, GV_MKN, None, kT_d[KCN + hp * 128:KCN + (hp + 1) * 128, t0:t0 + tn])
                for s in range(tn // 128):
                    ts_ = slice(s * 128, (s + 1) * 128)
                    vjobs = [(lambda c: hTg[:, c, ts_], lambda c: w1[:, c, O_AV:O_AV + 128], 8, 128, VA, [hTg, w1]),
                             (lambda c: hTg[:, c, ts_], lambda c: w1[:, c, O_BV:O_BV + 512], 8, 512, VB, [hTg, w1]),
                             (lambda c: hTg[:, c, ts_], lambda c: w1[:, c, O_DV:O_DV + 512], 8, 512, VD, [hTg, w1]),
                             (lambda c: ckvn[:, c, ts_], lambda c: wukv[:, c, 1].rearrange("p h e -> p (h e)"), 2, 512, VC,
                              [ckvn, wukv])]
                    for (lf, rf, nk, ncol, vo, rd) in vjobs:
                        pm = pmm.next()
                        for c in range(nk):
                            S.op("pe", lambda: nc.tensor.matmul(pm[:, :ncol], lhsT=lf(c), rhs=rf(c), start=(c == 0),
                                                                stop=(c == nk - 1)), reads=rd, writes=[pm])
                        ob = outr.next()
                        S.op("act", lambda: nc.scalar.copy(out=ob[:, :ncol], in_=pm[:, :ncol]), reads=[pm], writes=[ob])
                        S.dma("sp", v_d[t0 + s * 128:t0 + (s + 1) * 128, vo:vo + ncol], ob[:, :ncol], reads=[ob])
            S.barrier()
        with ExitStack() as es:
          if not kv_only:
              w2 = X.sb(es, "w2", [128, 8, 4096], BF16)
              for c in range(8):
                  S.dma("pool", w2[:, c, :], w_in[c * 128:(c + 1) * 128, O_G:O_G + 4096], writes=[w2], par=True)
              hr = X.ring(es, "hT2", [128, 8, 512], BF16, 2)
              outr = X.ring(es, "ob2", [128, 512], BF16, 4)
              pmm = X.ring(es, "pmm2", [128, 512], F32, 4, psum=True)
              for gi, (t0, tn) in enumerate(groups):
                  hTg = hr.next()
                  S.dma("sp", hTg[:, :, :tn], hT_d[:, t0:t0 + tn].rearrange("(c p) t -> p c t", p=128), reads=[hT_db],
                        writes=[hTg])
                  for b in range(32):
                      pm = pmm.next()
                      for c in range(8):
                          S.op("pe", lambda: nc.tensor.matmul(pm[:, :tn], lhsT=w2[:, c, b * 128:(b + 1) * 128], rhs=hTg[:, c, :tn],
                                                              start=(c == 0), stop=(c == 7)), reads=[w2, hTg], writes=[pm])
                      ob = outr.next()
                      S.op("act", lambda: nc.scalar.activation(out=ob[:, :tn], in_=pm[:, :tn], func=AF.Sigmoid),
                           reads=[pm], writes=[ob])
                      S.dma("sp", gT_d[512 + b * 128:512 + (b + 1) * 128, t0:t0 + tn], ob[:, :tn], reads=[ob])
        S.barrier()


NKT = 130
NKEY = NKT * 128
BR_A, BR_B, BR_C, BR_D = 0, 512, 1024, 1536
RT_RELF, RT_MSKF, RT_RELB, RT_MSKB, RT_QDF, RT_QDB, RT_KDF, RT_KDB, RT_SEL = 0, 128, 256, 384, 512, 640, 768, 769, 770
NRT = 770 + 64
AX = mybir.AxisListType


def emit_B(X, T, l, v, need_ctx):
    nc, S = X.nc, X.S
    layer = l
    lam_init = 0.8 - 0.6 * math.exp(-0.3 * layer)
    groups = GROUPS if need_ctx else GROUPS[:4]
    xin = T.xin[l]
    xout = T.xout[l]

    def xci(t0, tn):
        c0 = (8 * NLAT + t0 - NLAT) if t0 >= NLAT else (v * NLAT + t0)
        return xin[:, c0:c0 + tn]

    def xco(t0, tn):
        if l == DEPTH - 1:
            return xout[:, t0:t0 + tn]
        c0 = (8 * NLAT + t0 - NLAT) if t0 >= NLAT else (v * NLAT + t0)
        return xout[:, c0:c0 + tn]
    qT, kTl, vl, kTg, vg, gT = T.qT_s[v], T.kT_s[v], T.v_s[v], T.kTg, T.vg, T.gT_s[v]
    modT_i, g1T = T.modT_s[l], T.g1T[l]
    w_branch, w_out, w_fi, w_fo = T.w_branch[l], T.w_out[l], T.w_fi[l], T.w_fo[l]
    gvec, consts, rt_i, retE, retd_i, dlam_i = T.gvec[l], T.consts, T.rt, T.retE[v], T.retd[l], T.dlam[l]
    brT_d, x1_d, h2_d = T.brT_s, T.x1T_s, T.h2T_s
    brT_b, x1_b, h2_b = Buf(brT_d, "brT"), Buf(x1_d, "x1"), Buf(h2_d, "h2")
    with ExitStack() as es0:
        S.barrier()
        cst = X.sb(es0, "cst", [128, NCONST], F32)
        gv = X.sb(es0, "gv", [128, NGV], F32)
        modT = X.sb(es0, "modTs", [128, 48, 2], F32)
        rt = X.sb(es0, "rt", [128, NRT], F32)
        g1 = X.sb(es0, "g1", [128, 8], F32)
        Gm2 = X.sb(es0, "Gm2", [128, 8, 2], F32)
        small = X.sb(es0, "small", [128, 64], F32)
        onesb = X.sb(es0, "onesb", [128, 128], BF16)
        S.dma("sp", cst[:], consts, writes=[cst])
        S.dma("sp", gv[:], gvec, writes=[gv])
        S.dma("sp", modT[:], modT_i.rearrange("p (a j) -> p a j", j=2), writes=[modT])
        S.dma("sp", rt[:], rt_i, writes=[rt])
        S.dma("sp", g1[:], g1T, writes=[g1])
        S.barrier()
        S.op("dve", lambda: nc.vector.tensor_copy(out=onesb[:], in_=cst[:, C_ONES:C_ONES + 128]), reads=[cst], writes=[onesb])
        S.op("pool", lambda: nc.gpsimd.memset(small[:], 0.0), writes=[small])
        for j in range(2):
            S.op("dve", lambda: nc.vector.scalar_tensor_tensor(out=Gm2[:, :, j], in0=modT[:, 32:40, j], scalar=1.0,
                                                               in1=g1[:], op0=ALU.add, op1=ALU.mult),
                 reads=[modT, g1], writes=[Gm2])
        ones = cst[:, C_ONES:C_ONES + 128]
        one_col = cst[:, C_ONES:C_ONES + 1]
        epsc = gv[:, GV_EPS:GV_EPS + 1]
        sel = rt[:, RT_SEL:RT_SEL + 64]
        with ExitStack() as es:
            dl = X.sb(es, "dl", [128, 4, 64], F32)
            rd = X.sb(es, "rd", [128, 8], F32)
            pr = X.sb(es, "prd", [128, 2, 64], F32)
            S.dma("sp", dl[:], dlam_i.rearrange("p (a e) -> p a e", e=64), writes=[dl])
            S.dma("sp", rd[:], retd_i, writes=[rd])
            S.op("dve", lambda: nc.vector.tensor_tensor(out=pr[:, 0, :], in0=dl[:, 0, :], in1=dl[:, 1, :], op=ALU.mult),
                 reads=[dl], writes=[pr])
            S.op("dve", lambda: nc.vector.tensor_tensor(out=pr[:, 1, :], in0=dl[:, 2, :], in1=dl[:, 3, :], op=ALU.mult),
                 reads=[dl], writes=[pr])
            for m in range(2):
                S.op("act", lambda: nc.scalar.activation(out=dl[:, m, :], in_=pr[:, m, :], func=AF.Identity,
                                                         accum_out=small[:, 2 + m:3 + m]), reads=[pr], writes=[small, dl])
            S.op("act", lambda: nc.scalar.activation(out=small[:, 2:4], in_=small[:, 2:4], func=AF.Exp), reads=[small], writes=[small])
            S.op("dve", lambda: nc.vector.tensor_tensor(out=small[:, 0:1], in0=small[:, 3:4], in1=small[:, 2:3], op=ALU.subtract),
                 reads=[small], writes=[small])
            S.op("dve", lambda: nc.vector.tensor_scalar(out=small[:, 0:1], in0=small[:, 0:1], scalar1=-lam_init, scalar2=None,
                                                        op0=ALU.add), reads=[small], writes=[small])
            S.op("dve", lambda: nc.vector.tensor_scalar(out=small[:, 1:2], in0=gv[:, GV_SUBLN:GV_SUBLN + 1],
                                                        scalar1=1.0 - lam_init, scalar2=None, op0=ALU.mult),
                 reads=[gv], writes=[small])
            S.op("act", lambda: nc.scalar.activation(out=small[:, 8:16], in_=rd[:], func=AF.Exp, scale=-1.0), reads=[rd], writes=[small])
            S.op("act", lambda: nc.scalar.activation(out=small[:, 8:16], in_=small[:, 8:16], func=AF.Ln, bias=one_col),
                 reads=[small], writes=[small])
            S.op("dve", lambda: nc.vector.tensor_scalar(out=small[:, 8:16], in0=small[:, 8:16], scalar1=-1.0, scalar2=None,
                                                        op0=ALU.mult), reads=[small], writes=[small])
            S.op("act", lambda: nc.scalar.activation(out=small[:, 16:24], in_=small[:, 8:16], func=AF.Exp, scale=128.0),
                 reads=[small], writes=[small])
            for dr in range(2):
                for h in range(4):
                    cc = dr * 4 + h
                    S.op("act", lambda: nc.scalar.activation(out=small[:, 24 + cc:25 + cc], in_=rt[:, RT_KDF + dr:RT_KDF + dr + 1],
                                                             func=AF.Exp, scale=small[:, 8 + cc:9 + cc]),
                         reads=[small, rt], writes=[small])
            S.barrier()
        neg_lam = small[:, 0:1]
        gsub = small[:, 1:2]

        with ExitStack() as es:
            pst = X.ring(es, "pst", [128, 2, 512], F32, 2, psum=True)
            pacc = X.ring(es, "pacc", [128, 512], F32, 4, psum=True)
            kbuf = X.ring(es, "kTu", [128, NKEY], BF16, 2)
            vbuf = X.ring(es, "vau", [128, NKT, 128], BF16, 2)
            qr = X.ring(es, "qh", [128, NT], BF16, 3)
            ptr_ = X.ring(es, "pt", [128, 2, 512], BF16, 3)
            fin = X.ring(es, "fin", [128, 512], F32, 6)
            obr = X.ring(es, "obf", [128, 512], BF16, 3)
            for vb_ in vbuf.bufs:
                S.op("pool", lambda: nc.gpsimd.memset(vb_[:, :, 64:128], 1.0), writes=[vb_])

            def pair_run(steps, tn, scale, rd_qk):
                n = len(steps)
                LA = 2
                pts = [None] * n
                for i in range(n + LA):
                    if i < n:
                        qk = steps[i][0]
                        ps_ = pst.next()
                        for slot, (lh, rh) in enumerate(qk):
                            S.op("pe", lambda: nc.tensor.matmul(ps_[:, slot, :tn], lhsT=lh, rhs=rh, start=True, stop=True),
                                 reads=rd_qk, writes=[ps_])
                        pt = ptr_.next()
                        ns = len(qk)
                        S.op("act", lambda: nc.scalar.activation(out=pt[:, 0:ns, :tn], in_=ps_[:, 0:ns, :tn], func=AF.Exp, scale=scale),
                             reads=[ps_], writes=[pt])
                        pts[i] = pt
                    if i >= LA:
                        j = i - LA
                        for (acc, lf, slot, st_, sp_, rdl) in steps[j][1]:
                            S.op("pe", lambda: nc.tensor.matmul(acc[:, :tn], lhsT=lf, rhs=pts[j][:, slot, :tn], start=st_, stop=sp_),
                                 reads=[pts[j]] + rdl, writes=[acc])

            def finalize_AC(acc, tn, dst_ap):
                a = fin.next()
                S.op("dve", lambda: nc.vector.tensor_copy(out=a[:, :tn], in_=acc[:, :tn]), reads=[acc], writes=[a])
                pn = pst.next()
                S.op("pe", lambda: nc.tensor.matmul(pn[0:64, 0, :tn], lhsT=sel, rhs=a[:, :tn], start=True, stop=True),
                     reads=[a, rt], writes=[pn])
                rc = fin.next()
                S.op("dve", lambda: nc.vector.reciprocal(out=rc[0:64, :tn], in_=pn[0:64, 0, :tn]), reads=[pn], writes=[rc])
                ob = obr.next()
                S.op("dve", lambda: nc.vector.tensor_tensor(out=ob[0:64, :tn], in0=a[0:64, :tn], in1=rc[0:64, :tn], op=ALU.mult),
                     reads=[a, rc], writes=[ob])
                S.dma("sp", dst_ap, ob[0:64, :tn], reads=[ob], writes=[brT_b], par=True)

            units = [("A", g) for g in range(2)] + [("C", h) for h in range(8)] + [("D", h) for h in range(4)]
            all_kt = list(range(NKT))
            ctx_kt = [128, 129]
            for (kind, u) in units:
                kb_ = kbuf.next()
                vb_ = vbuf.next()
                if kind == "A":
                    S.dma("sp", kb_[0:64, :], kTg[KA + u * 64:KA + (u + 1) * 64, :], writes=[kb_], par=True)
                    S.dma("sp", kb_[64:128, :], kTg[KA + u * 64:KA + (u + 1) * 64, :], writes=[kb_], par=True)
                    vsrc = vg[:, VA + u * 64:VA + (u + 1) * 64]
                    qtiles = [(QA + (4 * u + 2 * p) * 64, 128) for p in range(2)]
                    vw, scale = 64, 0.125
                elif kind == "C":
                    S.dma("sp", kb_[0:64, :], kTg[KCN + u * 64:KCN + (u + 1) * 64, :], writes=[kb_], par=True)
                    S.dma("sp", kb_[64:96, :], kTg[KCR:KCR + 32, :], writes=[kb_], par=True)
                    vsrc = vg[:, VC + u * 64:VC + (u + 1) * 64]
                    qtiles = [(QC + u * 96, 96)]
                    vw, scale = 64, 96.0 ** -0.5
                else:
                    S.dma("sp", kb_[:, :], kTg[KD + u * 128:KD + (u + 1) * 128, :], writes=[kb_])
                    vsrc = vg[:, VD + u * 128:VD + (u + 1) * 128]
                    qtiles = [(QD + u * 128, 128)]
                    vw, scale = 128, 0.125
                for pc in range(5):
                    S.dma("sp", vb_[:, pc * 26:(pc + 1) * 26, 0:vw],
                          vsrc[pc * 26 * 128:(pc + 1) * 26 * 128, :].rearrange("(t p) e -> p t e", p=128), writes=[vb_], par=True)
                for qi, (qrow, qrows) in enumerate(qtiles):
                    qh = qr.next()
                    S.dma("sp", qh[0:qrows, :], qT[qrow:qrow + qrows, :], writes=[qh])
                    rd_qk = [kb_, qh]
                    for gi, (t0, tn) in enumerate(groups):
                        kts = ctx_kt if gi == 4 else all_kt
                        nk = len(kts)
                        qs = slice(t0, t0 + tn)
                        if kind == "A":
                            a0, a1 = pacc.next(), pacc.next()
                            steps = []
                            for ii, kt in enumerate(kts):
                                ks = slice(kt * 128, (kt + 1) * 128)
                                st_, sp_ = (ii == 0), (ii == nk - 1)
                                steps.append(([(kb_[0:64, ks], qh[0:64, qs]), (kb_[64:128, ks], qh[64:128, qs])],
                                              [(a0, vb_[:, kt, :], 0, st_, sp_, [vb_]), (a1, vb_[:, kt, :], 1, st_, sp_, [vb_])]))
                            pair_run(steps, tn, scale, rd_qk)
                            hd = 4 * u + 2 * qi
                            finalize_AC(a0, tn, brT_d[BR_A + hd * 64:BR_A + (hd + 1) * 64, qs])
                            finalize_AC(a1, tn, brT_d[BR_A + (hd + 1) * 64:BR_A + (hd + 2) * 64, qs])
                        elif kind == "C":
                            acc = pacc.next()
                            steps = []
                            for ii in range(0, nk, 2):
                                k0, k1 = kts[ii], kts[ii + 1]
                                steps.append(([(kb_[0:96, k0 * 128:(k0 + 1) * 128], qh[0:96, qs]), (kb_[0:96, k1 * 128:(k1 + 1) * 128], qh[0:96, qs])],
                                              [(acc, vb_[:, k0, :], 0, ii == 0, False, [vb_]), (acc, vb_[:, k1, :], 1, False, ii + 2 >= nk, [vb_])]))
                            pair_run(steps, tn, scale, rd_qk)
                            finalize_AC(acc, tn, brT_d[BR_C + u * 64:BR_C + (u + 1) * 64, qs])
                        else:
                            ao0, as0, ao1, as1 = pacc.next(), pacc.next(), pacc.next(), pacc.next()
                            steps = []
                            for ii, kt in enumerate(kts):
                                ks = slice(kt * 128, (kt + 1) * 128)
                                st_, sp_ = (ii == 0), (ii == nk - 1)
                                steps.append(([(kb_[0:64, ks], qh[0:64, qs]), (kb_[64:128, ks], qh[64:128, qs])],
                                              [(ao0, vb_[:, kt, :], 0, st_, sp_, [vb_]), (as0, onesb[:], 0, st_, sp_, [onesb]),
                                               (ao1, vb_[:, kt, :], 1, st_, sp_, [vb_]), (as1, onesb[:], 1, st_, sp_, [onesb])]))
                            pair_run(steps, tn, scale, rd_qk)
                            accs = [(ao0, as0), (ao1, as1)]
                            brow = BR_D + u * 128
                            r0, r1, t0_, t1_ = fin.next(), fin.next(), fin.next(), fin.next()
                            S.op("dve", lambda: nc.vector.reciprocal(out=r0[:, :tn], in_=accs[0][1][:, :tn]), reads=[accs[0][1]], writes=[r0])
                            S.op("dve", lambda: nc.vector.reciprocal(out=r1[:, :tn], in_=accs[1][1][:, :tn]), reads=[accs[1][1]], writes=[r1])
                            S.op("dve", lambda: nc.vector.tensor_tensor(out=t0_[:, :tn], in0=accs[0][0][:, :tn], in1=r0[:, :tn], op=ALU.mult),
                                 reads=[accs[0][0], r0], writes=[t0_])
                            S.op("dve", lambda: nc.vector.scalar_tensor_tensor(out=t1_[:, :tn], in0=accs[1][0][:, :tn], scalar=neg_lam,
                                                                               in1=r1[:, :tn], op0=ALU.mult, op1=ALU.mult),
                                 reads=[accs[1][0], r1, small], writes=[t1_])
                            S.op("dve", lambda: nc.vector.tensor_tensor(out=t0_[:, :tn], in0=t0_[:, :tn], in1=t1_[:, :tn], op=ALU.add),
                                 reads=[t0_, t1_], writes=[t0_])
                            S.op("act", lambda: nc.scalar.activation(out=r0[:, :tn], in_=t0_[:, :tn], func=AF.Square), reads=[t0_], writes=[r0])
                            pn = pst.next()
                            S.op("pe", lambda: nc.tensor.matmul(pn[:, 0, :tn], lhsT=ones, rhs=r0[:, :tn], start=True, stop=True),
                                 reads=[r0], writes=[pn])
                            S.op("act", lambda: nc.scalar.activation(out=r1[:, :tn], in_=pn[:, 0, :tn], func=AF.Ln, scale=1.0 / 128, bias=epsc),
                                 reads=[pn], writes=[r1])
                            S.op("act", lambda: nc.scalar.activation(out=r1[:, :tn], in_=r1[:, :tn], func=AF.Exp, scale=-0.5),
                                 reads=[r1], writes=[r1])
                            ob = obr.next()
                            S.op("dve", lambda: nc.vector.scalar_tensor_tensor(out=ob[:, :tn], in0=t0_[:, :tn], scalar=gsub, in1=r1[:, :tn],
                                                                               op0=ALU.mult, op1=ALU.mult),
                                 reads=[t0_, r1, small], writes=[ob])
                            S.dma("sp", brT_d[brow:brow + 128, qs], ob[:, :tn], reads=[ob], writes=[brT_b], par=True)
            S.barrier()

        ntl = 18 if need_ctx else 16
        with ExitStack() as es:
            mask = X.sb(es, "rmask", [128, 4, 128], F32)
            qdec = X.sb(es, "qdec", [64, 8, 128], F32)
            coef = X.sb(es, "coef", [128, 2, NKT], F32)
            Eb = X.sb(es, "retEs", [128, 4, NKT], F32)
            kbg = X.sb(es, "kbg", [128, NKT, 64], BF16)
            vbg = X.sb(es, "vbg", [128, NKT, 128], BF16)
            kw = X.ring(es, "kw", [128, NKT, 64], BF16, 2)
            kbl = X.sb(es, "kbl", [128, 18, 64], BF16)
            vbl = X.sb(es, "vbl", [128, 18, 128], BF16)
            kdl = X.ring(es, "kdl", [128, 18, 64], BF16, 2)
            qh = X.sb(es, "rqh", [64, NT], BF16)
            kh = X.sb(es, "rkh", [64, NT], BF16)
            gh = X.sb(es, "rgh", [128, NT], BF16)
            qd = X.ring(es, "rqd", [64, 2, 128], BF16, 3)
            st = X.ring(es, "rst", [64, 128], F32, 2)
            snap = X.sb(es, "snap", [64, 2, 18, 128], BF16)
            sm = X.ring(es, "rsm", [128, 128], BF16, 3)
            tmpf = X.ring(es, "rtmp", [128, 512], F32, 4)
            obr = X.ring(es, "rob", [128, 512], BF16, 2)
            pU = X.ring(es, "pU", [64, 128], F32, 2, psum=True)
            pS = X.ring(es, "pS", [128, 128], F32, 2, psum=True)
            pO = X.ring(es, "pO", [128, 512], F32, 2, psum=True)
            pN = X.ring(es, "pN2", [128, 512], F32, 1, psum=True)
            S.dma("sp", Eb[:], retE, writes=[Eb])
            for h in range(4):
                for dr in range(2):
                    cc = dr * 4 + h
                    t = tmpf.next()
                    rel = rt[:, RT_RELF + 256 * dr:RT_RELF + 256 * dr + 128]
                    msk = rt[:, RT_MSKF + 256 * dr:RT_MSKF + 256 * dr + 128]
                    S.op("act", lambda: nc.scalar.activation(out=t[:, 0:128], in_=rel, func=AF.Exp, scale=small[:, 8 + cc:9 + cc]),
                         reads=[small], writes=[t])
                    if dr == 0:
                        S.op("dve", lambda: nc.vector.tensor_tensor(out=mask[:, h, :], in0=t[:, 0:128], in1=msk, op=ALU.mult),
                             reads=[t], writes=[mask])
                    else:
                        S.op("dve", lambda: nc.vector.tensor_tensor(out=t[:, 0:128], in0=t[:, 0:128], in1=msk, op=ALU.mult),
                             reads=[t], writes=[t])
                        S.op("dve", lambda: nc.vector.tensor_tensor(out=mask[:, h, :], in0=mask[:, h, :], in1=t[:, 0:128], op=ALU.add),
                             reads=[t, mask], writes=[mask])
                    S.op("act", lambda: nc.scalar.activation(out=qdec[:, cc, :], in_=rt[0:64, RT_QDF + 128 * dr:RT_QDF + 128 * dr + 128],
                                                             func=AF.Exp, scale=small[0:64, 8 + cc:9 + cc]),
                         reads=[small], writes=[qdec])
            for h in range(4):
                for pc in range(5):
                    S.dma("sp", kbg[:, pc * 26:(pc + 1) * 26, :],
                          vg[pc * 3328:(pc + 1) * 3328, VBK + h * 64:VBK + (h + 1) * 64].rearrange("(t p) e -> p t e", p=128),
                          writes=[kbg], par=True)
                    S.dma("sp", vbg[:, pc * 26:(pc + 1) * 26, :],
                          vg[pc * 3328:(pc + 1) * 3328, VB + h * 128:VB + (h + 1) * 128].rearrange("(t p) e -> p t e", p=128),
                          writes=[vbg], par=True)
                S.dma("sp", kbl[:], vl[:, VBK + h * 64:VBK + (h + 1) * 64].rearrange("(t p) e -> p t e", p=128), writes=[kbl])
                S.dma("sp", vbl[:], vl[:, VB + h * 128:VB + (h + 1) * 128].rearrange("(t p) e -> p t e", p=128), writes=[vbl])
                S.dma("sp", qh[:], qT[QB + h * 64:QB + (h + 1) * 64, :], writes=[qh])
                S.dma("sp", kh[:], kTl[KB + h * 64:KB + (h + 1) * 64, :], writes=[kh])
                S.dma("sp", gh[:], gT[h * 128:(h + 1) * 128, :], writes=[gh])
                for dr in range(2):
                    cc = dr * 4 + h
                    S.op("act", lambda: nc.scalar.activation(out=coef[:, dr, :], in_=Eb[:, 2 * dr, :], func=AF.Exp,
                                                             scale=small[:, 8 + cc:9 + cc]), reads=[Eb, small], writes=[coef])
                    S.op("dve", lambda: nc.vector.tensor_tensor(out=coef[:, dr, :], in0=coef[:, dr, :], in1=Eb[:, 2 * dr + 1, :], op=ALU.mult),
                         reads=[Eb, coef], writes=[coef])
                    kw_ = kw.next()
                    S.op("dve", lambda: nc.vector.tensor_tensor(out=kw_[:], in0=kbg[:], in1=coef[:, dr, :].unsqueeze(2).to_broadcast([128, NKT, 64]),
                                                                op=ALU.mult), reads=[kbg, coef], writes=[kw_])
                    pu = pU.next()
                    for t in range(NKT):
                        S.op("pe", lambda: nc.tensor.matmul(pu[:], lhsT=kw_[:, t, :], rhs=vbg[:, t, :], start=(t == 0), stop=(t == NKT - 1)),
                             reads=[kw_, vbg], writes=[pu])
                    s_ = st.next()
                    S.op("dve", lambda: nc.vector.tensor_copy(out=s_[:], in_=pu[:]), reads=[pu], writes=[s_])
                    kd_ = kdl.next()
                    S.op("dve", lambda: nc.vector.tensor_scalar(out=kd_[:], in0=kbl[:], scalar1=small[:, 24 + cc:25 + cc], scalar2=None,
                                                                op0=ALU.mult), reads=[kbl, small], writes=[kd_])
                    order = list(range(16)) if dr == 0 else list(range(15, -1, -1))
                    cdc = small[0:64, 16 + cc:17 + cc]
                    for i in order:
                        S.op("act", lambda: nc.scalar.copy(out=snap[:, dr, i, :], in_=s_[:]), reads=[s_], writes=[snap])
                        pu = pU.next()
                        S.op("pe", lambda: nc.tensor.matmul(pu[:], lhsT=kd_[:, i, :], rhs=vbl[:, i, :], start=True, stop=True),
                             reads=[kd_, vbl], writes=[pu])
                        S.op("dve", lambda: nc.vector.scalar_tensor_tensor(out=s_[:], in0=s_[:], scalar=cdc, in1=pu[:],
                                                                           op0=ALU.mult, op1=ALU.add), reads=[s_, pu, small], writes=[s_])
                    if need_ctx:
                        first, second = (16, 17) if dr == 0 else (17, 16)
                        S.op("pool", lambda: nc.gpsimd.memset(snap[:, dr, first, :], 0.0), writes=[snap])
                        pu = pU.next()
                        S.op("pe", lambda: nc.tensor.matmul(pu[:], lhsT=kd_[:, first, :], rhs=vbl[:, first, :], start=True, stop=True),
                             reads=[kd_, vbl], writes=[pu])
                        S.op("act", lambda: nc.scalar.copy(out=snap[:, dr, second, :], in_=pu[:]), reads=[pu], writes=[snap])
                for gi, (t0, tn) in enumerate(groups):
                    po = pO.next()
                    for s in range(tn // 128):
                        i = t0 // 128 + s
                        ts_ = slice(i * 128, (i + 1) * 128)
                        psc = pS.next()
                        S.op("pe", lambda: nc.tensor.matmul(psc[:], lhsT=kh[:, ts_], rhs=qh[:, ts_], start=True, stop=True),
                             reads=[kh, qh], writes=[psc])
                        sm_ = sm.next()
                        S.op("dve", lambda: nc.vector.tensor_tensor(out=sm_[:], in0=psc[:], in1=mask[:, h, :], op=ALU.mult),
                             reads=[psc, mask], writes=[sm_])
                        qd_ = qd.next()
                        for dr in range(2):
                            S.op("pool", lambda: nc.gpsimd.tensor_tensor(out=qd_[:, dr, :], in0=qh[:, ts_], in1=qdec[:, dr * 4 + h, :], op=ALU.mult),
                                 reads=[qh, qdec], writes=[qd_])
                        S.op("pe", lambda: nc.tensor.matmul(po[:, s * 128:(s + 1) * 128], lhsT=vbl[:, i, :], rhs=sm_[:], start=True, stop=False),
                             reads=[vbl, sm_], writes=[po])
                        for dr in range(2):
                            S.op("pe", lambda: nc.tensor.matmul(po[:, s * 128:(s + 1) * 128], lhsT=snap[:, dr, i, :], rhs=qd_[:, dr, :],
                                                                start=False, stop=(dr == 1)), reads=[snap, qd_], writes=[po])
                    o_ = tmpf.next()
                    S.op("act", lambda: nc.scalar.copy(out=o_[:, :tn], in_=po[:, :tn]), reads=[po], writes=[o_])
                    q_ = tmpf.next()
                    S.op("act", lambda: nc.scalar.activation(out=q_[:, :tn], in_=po[:, :tn], func=AF.Square), reads=[po], writes=[q_])
                    pn = pN.next()
                    S.op("pe", lambda: nc.tensor.matmul(pn[:, :tn], lhsT=ones, rhs=q_[:, :tn], start=True, stop=True), reads=[q_], writes=[pn])
                    S.op("act", lambda: nc.scalar.activation(out=q_[:, :tn], in_=pn[:, :tn], func=AF.Ln, scale=1.0 / 128, bias=epsc),
                         reads=[pn], writes=[q_])
                    S.op("act", lambda: nc.scalar.activation(out=q_[:, :tn], in_=q_[:, :tn], func=AF.Exp, scale=-0.5), reads=[q_], writes=[q_])
                    S.op("dve", lambda: nc.vector.tensor_tensor(out=o_[:, :tn], in0=o_[:, :tn], in1=q_[:, :tn], op=ALU.mult),
                         reads=[o_, q_], writes=[o_])
                    ob = obr.next()
                    S.op("dve", lambda: nc.vector.tensor_tensor(out=ob[:, :tn], in0=o_[:, :tn], in1=gh[:, t0:t0 + tn], op=ALU.mult),
                         reads=[o_, gh], writes=[ob])
                    S.dma("sp", brT_d[BR_B + h * 128:BR_B + (h + 1) * 128, t0:t0 + tn], ob[:, :tn], reads=[ob], writes=[brT_b], par=True)
            S.barrier()

        with ExitStack() as es:
            wb = X.sb(es, "wbr", [128, 16, D], BF16)
            wo = X.sb(es, "wo", [128, 8, D], BF16)
            S.dma("pool", wb[:], w_branch.rearrange("(c p) n -> p c n", p=128), writes=[wb])
            S.dma("pool", wo[:], w_out.rearrange("(c p) n -> p c n", p=128), writes=[wo])
            brg = X.sb(es, "brg", [128, 16, 512], BF16)
            gtg = X.sb(es, "gtg", [128, 32, 512], BF16)
            xg = X.sb(es, "xg3", [128, 8, 512], F32)
            x1g = X.sb(es, "x1g", [128, 8, 512], F32)
            mg = X.sb(es, "mg", [128, 8, 512], BF16)
            h2g = X.sb(es, "h2g", [128, 8, 512], BF16)
            macc = X.ring(es, "macc", [128, 512], F32, 2)
            tmp = X.ring(es, "tmp3", [128, 512], F32, 4)
            tmp.rstd = X.sb(es, "rstd3", [128, 512], F32)
            sqr = X.ring(es, "sq3", [128, 512], F32, 2)
            pp = X.ring(es, "pp3", [128, 512], F32, 4, psum=True)
            pms_r = X.ring(es, "pms3", [128, 512], F32, 2, psum=True)
            for gi, (t0, tn) in enumerate(groups):
                jcol = 1 if gi == 4 else 0
                S.dma("sp", brg[:, :, :tn], brT_d[:, t0:t0 + tn].rearrange("(c p) t -> p c t", p=128), reads=[brT_b], writes=[brg])
                S.dma("sp", gtg[:, :, :tn], gT[512:, t0:t0 + tn].rearrange("(c p) t -> p c t", p=128), writes=[gtg])
                S.dma("sp", xg[:, :, :tn], xci(t0, tn).rearrange("(c p) t -> p c t", p=128), writes=[xg])
                for ob in range(8):
                    m_ = macc.next()
                    for n in range(4):
                        p_ = pp.next()
                        for kc in range(4):
                            S.op("pe", lambda: nc.tensor.matmul(p_[:, :tn], lhsT=wb[:, n * 4 + kc, ob * 128:(ob + 1) * 128],
                                                                rhs=brg[:, n * 4 + kc, :tn], start=(kc == 0), stop=(kc == 3)),
                                 reads=[wb, brg], writes=[p_])
                        if n == 0:
                            S.op("dve", lambda: nc.vector.tensor_tensor(out=m_[:, :tn], in0=p_[:, :tn], in1=gtg[:, n * 8 + ob, :tn], op=ALU.mult),
                                 reads=[p_, gtg], writes=[m_])
                        else:
                            t_ = tmp.next()
                            S.op("dve", lambda: nc.vector.tensor_tensor(out=t_[:, :tn], in0=p_[:, :tn], in1=gtg[:, n * 8 + ob, :tn], op=ALU.mult),
                                 reads=[p_, gtg], writes=[t_])
                            if n < 3:
                                S.op("pool", lambda: nc.gpsimd.tensor_tensor(out=m_[:, :tn], in0=m_[:, :tn], in1=t_[:, :tn], op=ALU.add),
                                     reads=[m_, t_], writes=[m_])
                            else:
                                S.op("pool", lambda: nc.gpsimd.tensor_tensor(out=mg[:, ob, :tn], in0=m_[:, :tn], in1=t_[:, :tn], op=ALU.add),
                                     reads=[m_, t_], writes=[mg])
                for ob in range(8):
                    p_ = pp.next()
                    for kc in range(8):
                        S.op("pe", lambda: nc.tensor.matmul(p_[:, :tn], lhsT=wo[:, kc, ob * 128:(ob + 1) * 128], rhs=mg[:, kc, :tn],
                                                            start=(kc == 0), stop=(kc == 7)), reads=[wo, mg], writes=[p_])
                    S.op("dve", lambda: nc.vector.scalar_tensor_tensor(out=x1g[:, ob, :tn], in0=p_[:, :tn], scalar=modT[:, 16 + ob, jcol:jcol + 1],
                                                                       in1=xg[:, ob, :tn], op0=ALU.mult, op1=ALU.add),
                         reads=[p_, modT, xg], writes=[x1g])
                S.dma("sp", x1_d[:, t0:t0 + tn].rearrange("(c p) t -> p c t", p=128), x1g[:, :, :tn], reads=[x1g], writes=[x1_b], par=True)
                emit_norm_mod(X, nc, S, x1g, tn, jcol, Gm2, _Shift(modT, 24), ones, epsc,
                              sqr, pms_r, tmp, h2g)
                S.dma("sp", h2_d[:, t0:t0 + tn].rearrange("(c p) t -> p c t", p=128), h2g[:, :, :tn], reads=[h2g], writes=[h2_b], par=True)
            S.barrier()
        with ExitStack() as es:
            wfi = X.sb(es, "wfi", [128, 8, 2 * FFN], BF16)
            wfo = X.sb(es, "wfo", [128, 22, D], BF16)
            for c in range(8):
                S.dma("pool", wfi[:, c, :], w_fi[c * 128:(c + 1) * 128, :], writes=[wfi], par=True)
            S.dma("pool", wfo[:], w_fo.rearrange("(c p) n -> p c n", p=128), writes=[wfo])
            TG = 256
            h2r = X.ring(es, "h2r", [128, 8, TG], BF16, 2)
            x1r = X.ring(es, "x1r", [128, 8, TG], F32, 2)
            x2r = X.ring(es, "x2r", [128, 8, TG], F32, 2)
            act = X.sb(es, "actT", [128, 22, TG], BF16)
            sgr = X.ring(es, "sgr", [128, TG], F32, 3)
            pg_r = X.ring(es, "pg", [128, 512], F32, 3, psum=True)
            pu_r = X.ring(es, "pu", [128, 512], F32, 3, psum=True)
            po_r = X.ring(es, "po4", [128, 512], F32, 2, psum=True)
            ntok = NT if need_ctx else NLAT
            for t0 in range(0, ntok, TG):
                jcol = 1 if t0 >= NLAT else 0
                h2 = h2r.next()
                x1 = x1r.next()
                x2 = x2r.next()
                S.dma("sp", h2[:], h2_d[:, t0:t0 + TG].rearrange("(c p) t -> p c t", p=128), reads=[h2_b], writes=[h2])
                S.dma("sp", x1[:], x1_d[:, t0:t0 + TG].rearrange("(c p) t -> p c t", p=128), reads=[x1_b], writes=[x1])
                for fb in range(22):
                    pg, pu = pg_r.next(), pu_r.next()
                    for kc in range(8):
                        S.op("pe", lambda: nc.tensor.matmul(pg[:, :TG], lhsT=wfi[:, kc, fb * 128:(fb + 1) * 128], rhs=h2[:, kc, :],
                                                            start=(kc == 0), stop=(kc == 7)), reads=[wfi, h2], writes=[pg])
                    for kc in range(8):
                        S.op("pe", lambda: nc.tensor.matmul(pu[:, :TG], lhsT=wfi[:, kc, FFN + fb * 128:FFN + (fb + 1) * 128], rhs=h2[:, kc, :],
                                                            start=(kc == 0), stop=(kc == 7)), reads=[wfi, h2], writes=[pu])
                    sg = sgr.next()
                    S.op("act", lambda: nc.scalar.activation(out=sg[:], in_=pg[:, :TG], func=AF.Silu), reads=[pg], writes=[sg])
                    S.op("dve", lambda: nc.vector.tensor_tensor(out=act[:, fb, :], in0=pu[:, :TG], in1=sg[:], op=ALU.mult),
                         reads=[pu, sg], writes=[act])
                for ob in range(8):
                    po = po_r.next()
                    for fb in range(22):
                        S.op("pe", lambda: nc.tensor.matmul(po[:, :TG], lhsT=wfo[:, fb, ob * 128:(ob + 1) * 128], rhs=act[:, fb, :],
                                                            start=(fb == 0), stop=(fb == 21)), reads=[wfo, act], writes=[po])
                    S.op("dve", lambda: nc.vector.scalar_tensor_tensor(out=x2[:, ob, :], in0=po[:, :TG], scalar=modT[:, 40 + ob, jcol:jcol + 1],
                                                                       in1=x1[:, ob, :], op0=ALU.mult, op1=ALU.add),
                         reads=[po, modT, x1], writes=[x2])
                S.dma("sp", xco(t0, TG).rearrange("(c p) t -> p c t", p=128), x2[:], reads=[x2])
        S.barrier()


def make_consts():
    c = np.zeros((128, NCONST), np.float32)
    c[:, C_ONES:C_ONES + 128] = 1.0
    for b in range(2):
        c[b * 64:(b + 1) * 64, C_BD64 + b * 64:C_BD64 + (b + 1) * 64] = 1.0
    c[0:64, C_BD96:C_BD96 + 64] = 1.0
    c[64:96, C_BD96 + 64:C_BD96 + 96] = 1.0
    for p in range(128):
        c[p, C_PSW + (p ^ 1)] = 1.0
        c[p, C_ID + p] = 1.0
    return c


def make_rope_tables(core):
    s = core * NLAT + np.arange(NLAT)
    row = (s // 64).astype(np.float64)
    col = (s % 64).astype(np.float64)

    def ang(dim):
        q = dim // 4
        f = 10000.0 ** (-(np.arange(q, dtype=np.float32) / np.float32(q))).astype(np.float32)
        f = f.astype(np.float32)
        a = np.concatenate([row[:, None].astype(np.float32) * f[None, :], col[:, None].astype(np.float32) * f[None, :]], -1)
        return a.astype(np.float32)
    out = np.zeros((6, 128, NT), np.float32)
    out[0::2, :, :] = 1.0
    a64 = ang(64)
    a32 = ang(32)
    sign = np.where(np.arange(128) % 2 == 0, -1.0, 1.0).astype(np.float32)
    p = np.arange(128)
    idx64 = (p % 64) // 2
    out[0, :, :NLAT] = np.cos(a64)[:, idx64].T
    out[1, :, :NLAT] = (np.sin(a64)[:, idx64] * sign[None, :]).T
    p32 = np.arange(32)
    idx32 = p32 // 2
    out[2, 64:96, :NLAT] = np.cos(a32)[:, idx32].T
    out[3, 64:96, :NLAT] = (np.sin(a32)[:, idx32] * sign[None, :32]).T
    out[4, 0:32, :NLAT] = np.cos(a32)[:, idx32].T
    out[5, 0:32, :NLAT] = (np.sin(a32)[:, idx32] * sign[None, :32]).T
    return out


def tile_col(v, n=128):
    v = np.asarray(v, np.float32).reshape(-1)
    reps = -(-n // v.size)
    return np.tile(v, reps)[:n] if v.size <= n and n % v.size == 0 else np.pad(v, (0, n - v.size))


def make_gvec(inp, l):
    g = np.zeros((128, NGV), np.float32)
    g[:, GV_AQ] = tile_col(inp["gqa_qk_gain"][l, 0])
    g[:, GV_AK] = tile_col(inp["gqa_qk_gain"][l, 1])
    g[:, GV_DQ] = tile_col(inp["diff_qk_gain"][l, 0])
    g[:, GV_DK] = tile_col(inp["diff_qk_gain"][l, 1])
    g[:, GV_CQ:GV_CQ + 3] = inp["mla_cq_gain"][l].reshape(3, 128).T
    g[:, GV_CKV:GV_CKV + 2] = inp["mla_ckv_gain"][l].reshape(2, 128).T
    g[:96, GV_MQ] = inp["mla_qk_gain"][l, 0]
    g[:, GV_MKN] = tile_col(inp["mla_qk_gain"][l, 1, :64])
    g[:32, GV_MKR] = inp["mla_qk_gain"][l, 1, 64:]
    g[:64, GV_SC96] = 1.0 / 64
    g[64:96, GV_SC96] = 1.0 / 32
    g[:, GV_EPS] = EPS
    g[:, GV_SUBLN] = inp["diff_subln_gain"][l]
    return g


class _Shift:
    def __init__(self, buf, base):
        self.buf = buf
        self.base = base
        self.lw = buf.lw
        self.rd = buf.rd

    def __getitem__(self, idx):
        p, c, j = idx
        return self.buf.t[p, self.base + c, j]


def make_rt():
    r = np.zeros((128, NRT), np.float32)
    j = np.arange(128)[:, None].astype(np.float32)
    i = np.arange(128)[None, :].astype(np.float32)
    r[:, RT_RELF:RT_RELF + 128] = np.maximum(i - j, 0)
    r[:, RT_MSKF:RT_MSKF + 128] = (i >= j)
    r[:, RT_RELB:RT_RELB + 128] = np.maximum(j - i, 0)
    r[:, RT_MSKB:RT_MSKB + 128] = (j >= i)
    r[:, RT_QDF:RT_QDF + 128] = i + 1.0
    r[:, RT_QDB:RT_QDB + 128] = 128.0 - i
    r[:, RT_KDF] = 127.0 - np.arange(128)
    r[:, RT_KDB] = np.arange(128)
    for k in range(64):
        r[64 + k, RT_SEL + k] = 1.0
    return r


def make_retE(core, rot=0):
    Eg = _make_retE_global(core)
    perm = [((rot + u // 16) % NCORES) * 16 + u % 16 for u in range(128)] + [128, 129]
    return np.ascontiguousarray(Eg[:, :, perm])


def _make_retE_global(core):
    E = np.zeros((128, 4, NKT), np.float32)
    j = np.arange(128).astype(np.float32)
    T0 = core * 16
    pos = np.zeros(NKT)
    pos[128], pos[129] = 0, 1
    pos[:128] = 2 + np.arange(128)
    P0 = 2 + T0
    for u in range(NKT):
        if pos[u] < P0:
            E[:, 0, u] = 127.0 - j + 128.0 * (P0 - 1 - pos[u])
            E[:, 1, u] = 1.0
    pos[129], pos[128] = 0, 1
    pos[:128] = 2 + 127 - np.arange(128)
    P0 = 2 + 127 - (T0 + 15)
    for u in range(NKT):
        if pos[u] < P0:
            E[:, 2, u] = j + 128.0 * (P0 - 1 - pos[u])
            E[:, 3, u] = 1.0
    return E


class _NS:
    pass


def build_fused():
    nc = bass.Bass("TRN2", target_bir_lowering=False)
    di = lambda name, shape, dt=F32: nc.dram_tensor(name, shape, dt, kind="ExternalInput").ap()
    scr = lambda name, shape, dt=BF16: nc.dram_tensor(name, shape, dt, kind="Internal").ap()
    T = _NS()
    xT_all = di("xT_all", [D, NKEY])
    cT = di("cT", [D, 2])
    w_mod = di("w_mod", [DEPTH, D, 6 * D])
    bmodT = di("bmodT", [DEPTH, 128, 48])
    T.g0T = di("g0T", [DEPTH, 128, 8])
    T.g1T = di("g1T", [DEPTH, 128, 8])
    T.w_in = di("w_in", [DEPTH, D, IN_W])
    T.w_uq = di("w_uq", [DEPTH, 384, 768])
    T.w_ukv = di("w_ukv", [DEPTH, 256, 1024])
    T.gvec = di("gvec", [DEPTH, 128, NGV])
    T.consts = di("consts", [128, NCONST])
    T.rope = di("rope", [NCORES, 6, 128, NT])
    T.w_branch = di("w_branch", [DEPTH, 2048, D])
    T.w_out = di("w_out", [DEPTH, D, D])
    T.w_fi = di("w_fi", [DEPTH, D, 2 * FFN])
    T.w_fo = di("w_fo", [DEPTH, FFN, D])
    T.rt = di("rt", [128, NRT])
    T.retE = di("retE", [NCORES, 128, 4, NKT])
    T.retd = di("retd", [DEPTH, 128, 8])
    T.dlam = di("dlam", [DEPTH, 128, 256])
    out = nc.dram_tensor("xTo", [D, NLAT], F32, kind="ExternalOutput").ap()
    x1_all = scr("x1_all", [D, NKEY], F32)
    T.xin = [xT_all, x1_all]
    T.xout = [x1_all, out]
    T.modT_s = [scr("modT_s%d" % l, [128, 96], F32) for l in range(DEPTH)]
    T.qT_s = [scr("qT_s%d" % v, [NQ, NT]) for v in range(NCORES)]
    T.kT_s = [scr("kT_s%d" % v, [NK, NT]) for v in range(NCORES)]
    T.v_s = [scr("v_s%d" % v, [NT, NV]) for v in range(NCORES)]
    T.gT_s = [scr("gT_s%d" % v, [NG, NT]) for v in range(NCORES)]
    T.hT_s = scr("hT_s", [D, NT])
    T.kTg = scr("kTg", [NK, NKEY])
    T.vg = scr("vg", [NKEY, NV])
    T.brT_s = scr("brT_s", [2048, NT])
    T.x1T_s = scr("x1T_s", [D, NT], F32)
    T.h2T_s = scr("h2T_s", [D, NT])
    with ExitStack() as esr:
        X = Ctx(nc, esr)
        S = X.S
        for l in range(DEPTH):
            with ExitStack() as esm:
                dummy = X.sb(esm, "modkeep", [128, 1], F32)
                with ExitStack() as est:
                    emit_modvec(X, esm, est, cT, w_mod[l], bmodT[l], T.modT_s[l])
                    S.barrier()
            for v in range(NCORES):
                emit_A(X, T, l, v, with_ctx=(v == 0), kv_only=(l == DEPTH - 1 and v > 0))
                S.dma("sp", T.kTg[:, v * NLAT:(v + 1) * NLAT], T.kT_s[v][:, 0:NLAT])
                S.dma("sp", T.vg[v * NLAT:(v + 1) * NLAT, :], T.v_s[v][0:NLAT, :])
                if v == 0:
                    S.dma("sp", T.kTg[:, 8 * NLAT:], T.kT_s[v][:, NLAT:])
                    S.dma("sp", T.vg[8 * NLAT:, :], T.v_s[v][NLAT:, :])
            S.barrier()
            for v in (range(NCORES) if l < DEPTH - 1 else [0]):
                emit_B(X, T, l, v, need_ctx=(l < DEPTH - 1 and v == 0))
        S.finish("sp")
        print("fused: instructions", S.nins, "waits", S.nwaits)
    return nc


def inputs_fused(inp):
    consts = make_consts()
    rtab = make_rt()
    x = inp["x"][0]
    ctx = inp["ctx"][0]
    c_cols = np.ascontiguousarray(np.stack([inp["c"][0], inp["c_ctx"]], 1))
    shared = {
        "cT": c_cols,
        "w_mod": np.ascontiguousarray(inp["w_mod"]),
        "bmodT": np.ascontiguousarray(inp["b_mod"].reshape(DEPTH, 48, 128).transpose(0, 2, 1)),
        "g0T": np.ascontiguousarray(inp["norm_gain"][:, 0].reshape(DEPTH, 8, 128).transpose(0, 2, 1)),
        "g1T": np.ascontiguousarray(inp["norm_gain"][:, 1].reshape(DEPTH, 8, 128).transpose(0, 2, 1)),
        "w_in": np.ascontiguousarray(inp["w_in"]),
        "w_uq": np.ascontiguousarray(inp["mla_w_uq"]),
        "w_ukv": np.ascontiguousarray(inp["mla_w_ukv"]),
        "gvec": np.stack([make_gvec(inp, l) for l in range(DEPTH)]),
        "consts": consts,
        "w_branch": np.ascontiguousarray(inp["w_branch"].reshape(DEPTH, 2048, D)),
        "w_out": np.ascontiguousarray(inp["w_out"]),
        "w_fi": np.ascontiguousarray(inp["w_ffn_in"]),
        "w_fo": np.ascontiguousarray(inp["w_ffn_out"]),
        "rt": rtab,
        "retd": np.ascontiguousarray(np.tile(inp["ret_decay"].reshape(DEPTH, 1, 8), (1, 128, 1))),
        "dlam": np.ascontiguousarray(np.tile(inp["diff_lambda"].reshape(DEPTH, 1, 256), (1, 128, 1))),
    }
    maps = []
    for r in range(NCORES):
        order = [(r + v) % NCORES for v in range(NCORES)]
        xrot = np.concatenate([x[g * NLAT:(g + 1) * NLAT] for g in order] + [ctx], 0)
        m = dict(shared)
        m["xT_all"] = np.ascontiguousarray(xrot.T)
        m["rope"] = np.stack([make_rope_tables(g) for g in order])
        m["retE"] = np.stack([make_retE(g, r) for g in order])
        maps.append(m)
    return maps


_PROG = {}


def kernel(**inp):
    inp = {k: np.asarray(v) for k, v in inp.items()}
    if "f" not in _PROG:
        _PROG["f"] = build_fused()
    res = run_bass_kernel_spmd(_PROG["f"], inputs_fused(inp), core_ids=list(range(NCORES)))
    out = np.concatenate([np.asarray(res.results[r]["xTo"]).T for r in range(NCORES)], 0)[None]
    return np.ascontiguousarray(out.astype(np.float32))
```

```python
import math
import numpy as np
import ml_dtypes
import concourse.bass as bass
import concourse.mybir as mybir
from concourse.bass_utils import run_bass_kernel_spmd
from contextlib import ExitStack

F32 = mybir.dt.float32
BF16 = mybir.dt.bfloat16
AF = mybir.ActivationFunctionType
ALU = mybir.AluOpType

NCORES = 8
D = 1024
SEQ = 16384
NLAT = SEQ // NCORES
NCTX = 256
NT = NLAT + NCTX
GROUPS = [(0, 512), (512, 512), (1024, 512), (1536, 512), (2048, 256)]
DEPTH = 2
FFN = 2816
EPS = 1e-6

O_AQ, O_AK, O_AV = 0, 512, 640
O_BQ, O_BK, O_BV, O_BG = 768, 1024, 1280, 1792
O_CQ, O_CKV, O_CKR = 2304, 2688, 2944
O_DQ, O_DK, O_DV = 2976, 3488, 4000
O_G = 4512
IN_W = 8608
QA, QB, QC, QD = 0, 512, 768, 1536
NQ = 2048
KA, KCN, KCR, KD, KB = 0, 128, 640, 672, 1184
NK = 1440
VA, VC, VD, VB, VBK = 0, 128, 640, 1152, 1664
NV = 1920
NG = 4608
GV_AQ, GV_AK, GV_DQ, GV_DK, GV_CQ, GV_CKV, GV_MQ, GV_MKN, GV_MKR, GV_SC96, GV_EPS, GV_SUBLN = 0, 1, 2, 3, 4, 7, 9, 10, 11, 12, 13, 14
NGV = 16
C_ONES, C_BD64, C_BD96, C_PSW, C_ID = 0, 128, 256, 384, 512
NCONST = 640


class Buf:
    __slots__ = ("t", "lw", "rd", "name")

    def __init__(self, t, name=""):
        self.t = t
        self.lw = {}
        self.rd = {}
        self.name = name

    def __getitem__(self, idx):
        return self.t[idx]


class Ring:
    def __init__(self, bufs):
        self.bufs = bufs
        self.i = 0

    def next(self):
        b = self.bufs[self.i % len(self.bufs)]
        self.i += 1
        return b


class Sync:
    def __init__(self, nc, es, n_dma_sems=24):
        self.nc = nc
        self.engs = {"pe": nc.tensor, "act": nc.scalar, "dve": nc.vector,
                     "pool": nc.gpsimd, "sp": nc.sync}
        self.sems = {}
        self.cnt = {}
        for k in self.engs:
            self.sems[k] = es.enter_context(nc.semaphore("s_" + k))
            self.cnt[k] = 0
        self.dsems = []
        for i in range(n_dma_sems):
            key = "d%d" % i
            self.sems[key] = es.enter_context(nc.semaphore("s_" + key))
            self.cnt[key] = 0
            self.dsems.append(key)
        self.dnext = 0
        self.seen = {k: {} for k in self.engs}
        self.nwaits = 0
        self.nins = 0

    def _wait(self, e, tok):
        if tok is None:
            return
        key, val = tok
        if key == e:
            return
        if self.seen[e].get(key, 0) >= val:
            return
        self.engs[e].wait_ge(self.sems[key], val)
        self.seen[e][key] = val
        self.nwaits += 1

    def _deps(self, e, reads, writes, par=False):
        for b in reads:
            for k, v in b.lw.items():
                self._wait(e, (k, v))
        for b in writes:
            for k, v in b.lw.items():
                if par and k[0] == "d":
                    continue
                self._wait(e, (k, v))
            for k, v in b.rd.items():
                self._wait(e, (k, v))

    def _mark(self, tok, reads, writes, par=False):
        for b in reads:
            b.rd[tok[0]] = tok[1]
        for b in writes:
            if par:
                b.lw[tok[0]] = tok[1]
            else:
                b.lw = {tok[0]: tok[1]}
                b.rd = {}

    def op(self, e, fn, reads=(), writes=()):
        self._deps(e, reads, writes)
        ins = fn()
        self.cnt[e] += 1
        ins.then_inc(self.sems[e], 1)
        tok = (e, self.cnt[e])
        self._mark(tok, reads, writes)
        self.nins += 1
        return tok

    def dma(self, q, out, in_, reads=(), writes=(), par=False, **kw):
        key = self.dsems[self.dnext]
        self.dnext = (self.dnext + 1) % len(self.dsems)
        self._wait(q, (key, self.cnt[key]))
        self._deps(q, reads, writes, par)
        ins = self.engs[q].dma_start(out=out, in_=in_, **kw)
        self.cnt[key] += 16
        ins.then_inc(self.sems[key], 16)
        tok = (key, self.cnt[key])
        self._mark(tok, reads, writes, par)
        self.nins += 1
        return tok

    def barrier(self):
        for e in self.engs:
            for k, v in self.cnt.items():
                if k != e and v > 0:
                    self._wait(e, (k, v))

    def finish(self, e="sp"):
        for k, v in self.cnt.items():
            if k != e and v > 0:
                self._wait(e, (k, v))


class Ctx:
    def __init__(self, nc, es):
        self.nc = nc
        self.es = es
        self.S = Sync(nc, es)
        self.uid = 0

    def sb(self, es, name, shape, dt):
        self.uid += 1
        return Buf(es.enter_context(self.nc.sbuf_tensor("sb%d_%s" % (self.uid, name), shape, dt)), name)

    def ps(self, es, name, shape, dt=F32):
        self.uid += 1
        return Buf(es.enter_context(self.nc.psum_tensor("ps%d_%s" % (self.uid, name), shape, dt)), name)

    def ring(self, es, name, shape, dt, n, psum=False):
        mk = self.ps if psum else self.sb
        return Ring([mk(es, "%s%d" % (name, i), shape, dt) for i in range(n)])


def emit_modvec(X, esp, es, cT, w_mod, bmodT, modT_d):
    nc, S = X.nc, X.S
    modT = X.sb(esp, "modT_sb", [128, 48, 2], F32)
    csb = X.sb(es, "csb", [128, 8, 2], F32)
    sg = X.sb(es, "csg", [128, 8, 2], F32)
    bm = X.sb(es, "bmod", [128, 48], F32)
    pm = X.ps(es, "pmod", [128, 48, 2], F32)
    wr = X.ring(es, "wmod", [128, 8, 512], F32, 2)
    S.dma("sp", csb[:], cT.rearrange("(c p) j -> p c j", p=128), writes=[csb])
    S.dma("sp", bm[:], bmodT, writes=[bm])
    S.op("act", lambda: nc.scalar.activation(out=sg[:], in_=csb[:], func=AF.Sigmoid), reads=[csb], writes=[sg])
    S.op("dve", lambda: nc.vector.tensor_tensor(out=sg[:], in0=sg[:], in1=csb[:], op=ALU.mult),
         reads=[sg, csb], writes=[sg])
    for ch in range(12):
        w = wr.next()
        S.dma("sp", w[:], w_mod[:, ch * 512:(ch + 1) * 512].rearrange("(c p) n -> p c n", p=128), writes=[w])
        for nb in range(4):
            for kc in range(8):
                S.op("pe", lambda: nc.tensor.matmul(pm[:, ch * 4 + nb, :], lhsT=w[:, kc, nb * 128:(nb + 1) * 128],
                                                    rhs=sg[:, kc, :], start=(kc == 0), stop=(kc == 7)),
                     reads=[w, sg], writes=[pm])
    for j in range(2):
        S.op("dve", lambda: nc.vector.tensor_tensor(out=modT[:, :, j], in0=pm[:, :, j], in1=bm[:], op=ALU.add),
             reads=[pm, bm], writes=[modT])
    S.dma("sp", modT_d.rearrange("p (a j) -> p a j", j=2), modT[:], reads=[modT])
    return modT


def emit_norm_mod(X, nc, S, xg, tn, jcol, Gm, Sft, ones, epsc, sqr, pms_r, tmp_r, hT_out):
    pms = pms_r.next()
    for c in range(8):
        sq = sqr.next()
        S.op("act", lambda: nc.scalar.activation(out=sq[:, :tn], in_=xg[:, c, :tn], func=AF.Square),
             reads=[xg], writes=[sq])
        S.op("pe", lambda: nc.tensor.matmul(pms[:, :tn], lhsT=ones, rhs=sq[:, :tn], start=(c == 0), stop=(c == 7)),
             reads=[sq], writes=[pms])
    rstd = getattr(tmp_r, "rstd", None) or tmp_r.next()
    S.op("act", lambda: nc.scalar.activation(out=rstd[:, :tn], in_=pms[:, :tn], func=AF.Ln, scale=1.0 / D, bias=epsc),
         reads=[pms], writes=[rstd])
    S.op("act", lambda: nc.scalar.activation(out=rstd[:, :tn], in_=rstd[:, :tn], func=AF.Exp, scale=-0.5),
         reads=[rstd], writes=[rstd])
    for c in range(8):
        t = tmp_r.next()
        S.op("dve", lambda: nc.vector.scalar_tensor_tensor(out=t[:, :tn], in0=xg[:, c, :tn], scalar=Gm[:, c, jcol:jcol + 1],
                                                           in1=rstd[:, :tn], op0=ALU.mult, op1=ALU.mult),
             reads=[xg, Gm, rstd], writes=[t])
        S.op("act", lambda: nc.scalar.activation(out=hT_out[:, c, :tn], in_=t[:, :tn], func=AF.Identity,
                                                 bias=Sft[:, c, jcol:jcol + 1]),
             reads=[t, Sft], writes=[hT_out])


def emit_A(X, T, l, vs):
    nc, S = X.nc, X.S
    xin = T.xin[l]
    v = vs[0]

    def xc(t0, tn):
        c0 = (8 * NLAT + t0 - NLAT) if t0 >= NLAT else (v * NLAT + t0)
        return xin[:, c0:c0 + tn]
    w_in, w_uq, w_ukv, g0T, gvec, consts = T.w_in[l], T.w_uq[l], T.w_ukv[l], T.g0T[l], T.gvec[l], T.consts
    modT_i = T.modT_s[l]
    with ExitStack() as es0:
        S.barrier()
        cst = X.sb(es0, "cst", [128, NCONST], F32)
        gv = X.sb(es0, "gv", [128, NGV], F32)
        S.dma("sp", cst[:], consts, writes=[cst])
        S.dma("sp", gv[:], gvec, writes=[gv])
        identb = X.sb(es0, "identb", [128, 128], BF16)
        S.op("dve", lambda: nc.vector.tensor_copy(out=identb[:], in_=cst[:, C_ID:C_ID + 128]), reads=[cst], writes=[identb])
        ones = cst[:, C_ONES:C_ONES + 128]
        epsc = gv[:, GV_EPS:GV_EPS + 1]
        modT = X.sb(es0, "modTa", [128, 48, 2], F32)
        S.dma("sp", modT[:], modT_i.rearrange("p (a j) -> p a j", j=2), writes=[modT])
        g0 = X.sb(es0, "g0", [128, 8], F32)
        S.dma("sp", g0[:], g0T, writes=[g0])
        Gm = X.sb(es0, "Gm", [128, 8, 2], F32)
        for j in range(2):
            S.op("dve", lambda: nc.vector.scalar_tensor_tensor(out=Gm[:, :, j], in0=modT[:, 8:16, j], scalar=1.0,
                                                               in1=g0[:], op0=ALU.add, op1=ALU.mult),
                 reads=[modT, g0], writes=[Gm])
        Sft = modT
        S.barrier()

        with ExitStack() as es:
            NW1 = O_G
            w1 = X.sb(es, "w1", [128, 8, NW1], BF16)
            for c in range(8):
                S.dma("pool", w1[:, c, :], w_in[c * 128:(c + 1) * 128, 0:NW1], writes=[w1], par=True)
            wuq = X.sb(es, "wuq", [128, 3, 768], BF16)
            S.dma("pool", wuq[:], w_uq.rearrange("(c p) n -> p c n", p=128), writes=[wuq])
            wukv = X.sb(es, "wukv", [128, 2, 2, 8, 64], BF16)
            for c in range(2):
                for t in range(2):
                    S.dma("pool", wukv[:, c, t], w_ukv[c * 128:(c + 1) * 128, :].rearrange("p (h t e) -> p t h e", h=8, t=2)[:, t],
                          writes=[wukv], par=True)
            xg = X.sb(es, "xg", [128, 8, 512], F32)
            hTg = X.sb(es, "hTg", [128, 8, 512], BF16)
            tab = X.sb(es, "tab", [128, 6, 512], F32)
            sqr = X.ring(es, "sq", [128, 512], F32, 2)
            tmp = X.ring(es, "tmp", [128, 512], F32, 8)
            tmp.rstd = X.sb(es, "rstd1", [128, 512], F32)
            xsr = X.ring(es, "xs", [128, 512], F32, 4)
            outr = X.ring(es, "ob", [128, 512], BF16, 4)
            cqn = X.sb(es, "cqn", [128, 3, 512], BF16)
            ckvn = X.sb(es, "ckvn", [128, 2, 512], BF16)
            pmm = X.ring(es, "pmm", [128, 512], F32, 3, psum=True)
            pms_r = X.ring(es, "pms", [128, 512], F32, 2, psum=True)
            prx = X.ring(es, "prx", [128, 512], F32, 2, psum=True)
            ptr = X.ps(es, "ptr", [128, 4, 128], BF16)

            def mm_fm(pm, M, tn, lhs_fn, rhs_fn, nk, rd):
                for c in range(nk):
                    S.op("pe", lambda: nc.tensor.matmul(pm[0:M, :tn], lhsT=lhs_fn(c), rhs=rhs_fn(c),
                                                        start=(c == 0), stop=(c == nk - 1)),
                         reads=rd, writes=[pm])

            def rstd_from(xsq_list, M, tn, bd, scale):
                pst = pms_r.next()
                n = len(xsq_list)
                for i, sq in enumerate(xsq_list):
                    S.op("pe", lambda: nc.tensor.matmul(pst[0:M, :tn], lhsT=bd, rhs=sq[0:M, :tn],
                                                        start=(i == 0), stop=(i == n - 1)),
                         reads=[sq, cst], writes=[pst])
                r = tmp.next()
                S.op("act", lambda: nc.scalar.activation(out=r[0:M, :tn], in_=pst[0:M, :tn], func=AF.Ln, scale=scale,
                                                         bias=epsc[0:M, :]),
                     reads=[pst, gv], writes=[r])
                S.op("act", lambda: nc.scalar.activation(out=r[0:M, :tn], in_=r[0:M, :tn], func=AF.Exp, scale=-0.5),
                     reads=[r], writes=[r])
                return r

            def rope_store(xn, M, tn, ti, dst_ap, dst_buf=None):
                pr = prx.next()
                S.op("pe", lambda: nc.tensor.matmul(pr[0:M, :tn], lhsT=cst[0:M, C_PSW:C_PSW + M], rhs=xn[0:M, :tn],
                                                    start=True, stop=True), reads=[xn, cst], writes=[pr])
                t1 = tmp.next()
                S.op("pool", lambda: nc.gpsimd.tensor_tensor(out=t1[0:M, :tn], in0=xn[0:M, :tn], in1=tab[0:M, 2 * ti, :tn],
                                                             op=ALU.mult), reads=[xn, tab], writes=[t1])
                t2 = tmp.next()
                S.op("dve", lambda: nc.vector.tensor_tensor(out=t2[0:M, :tn], in0=pr[0:M, :tn], in1=tab[0:M, 2 * ti + 1, :tn],
                                                            op=ALU.mult), reads=[pr, tab], writes=[t2])
                ob = outr.next()
                S.op("dve", lambda: nc.vector.tensor_tensor(out=ob[0:M, :tn], in0=t1[0:M, :tn], in1=t2[0:M, :tn], op=ALU.add),
                     reads=[t1, t2], writes=[ob])
                S.dma("sp", dst_ap, ob[0:M, :tn], reads=[ob])
                return ob

            def job_norm(pm, M, tn, bd, scale, gcol, ti, dst_ap):
                xs = xsr.next()
                S.op("act", lambda: nc.scalar.copy(out=xs[0:M, :tn], in_=pm[0:M, :tn]), reads=[pm], writes=[xs])
                sq = sqr.next()
                S.op("act", lambda: nc.scalar.activation(out=sq[0:M, :tn], in_=pm[0:M, :tn], func=AF.Square),
                     reads=[pm], writes=[sq])
                r = rstd_from([sq], M, tn, bd, scale)
                if ti is None:
                    ob = outr.next()
                    S.op("dve", lambda: nc.vector.scalar_tensor_tensor(out=ob[0:M, :tn], in0=xs[0:M, :tn],
                                                                       scalar=gv[0:M, gcol:gcol + 1], in1=r[0:M, :tn],
                                                                       op0=ALU.mult, op1=ALU.mult),
                         reads=[xs, gv, r], writes=[ob])
                    S.dma("sp", dst_ap, ob[0:M, :tn], reads=[ob])
                    return ob
                xn = tmp.next()
                S.op("dve", lambda: nc.vector.scalar_tensor_tensor(out=xn[0:M, :tn], in0=xs[0:M, :tn],
                                                                   scalar=gv[0:M, gcol:gcol + 1], in1=r[0:M, :tn],
                                                                   op0=ALU.mult, op1=ALU.mult),
                     reads=[xs, gv, r], writes=[xn])
                return rope_store(xn, M, tn, ti, dst_ap)

            bd64 = cst[:, C_BD64:C_BD64 + 128]
            for v in vs:
                kv_only = (l == DEPTH - 1 and v > 0)
                groups = GROUPS if v == 0 else GROUPS[:4]
                rope = T.rope[v]
                qT_d, kT_d, v_d, gT_d = T.qT_s[v], T.kT_s[v], T.v_s[v], T.gT_s[v]
                hT_d = T.hT_s[v]
                hT_db = Buf(hT_d, "hT_scr")
                for gi, (t0, tn) in enumerate(groups):
                    jcol = 1 if gi == 4 else 0
                    S.dma("sp", xg[:, :, :tn], xc(t0, tn).rearrange("(c p) t -> p c t", p=128), writes=[xg])
                    S.dma("sp", tab[:, :, :tn], rope[:, :, t0:t0 + tn].rearrange("s p t -> p s t"), writes=[tab])
                    emit_norm_mod(X, nc, S, xg, tn, jcol, Gm, Sft, ones, epsc, sqr, pms_r, tmp, hTg)
                    S.dma("sp", hT_d[:, t0:t0 + tn].rearrange("(c p) t -> p c t", p=128), hTg[:, :, :tn], reads=[hTg],
                          writes=[hT_db], par=True)
                    hrhs = lambda c: hTg[:, c, :tn]
                    fam = [(O_AQ, 4, GV_AQ, qT_d, QA), (O_AK, 1, GV_AK, kT_d, KA),
                           (O_DQ, 4, GV_DQ, qT_d, QD), (O_DK, 4, GV_DK, kT_d, KD)]
                    if kv_only:
                        fam = [f_ for f_ in fam if f_[3] is kT_d]
                    for (co, nb, gcol, dst, r0) in fam:
                        for b in range(nb):
                            pm = pmm.next()
                            mm_fm(pm, 128, tn, lambda c: w1[:, c, co + b * 128:co + (b + 1) * 128], hrhs, 8, [w1, hTg])
                            job_norm(pm, 128, tn, bd64, 1.0 / 64, gcol, 0, dst[r0 + b * 128:r0 + (b + 1) * 128, t0:t0 + tn])
                    for (co, nb, scl, dst, r0, tokmaj) in ([] if kv_only else [(O_BQ, 2, 1.0, qT_d, QB, False)]) + [(O_BK, 2, 0.125, kT_d, KB, True)]:
                        for b in range(nb):
                            pm = pmm.next()
                            mm_fm(pm, 128, tn, lambda c: w1[:, c, co + b * 128:co + (b + 1) * 128], hrhs, 8, [w1, hTg])
                            xn = tmp.next()
                            S.op("act", lambda: nc.scalar.mul(out=xn[:, :tn], in_=pm[:, :tn], mul=scl),
                                 reads=[pm], writes=[xn])
                            ob = rope_store(xn, 128, tn, 0, dst[r0 + b * 128:r0 + (b + 1) * 128, t0:t0 + tn])
                            if tokmaj:
                                nsub = tn // 128
                                for s in range(nsub):
                                    S.op("pe", lambda: nc.tensor.transpose(out=ptr[:, s, :], in_=ob[:, s * 128:(s + 1) * 128],
                                                                           identity=identb[:]),
                                         reads=[ob, identb], writes=[ptr])
                                kb = outr.next()
                                S.op("dve", lambda: nc.vector.tensor_copy(out=kb[:, :tn], in_=ptr[:, 0:nsub, :].rearrange("p s e -> p (s e)")),
                                     reads=[ptr], writes=[kb])
                                S.dma("sp", v_d[t0:t0 + tn, VBK + b * 128:VBK + (b + 1) * 128].rearrange("(s p) e -> p s e", p=128),
                                      kb[:, :tn].rearrange("p (s e) -> p s e", e=128), reads=[kb])
                    for b in range(0 if kv_only else 4):
                        pm = pmm.next()
                        mm_fm(pm, 128, tn, lambda c: w1[:, c, O_BG + b * 128:O_BG + (b + 1) * 128], hrhs, 8, [w1, hTg])
                        ob = outr.next()
                        S.op("act", lambda: nc.scalar.activation(out=ob[:, :tn], in_=pm[:, :tn], func=AF.Silu),
                             reads=[pm], writes=[ob])
                        S.dma("sp", gT_d[b * 128:(b + 1) * 128, t0:t0 + tn], ob[:, :tn], reads=[ob])
                    for (co, nb, gc0, dstt, nfeat) in ([] if kv_only else [(O_CQ, 3, GV_CQ, cqn, 384.0)]) + [(O_CKV, 2, GV_CKV, ckvn, 256.0)]:
                        xss, sqs = [], []
                        for b in range(nb):
                            pm = pmm.next()
                            mm_fm(pm, 128, tn, lambda c: w1[:, c, co + b * 128:co + (b + 1) * 128], hrhs, 8, [w1, hTg])
                            xs = xsr.next()
                            S.op("act", lambda: nc.scalar.copy(out=xs[:, :tn], in_=pm[:, :tn]), reads=[pm], writes=[xs])
                            sq = tmp.next()
                            S.op("act", lambda: nc.scalar.activation(out=sq[:, :tn], in_=pm[:, :tn], func=AF.Square),
                                 reads=[pm], writes=[sq])
                            xss.append(xs)
                            sqs.append(sq)
                        r = rstd_from(sqs, 128, tn, ones, 1.0 / nfeat)
                        for b in range(nb):
                            S.op("dve", lambda: nc.vector.scalar_tensor_tensor(out=dstt[:, b, :tn], in0=xss[b][:, :tn],
                                                                               scalar=gv[:, gc0 + b:gc0 + b + 1], in1=r[:, :tn],
                                                                               op0=ALU.mult, op1=ALU.mult),
                                 reads=[xss[b], gv, r], writes=[dstt])
                    pm = pmm.next()
                    mm_fm(pm, 32, tn, lambda c: w1[:, c, O_CKR:O_CKR + 32], hrhs, 8, [w1, hTg])
                    job_norm(pm, 32, tn, cst[0:32, C_BD64:C_BD64 + 32], 1.0 / 32, GV_MKR, 2, kT_d[KCR:KCR + 32, t0:t0 + tn])
                    for h in range(0 if kv_only else 8):
                        pm = pmm.next()
                        mm_fm(pm, 96, tn, lambda c: wuq[:, c, h * 96:(h + 1) * 96], lambda c: cqn[:, c, :tn], 3, [wuq, cqn])
                        job_norm(pm, 96, tn, cst[0:96, C_BD96:C_BD96 + 96], gv[0:96, GV_SC96:GV_SC96 + 1], GV_MQ, 1,
                                 qT_d[QC + h * 96:QC + (h + 1) * 96, t0:t0 + tn])
                    for hp in range(4):
                        pm = pmm.next()
                        mm_fm(pm, 128, tn, lambda c: wukv[:, c, 0, 2 * hp:2 * hp + 2, :].rearrange("p h e -> p (h e)"),
                              lambda c: ckvn[:, c, :tn], 2, [wukv, ckvn])
                        job_norm(pm, 128, tn, bd64, 1.0 / 64, GV_MKN, None, kT_d[KCN + hp * 128:KCN + (hp + 1) * 128, t0:t0 + tn])
                    for s in range(tn // 128):
                        ts_ = slice(s * 128, (s + 1) * 128)
                        vjobs = [(lambda c: hTg[:, c, ts_], lambda c: w1[:, c, O_AV:O_AV + 128], 8, 128, VA, [hTg, w1]),
                                 (lambda c: hTg[:, c, ts_], lambda c: w1[:, c, O_BV:O_BV + 512], 8, 512, VB, [hTg, w1]),
                                 (lambda c: hTg[:, c, ts_], lambda c: w1[:, c, O_DV:O_DV + 512], 8, 512, VD, [hTg, w1]),
                                 (lambda c: ckvn[:, c, ts_], lambda c: wukv[:, c, 1].rearrange("p h e -> p (h e)"), 2, 512, VC,
                                  [ckvn, wukv])]
                        for (lf, rf, nk, ncol, vo, rd) in vjobs:
                            pm = pmm.next()
                            for c in range(nk):
                                S.op("pe", lambda: nc.tensor.matmul(pm[:, :ncol], lhsT=lf(c), rhs=rf(c), start=(c == 0),
                                                                    stop=(c == nk - 1)), reads=rd, writes=[pm])
                            ob = outr.next()
                            S.op("act", lambda: nc.scalar.copy(out=ob[:, :ncol], in_=pm[:, :ncol]), reads=[pm], writes=[ob])
                            S.dma("sp", v_d[t0 + s * 128:t0 + (s + 1) * 128, vo:vo + ncol], ob[:, :ncol], reads=[ob])
            S.barrier()
        with ExitStack() as es:
          if True:
              w2 = X.sb(es, "w2", [128, 8, 4096], BF16)
              for c in range(8):
                  S.dma("pool", w2[:, c, :], w_in[c * 128:(c + 1) * 128, O_G:O_G + 4096], writes=[w2], par=True)
              hr = X.ring(es, "hT2", [128, 8, 512], BF16, 2)
              outr = X.ring(es, "ob2", [128, 512], BF16, 4)
              pmm = X.ring(es, "pmm2", [128, 512], F32, 4, psum=True)
              for v in vs:
                  if l == DEPTH - 1 and v > 0:
                      continue
                  groups = GROUPS if v == 0 else GROUPS[:4]
                  gT_d = T.gT_s[v]
                  hT_d = T.hT_s[v]
                  hT_db = Buf(hT_d, "hT_scr")
                  for gi, (t0, tn) in enumerate(groups):
                      hTg = hr.next()
                      S.dma("sp", hTg[:, :, :tn], hT_d[:, t0:t0 + tn].rearrange("(c p) t -> p c t", p=128), reads=[hT_db],
                            writes=[hTg])
                      for b in range(32):
                          pm = pmm.next()
                          for c in range(8):
                              S.op("pe", lambda: nc.tensor.matmul(pm[:, :tn], lhsT=w2[:, c, b * 128:(b + 1) * 128], rhs=hTg[:, c, :tn],
                                                                  start=(c == 0), stop=(c == 7)), reads=[w2, hTg], writes=[pm])
                          ob = outr.next()
                          S.op("act", lambda: nc.scalar.activation(out=ob[:, :tn], in_=pm[:, :tn], func=AF.Sigmoid),
                               reads=[pm], writes=[ob])
                          S.dma("sp", gT_d[512 + b * 128:512 + (b + 1) * 128, t0:t0 + tn], ob[:, :tn], reads=[ob])
        S.barrier()


NKT = 130
NKEY = NKT * 128
BR_A, BR_B, BR_C, BR_D = 0, 512, 1024, 1536
RT_RELF, RT_MSKF, RT_RELB, RT_MSKB, RT_QDF, RT_QDB, RT_KDF, RT_KDB, RT_SEL = 0, 128, 256, 384, 512, 640, 768, 769, 770
NRT = 770 + 64
AX = mybir.AxisListType


def emit_B(X, T, l, vs):
    nc, S = X.nc, X.S
    layer = l
    lam_init = 0.8 - 0.6 * math.exp(-0.3 * layer)
    xin = T.xin[l]
    xout = T.xout[l]
    v = vs[0]

    def xci(t0, tn):
        c0 = (8 * NLAT + t0 - NLAT) if t0 >= NLAT else (v * NLAT + t0)
        return xin[:, c0:c0 + tn]

    def xco(t0, tn):
        if l == DEPTH - 1:
            return xout[:, t0:t0 + tn]
        c0 = (8 * NLAT + t0 - NLAT) if t0 >= NLAT else (v * NLAT + t0)
        return xout[:, c0:c0 + tn]
    kTg, vg = T.kTg, T.vg
    modT_i, g1T = T.modT_s[l], T.g1T[l]
    w_branch, w_out, w_fi, w_fo = T.w_branch[l], T.w_out[l], T.w_fi[l], T.w_fo[l]
    gvec, consts, rt_i, retd_i, dlam_i = T.gvec[l], T.consts, T.rt, T.retd[l], T.dlam[l]
    with ExitStack() as es0:
        S.barrier()
        cst = X.sb(es0, "cst", [128, NCONST], F32)
        gv = X.sb(es0, "gv", [128, NGV], F32)
        modT = X.sb(es0, "modTs", [128, 48, 2], F32)
        rt = X.sb(es0, "rt", [128, NRT], F32)
        g1 = X.sb(es0, "g1", [128, 8], F32)
        Gm2 = X.sb(es0, "Gm2", [128, 8, 2], F32)
        small = X.sb(es0, "small", [128, 64], F32)
        onesb = X.sb(es0, "onesb", [128, 128], BF16)
        S.dma("sp", cst[:], consts, writes=[cst])
        S.dma("sp", gv[:], gvec, writes=[gv])
        S.dma("sp", modT[:], modT_i.rearrange("p (a j) -> p a j", j=2), writes=[modT])
        S.dma("sp", rt[:], rt_i, writes=[rt])
        S.dma("sp", g1[:], g1T, writes=[g1])
        S.barrier()
        S.op("dve", lambda: nc.vector.tensor_copy(out=onesb[:], in_=cst[:, C_ONES:C_ONES + 128]), reads=[cst], writes=[onesb])
        S.op("pool", lambda: nc.gpsimd.memset(small[:], 0.0), writes=[small])
        for j in range(2):
            S.op("dve", lambda: nc.vector.scalar_tensor_tensor(out=Gm2[:, :, j], in0=modT[:, 32:40, j], scalar=1.0,
                                                               in1=g1[:], op0=ALU.add, op1=ALU.mult),
                 reads=[modT, g1], writes=[Gm2])
        ones = cst[:, C_ONES:C_ONES + 128]
        one_col = cst[:, C_ONES:C_ONES + 1]
        epsc = gv[:, GV_EPS:GV_EPS + 1]
        sel = rt[:, RT_SEL:RT_SEL + 64]
        with ExitStack() as es:
            dl = X.sb(es, "dl", [128, 4, 64], F32)
            rd = X.sb(es, "rd", [128, 8], F32)
            pr = X.sb(es, "prd", [128, 2, 64], F32)
            S.dma("sp", dl[:], dlam_i.rearrange("p (a e) -> p a e", e=64), writes=[dl])
            S.dma("sp", rd[:], retd_i, writes=[rd])
            S.op("dve", lambda: nc.vector.tensor_tensor(out=pr[:, 0, :], in0=dl[:, 0, :], in1=dl[:, 1, :], op=ALU.mult),
                 reads=[dl], writes=[pr])
            S.op("dve", lambda: nc.vector.tensor_tensor(out=pr[:, 1, :], in0=dl[:, 2, :], in1=dl[:, 3, :], op=ALU.mult),
                 reads=[dl], writes=[pr])
            for m in range(2):
                S.op("act", lambda: nc.scalar.activation(out=dl[:, m, :], in_=pr[:, m, :], func=AF.Identity,
                                                         accum_out=small[:, 2 + m:3 + m]), reads=[pr], writes=[small, dl])
            S.op("act", lambda: nc.scalar.activation(out=small[:, 2:4], in_=small[:, 2:4], func=AF.Exp), reads=[small], writes=[small])
            S.op("dve", lambda: nc.vector.tensor_tensor(out=small[:, 0:1], in0=small[:, 3:4], in1=small[:, 2:3], op=ALU.subtract),
                 reads=[small], writes=[small])
            S.op("dve", lambda: nc.vector.tensor_scalar(out=small[:, 0:1], in0=small[:, 0:1], scalar1=-lam_init, scalar2=None,
                                                        op0=ALU.add), reads=[small], writes=[small])
            S.op("dve", lambda: nc.vector.tensor_scalar(out=small[:, 1:2], in0=gv[:, GV_SUBLN:GV_SUBLN + 1],
                                                        scalar1=1.0 - lam_init, scalar2=None, op0=ALU.mult),
                 reads=[gv], writes=[small])
            S.op("act", lambda: nc.scalar.activation(out=small[:, 8:16], in_=rd[:], func=AF.Exp, scale=-1.0), reads=[rd], writes=[small])
            S.op("act", lambda: nc.scalar.activation(out=small[:, 8:16], in_=small[:, 8:16], func=AF.Ln, bias=one_col),
                 reads=[small], writes=[small])
            S.op("dve", lambda: nc.vector.tensor_scalar(out=small[:, 8:16], in0=small[:, 8:16], scalar1=-1.0, scalar2=None,
                                                        op0=ALU.mult), reads=[small], writes=[small])
            S.op("act", lambda: nc.scalar.activation(out=small[:, 16:24], in_=small[:, 8:16], func=AF.Exp, scale=128.0),
                 reads=[small], writes=[small])
            for dr in range(2):
                for h in range(4):
                    cc = dr * 4 + h
                    S.op("act", lambda: nc.scalar.activation(out=small[:, 24 + cc:25 + cc], in_=rt[:, RT_KDF + dr:RT_KDF + dr + 1],
                                                             func=AF.Exp, scale=small[:, 8 + cc:9 + cc]),
                         reads=[small, rt], writes=[small])
            S.barrier()
        neg_lam = small[:, 0:1]
        gsub = small[:, 1:2]

        with ExitStack() as es:
            pst = X.ring(es, "pst", [128, 2, 512], F32, 2, psum=True)
            pacc = X.ring(es, "pacc", [128, 512], F32, 4, psum=True)
            kbuf = X.ring(es, "kTu", [128, NKEY], BF16, 2)
            vbuf = X.ring(es, "vau", [128, NKT, 128], BF16, 2)
            qr = X.ring(es, "qh", [128, NT], BF16, 3)
            ptr_ = X.ring(es, "pt", [128, 2, 512], BF16, 3)
            fin = X.ring(es, "fin", [128, 512], F32, 6)
            obr = X.ring(es, "obf", [128, 512], BF16, 3)
            for vb_ in vbuf.bufs:
                S.op("pool", lambda: nc.gpsimd.memset(vb_[:, :, 64:128], 1.0), writes=[vb_])

            def pair_run(steps, tn, scale, rd_qk):
                n = len(steps)
                LA = 2
                pts = [None] * n
                for i in range(n + LA):
                    if i < n:
                        qk = steps[i][0]
                        ps_ = pst.next()
                        for slot, (lh, rh) in enumerate(qk):
                            S.op("pe", lambda: nc.tensor.matmul(ps_[:, slot, :tn], lhsT=lh, rhs=rh, start=True, stop=True),
                                 reads=rd_qk, writes=[ps_])
                        pt = ptr_.next()
                        ns = len(qk)
                        S.op("act", lambda: nc.scalar.activation(out=pt[:, 0:ns, :tn], in_=ps_[:, 0:ns, :tn], func=AF.Exp, scale=scale),
                             reads=[ps_], writes=[pt])
                        pts[i] = pt
                    if i >= LA:
                        j = i - LA
                        for (acc, lf, slot, st_, sp_, rdl) in steps[j][1]:
                            S.op("pe", lambda: nc.tensor.matmul(acc[:, :tn], lhsT=lf, rhs=pts[j][:, slot, :tn], start=st_, stop=sp_),
                                 reads=[pts[j]] + rdl, writes=[acc])

            def finalize_AC(acc, tn, dst_ap):
                a = fin.next()
                S.op("dve", lambda: nc.vector.tensor_copy(out=a[:, :tn], in_=acc[:, :tn]), reads=[acc], writes=[a])
                pn = pst.next()
                S.op("pe", lambda: nc.tensor.matmul(pn[0:64, 0, :tn], lhsT=sel, rhs=a[:, :tn], start=True, stop=True),
                     reads=[a, rt], writes=[pn])
                rc = fin.next()
                S.op("dve", lambda: nc.vector.reciprocal(out=rc[0:64, :tn], in_=pn[0:64, 0, :tn]), reads=[pn], writes=[rc])
                ob = obr.next()
                S.op("dve", lambda: nc.vector.tensor_tensor(out=ob[0:64, :tn], in0=a[0:64, :tn], in1=rc[0:64, :tn], op=ALU.mult),
                     reads=[a, rc], writes=[ob])
                S.dma("sp", dst_ap, ob[0:64, :tn], reads=[ob], writes=[brT_b], par=True)

            units = [("A", g) for g in range(2)] + [("C", h) for h in range(8)] + [("D", h) for h in range(4)]
            all_kt = list(range(NKT))
            ctx_kt = [128, 129]
            for (kind, u) in units:
                kb_ = kbuf.next()
                vb_ = vbuf.next()
                if kind == "A":
                    S.dma("sp", kb_[0:64, :], kTg[KA + u * 64:KA + (u + 1) * 64, :], writes=[kb_], par=True)
                    S.dma("sp", kb_[64:128, :], kTg[KA + u * 64:KA + (u + 1) * 64, :], writes=[kb_], par=True)
                    vsrc = vg[:, VA + u * 64:VA + (u + 1) * 64]
                    qtiles = [(QA + (4 * u + 2 * p) * 64, 128) for p in range(2)]
                    vw, scale = 64, 0.125
                elif kind == "C":
                    S.dma("sp", kb_[0:64, :], kTg[KCN + u * 64:KCN + (u + 1) * 64, :], writes=[kb_], par=True)
                    S.dma("sp", kb_[64:96, :], kTg[KCR:KCR + 32, :], writes=[kb_], par=True)
                    vsrc = vg[:, VC + u * 64:VC + (u + 1) * 64]
                    qtiles = [(QC + u * 96, 96)]
                    vw, scale = 64, 96.0 ** -0.5
                else:
                    S.dma("sp", kb_[:, :], kTg[KD + u * 128:KD + (u + 1) * 128, :], writes=[kb_])
                    vsrc = vg[:, VD + u * 128:VD + (u + 1) * 128]
                    qtiles = [(QD + u * 128, 128)]
                    vw, scale = 128, 0.125
                for pc in range(5):
                    S.dma("sp", vb_[:, pc * 26:(pc + 1) * 26, 0:vw],
                          vsrc[pc * 26 * 128:(pc + 1) * 26 * 128, :].rearrange("(t p) e -> p t e", p=128), writes=[vb_], par=True)
                for v in vs:
                    need_ctx = (l < DEPTH - 1 and v == 0)
                    groups = GROUPS if need_ctx else GROUPS[:4]
                    qT = T.qT_s[v]
                    brT_d = T.brT_s[v]
                    brT_b = Buf(brT_d, "brT")
                    for qi, (qrow, qrows) in enumerate(qtiles):
                        qh = qr.next()
                        S.dma("sp", qh[0:qrows, :], qT[qrow:qrow + qrows, :], writes=[qh])
                        rd_qk = [kb_, qh]
                        for gi, (t0, tn) in enumerate(groups):
                            kts = ctx_kt if gi == 4 else all_kt
                            nk = len(kts)
                            qs = slice(t0, t0 + tn)
                            if kind == "A":
                                a0, a1 = pacc.next(), pacc.next()
                                steps = []
                                for ii, kt in enumerate(kts):
                                    ks = slice(kt * 128, (kt + 1) * 128)
                                    st_, sp_ = (ii == 0), (ii == nk - 1)
                                    steps.append(([(kb_[0:64, ks], qh[0:64, qs]), (kb_[64:128, ks], qh[64:128, qs])],
                                                  [(a0, vb_[:, kt, :], 0, st_, sp_, [vb_]), (a1, vb_[:, kt, :], 1, st_, sp_, [vb_])]))
                                pair_run(steps, tn, scale, rd_qk)
                                hd = 4 * u + 2 * qi
                                finalize_AC(a0, tn, brT_d[BR_A + hd * 64:BR_A + (hd + 1) * 64, qs])
                                finalize_AC(a1, tn, brT_d[BR_A + (hd + 1) * 64:BR_A + (hd + 2) * 64, qs])
                            elif kind == "C":
                                acc = pacc.next()
                                steps = []
                                for ii in range(0, nk, 2):
                                    k0, k1 = kts[ii], kts[ii + 1]
                                    steps.append(([(kb_[0:96, k0 * 128:(k0 + 1) * 128], qh[0:96, qs]), (kb_[0:96, k1 * 128:(k1 + 1) * 128], qh[0:96, qs])],
                                                  [(acc, vb_[:, k0, :], 0, ii == 0, False, [vb_]), (acc, vb_[:, k1, :], 1, False, ii + 2 >= nk, [vb_])]))
                                pair_run(steps, tn, scale, rd_qk)
                                finalize_AC(acc, tn, brT_d[BR_C + u * 64:BR_C + (u + 1) * 64, qs])
                            else:
                                ao0, as0, ao1, as1 = pacc.next(), pacc.next(), pacc.next(), pacc.next()
                                steps = []
                                for ii, kt in enumerate(kts):
                                    ks = slice(kt * 128, (kt + 1) * 128)
                                    st_, sp_ = (ii == 0), (ii == nk - 1)
                                    steps.append(([(kb_[0:64, ks], qh[0:64, qs]), (kb_[64:128, ks], qh[64:128, qs])],
                                                  [(ao0, vb_[:, kt, :], 0, st_, sp_, [vb_]), (as0, onesb[:], 0, st_, sp_, [onesb]),
                                                   (ao1, vb_[:, kt, :], 1, st_, sp_, [vb_]), (as1, onesb[:], 1, st_, sp_, [onesb])]))
                                pair_run(steps, tn, scale, rd_qk)
                                accs = [(ao0, as0), (ao1, as1)]
                                brow = BR_D + u * 128
                                r0, r1, t0_, t1_ = fin.next(), fin.next(), fin.next(), fin.next()
                                S.op("dve", lambda: nc.vector.reciprocal(out=r0[:, :tn], in_=accs[0][1][:, :tn]), reads=[accs[0][1]], writes=[r0])
                                S.op("dve", lambda: nc.vector.reciprocal(out=r1[:, :tn], in_=accs[1][1][:, :tn]), reads=[accs[1][1]], writes=[r1])
                                S.op("dve", lambda: nc.vector.tensor_tensor(out=t0_[:, :tn], in0=accs[0][0][:, :tn], in1=r0[:, :tn], op=ALU.mult),
                                     reads=[accs[0][0], r0], writes=[t0_])
                                S.op("dve", lambda: nc.vector.scalar_tensor_tensor(out=t1_[:, :tn], in0=accs[1][0][:, :tn], scalar=neg_lam,
                                                                                   in1=r1[:, :tn], op0=ALU.mult, op1=ALU.mult),
                                     reads=[accs[1][0], r1, small], writes=[t1_])
                                S.op("dve", lambda: nc.vector.tensor_tensor(out=t0_[:, :tn], in0=t0_[:, :tn], in1=t1_[:, :tn], op=ALU.add),
                                     reads=[t0_, t1_], writes=[t0_])
                                S.op("act", lambda: nc.scalar.activation(out=r0[:, :tn], in_=t0_[:, :tn], func=AF.Square), reads=[t0_], writes=[r0])
                                pn = pst.next()
                                S.op("pe", lambda: nc.tensor.matmul(pn[:, 0, :tn], lhsT=ones, rhs=r0[:, :tn], start=True, stop=True),
                                     reads=[r0], writes=[pn])
                                S.op("act", lambda: nc.scalar.activation(out=r1[:, :tn], in_=pn[:, 0, :tn], func=AF.Ln, scale=1.0 / 128, bias=epsc),
                                     reads=[pn], writes=[r1])
                                S.op("act", lambda: nc.scalar.activation(out=r1[:, :tn], in_=r1[:, :tn], func=AF.Exp, scale=-0.5),
                                     reads=[r1], writes=[r1])
                                ob = obr.next()
                                S.op("dve", lambda: nc.vector.scalar_tensor_tensor(out=ob[:, :tn], in0=t0_[:, :tn], scalar=gsub, in1=r1[:, :tn],
                                                                                   op0=ALU.mult, op1=ALU.mult),
                                     reads=[t0_, r1, small], writes=[ob])
                                S.dma("sp", brT_d[brow:brow + 128, qs], ob[:, :tn], reads=[ob], writes=[brT_b], par=True)
            S.barrier()

        with ExitStack() as es:
            mask = X.sb(es, "rmask", [128, 4, 128], F32)
            qdec = X.sb(es, "qdec", [64, 8, 128], F32)
            coef = X.sb(es, "coef", [128, 2, NKT], F32)
            Eb = X.sb(es, "retEs", [128, 4, NKT], F32)
            kbg = X.sb(es, "kbg", [128, NKT, 64], BF16)
            vbg = X.sb(es, "vbg", [128, NKT, 128], BF16)
            kw = X.ring(es, "kw", [128, NKT, 64], BF16, 2)
            kbl = X.sb(es, "kbl", [128, 18, 64], BF16)
            vbl = X.sb(es, "vbl", [128, 18, 128], BF16)
            kdl = X.ring(es, "kdl", [128, 18, 64], BF16, 2)
            qh = X.sb(es, "rqh", [64, NT], BF16)
            kh = X.sb(es, "rkh", [64, NT], BF16)
            gh = X.sb(es, "rgh", [128, NT], BF16)
            qd = X.ring(es, "rqd", [64, 2, 128], BF16, 3)
            st = X.ring(es, "rst", [64, 128], F32, 2)
            snap = X.sb(es, "snap", [64, 2, 18, 128], BF16)
            sm = X.ring(es, "rsm", [128, 128], BF16, 3)
            tmpf = X.ring(es, "rtmp", [128, 512], F32, 4)
            obr = X.ring(es, "rob", [128, 512], BF16, 2)
            pU = X.ring(es, "pU", [64, 128], F32, 2, psum=True)
            pS = X.ring(es, "pS", [128, 128], F32, 2, psum=True)
            pO = X.ring(es, "pO", [128, 512], F32, 2, psum=True)
            pN = X.ring(es, "pN2", [128, 512], F32, 1, psum=True)
            for h in range(4):
                for dr in range(2):
                    cc = dr * 4 + h
                    t = tmpf.next()
                    rel = rt[:, RT_RELF + 256 * dr:RT_RELF + 256 * dr + 128]
                    msk = rt[:, RT_MSKF + 256 * dr:RT_MSKF + 256 * dr + 128]
                    S.op("act", lambda: nc.scalar.activation(out=t[:, 0:128], in_=rel, func=AF.Exp, scale=small[:, 8 + cc:9 + cc]),
                         reads=[small], writes=[t])
                    if dr == 0:
                        S.op("dve", lambda: nc.vector.tensor_tensor(out=mask[:, h, :], in0=t[:, 0:128], in1=msk, op=ALU.mult),
                             reads=[t], writes=[mask])
                    else:
                        S.op("dve", lambda: nc.vector.tensor_tensor(out=t[:, 0:128], in0=t[:, 0:128], in1=msk, op=ALU.mult),
                             reads=[t], writes=[t])
                        S.op("dve", lambda: nc.vector.tensor_tensor(out=mask[:, h, :], in0=mask[:, h, :], in1=t[:, 0:128], op=ALU.add),
                             reads=[t, mask], writes=[mask])
                    S.op("act", lambda: nc.scalar.activation(out=qdec[:, cc, :], in_=rt[0:64, RT_QDF + 128 * dr:RT_QDF + 128 * dr + 128],
                                                             func=AF.Exp, scale=small[0:64, 8 + cc:9 + cc]),
                         reads=[small], writes=[qdec])
            for h in range(4):
                for pc in range(5):
                    S.dma("sp", kbg[:, pc * 26:(pc + 1) * 26, :],
                          vg[pc * 3328:(pc + 1) * 3328, VBK + h * 64:VBK + (h + 1) * 64].rearrange("(t p) e -> p t e", p=128),
                          writes=[kbg], par=True)
                    S.dma("sp", vbg[:, pc * 26:(pc + 1) * 26, :],
                          vg[pc * 3328:(pc + 1) * 3328, VB + h * 128:VB + (h + 1) * 128].rearrange("(t p) e -> p t e", p=128),
                          writes=[vbg], par=True)
                for v in vs:
                    need_ctx = (l < DEPTH - 1 and v == 0)
                    groups = GROUPS if need_ctx else GROUPS[:4]
                    qT, kTl, vl, gT = T.qT_s[v], T.kT_s[v], T.v_s[v], T.gT_s[v]
                    brT_d = T.brT_s[v]
                    brT_b = Buf(brT_d, "brT")
                    S.dma("sp", Eb[:], T.retE[v], writes=[Eb])
                    S.dma("sp", kbl[:], vl[:, VBK + h * 64:VBK + (h + 1) * 64].rearrange("(t p) e -> p t e", p=128), writes=[kbl])
                    S.dma("sp", vbl[:], vl[:, VB + h * 128:VB + (h + 1) * 128].rearrange("(t p) e -> p t e", p=128), writes=[vbl])
                    S.dma("sp", qh[:], qT[QB + h * 64:QB + (h + 1) * 64, :], writes=[qh])
                    S.dma("sp", kh[:], kTl[KB + h * 64:KB + (h + 1) * 64, :], writes=[kh])
                    S.dma("sp", gh[:], gT[h * 128:(h + 1) * 128, :], writes=[gh])
                    for dr in range(2):
                        cc = dr * 4 + h
                        S.op("act", lambda: nc.scalar.activation(out=coef[:, dr, :], in_=Eb[:, 2 * dr, :], func=AF.Exp,
                                                                 scale=small[:, 8 + cc:9 + cc]), reads=[Eb, small], writes=[coef])
                        S.op("dve", lambda: nc.vector.tensor_tensor(out=coef[:, dr, :], in0=coef[:, dr, :], in1=Eb[:, 2 * dr + 1, :], op=ALU.mult),
                             reads=[Eb, coef], writes=[coef])
                        kw_ = kw.next()
                        S.op("dve", lambda: nc.vector.tensor_tensor(out=kw_[:], in0=kbg[:], in1=coef[:, dr, :].unsqueeze(2).to_broadcast([128, NKT, 64]),
                                                                    op=ALU.mult), reads=[kbg, coef], writes=[kw_])
                        pu = pU.next()
                        for t in range(NKT):
                            S.op("pe", lambda: nc.tensor.matmul(pu[:], lhsT=kw_[:, t, :], rhs=vbg[:, t, :], start=(t == 0), stop=(t == NKT - 1)),
                                 reads=[kw_, vbg], writes=[pu])
                        s_ = st.next()
                        S.op("dve", lambda: nc.vector.tensor_copy(out=s_[:], in_=pu[:]), reads=[pu], writes=[s_])
                        kd_ = kdl.next()
                        S.op("dve", lambda: nc.vector.tensor_scalar(out=kd_[:], in0=kbl[:], scalar1=small[:, 24 + cc:25 + cc], scalar2=None,
                                                                    op0=ALU.mult), reads=[kbl, small], writes=[kd_])
                        order = list(range(16)) if dr == 0 else list(range(15, -1, -1))
                        cdc = small[0:64, 16 + cc:17 + cc]
                        for i in order:
                            S.op("act", lambda: nc.scalar.copy(out=snap[:, dr, i, :], in_=s_[:]), reads=[s_], writes=[snap])
                            pu = pU.next()
                            S.op("pe", lambda: nc.tensor.matmul(pu[:], lhsT=kd_[:, i, :], rhs=vbl[:, i, :], start=True, stop=True),
                                 reads=[kd_, vbl], writes=[pu])
                            S.op("dve", lambda: nc.vector.scalar_tensor_tensor(out=s_[:], in0=s_[:], scalar=cdc, in1=pu[:],
                                                                               op0=ALU.mult, op1=ALU.add), reads=[s_, pu, small], writes=[s_])
                        if need_ctx:
                            first, second = (16, 17) if dr == 0 else (17, 16)
                            S.op("pool", lambda: nc.gpsimd.memset(snap[:, dr, first, :], 0.0), writes=[snap])
                            pu = pU.next()
                            S.op("pe", lambda: nc.tensor.matmul(pu[:], lhsT=kd_[:, first, :], rhs=vbl[:, first, :], start=True, stop=True),
                                 reads=[kd_, vbl], writes=[pu])
                            S.op("act", lambda: nc.scalar.copy(out=snap[:, dr, second, :], in_=pu[:]), reads=[pu], writes=[snap])
                    for gi, (t0, tn) in enumerate(groups):
                        po = pO.next()
                        for s in range(tn // 128):
                            i = t0 // 128 + s
                            ts_ = slice(i * 128, (i + 1) * 128)
                            psc = pS.next()
                            S.op("pe", lambda: nc.tensor.matmul(psc[:], lhsT=kh[:, ts_], rhs=qh[:, ts_], start=True, stop=True),
                                 reads=[kh, qh], writes=[psc])
                            sm_ = sm.next()
                            S.op("dve", lambda: nc.vector.tensor_tensor(out=sm_[:], in0=psc[:], in1=mask[:, h, :], op=ALU.mult),
                                 reads=[psc, mask], writes=[sm_])
                            qd_ = qd.next()
                            for dr in range(2):
                                S.op("pool", lambda: nc.gpsimd.tensor_tensor(out=qd_[:, dr, :], in0=qh[:, ts_], in1=qdec[:, dr * 4 + h, :], op=ALU.mult),
                                     reads=[qh, qdec], writes=[qd_])
                            S.op("pe", lambda: nc.tensor.matmul(po[:, s * 128:(s + 1) * 128], lhsT=vbl[:, i, :], rhs=sm_[:], start=True, stop=False),
                                 reads=[vbl, sm_], writes=[po])
                            for dr in range(2):
                                S.op("pe", lambda: nc.tensor.matmul(po[:, s * 128:(s + 1) * 128], lhsT=snap[:, dr, i, :], rhs=qd_[:, dr, :],
                                                                    start=False, stop=(dr == 1)), reads=[snap, qd_], writes=[po])
                        o_ = tmpf.next()
                        S.op("act", lambda: nc.scalar.copy(out=o_[:, :tn], in_=po[:, :tn]), reads=[po], writes=[o_])
                        q_ = tmpf.next()
                        S.op("act", lambda: nc.scalar.activation(out=q_[:, :tn], in_=po[:, :tn], func=AF.Square), reads=[po], writes=[q_])
                        pn = pN.next()
                        S.op("pe", lambda: nc.tensor.matmul(pn[:, :tn], lhsT=ones, rhs=q_[:, :tn], start=True, stop=True), reads=[q_], writes=[pn])
                        S.op("act", lambda: nc.scalar.activation(out=q_[:, :tn], in_=pn[:, :tn], func=AF.Ln, scale=1.0 / 128, bias=epsc),
                             reads=[pn], writes=[q_])
                        S.op("act", lambda: nc.scalar.activation(out=q_[:, :tn], in_=q_[:, :tn], func=AF.Exp, scale=-0.5), reads=[q_], writes=[q_])
                        S.op("dve", lambda: nc.vector.tensor_tensor(out=o_[:, :tn], in0=o_[:, :tn], in1=q_[:, :tn], op=ALU.mult),
                             reads=[o_, q_], writes=[o_])
                        ob = obr.next()
                        S.op("dve", lambda: nc.vector.tensor_tensor(out=ob[:, :tn], in0=o_[:, :tn], in1=gh[:, t0:t0 + tn], op=ALU.mult),
                             reads=[o_, gh], writes=[ob])
                        S.dma("sp", brT_d[BR_B + h * 128:BR_B + (h + 1) * 128, t0:t0 + tn], ob[:, :tn], reads=[ob], writes=[brT_b], par=True)
            S.barrier()

        with ExitStack() as es:
            wb = X.sb(es, "wbr", [128, 16, D], BF16)
            wo = X.sb(es, "wo", [128, 8, D], BF16)
            S.dma("pool", wb[:], w_branch.rearrange("(c p) n -> p c n", p=128), writes=[wb])
            S.dma("pool", wo[:], w_out.rearrange("(c p) n -> p c n", p=128), writes=[wo])
            brg = X.sb(es, "brg", [128, 16, 512], BF16)
            gtg = X.sb(es, "gtg", [128, 32, 512], BF16)
            xg = X.sb(es, "xg3", [128, 8, 512], F32)
            x1g = X.sb(es, "x1g", [128, 8, 512], F32)
            mg = X.sb(es, "mg", [128, 8, 512], BF16)
            h2g = X.sb(es, "h2g", [128, 8, 512], BF16)
            macc = X.ring(es, "macc", [128, 512], F32, 2)
            tmp = X.ring(es, "tmp3", [128, 512], F32, 4)
            tmp.rstd = X.sb(es, "rstd3", [128, 512], F32)
            sqr = X.ring(es, "sq3", [128, 512], F32, 2)
            pp = X.ring(es, "pp3", [128, 512], F32, 4, psum=True)
            pms_r = X.ring(es, "pms3", [128, 512], F32, 2, psum=True)
            for v in vs:
                need_ctx = (l < DEPTH - 1 and v == 0)
                groups = GROUPS if need_ctx else GROUPS[:4]
                gT = T.gT_s[v]
                brT_d, x1_d, h2_d = T.brT_s[v], T.x1T_s[v], T.h2T_s[v]
                brT_b, x1_b, h2_b = Buf(brT_d, "brT"), Buf(x1_d, "x1"), Buf(h2_d, "h2")
                for gi, (t0, tn) in enumerate(groups):
                    jcol = 1 if gi == 4 else 0
                    S.dma("sp", brg[:, :, :tn], brT_d[:, t0:t0 + tn].rearrange("(c p) t -> p c t", p=128), reads=[brT_b], writes=[brg])
                    S.dma("sp", gtg[:, :, :tn], gT[512:, t0:t0 + tn].rearrange("(c p) t -> p c t", p=128), writes=[gtg])
                    S.dma("sp", xg[:, :, :tn], xci(t0, tn).rearrange("(c p) t -> p c t", p=128), writes=[xg])
                    for ob in range(8):
                        m_ = macc.next()
                        for n in range(4):
                            p_ = pp.next()
                            for kc in range(4):
                                S.op("pe", lambda: nc.tensor.matmul(p_[:, :tn], lhsT=wb[:, n * 4 + kc, ob * 128:(ob + 1) * 128],
                                                                    rhs=brg[:, n * 4 + kc, :tn], start=(kc == 0), stop=(kc == 3)),
                                     reads=[wb, brg], writes=[p_])
                            if n == 0:
                                S.op("dve", lambda: nc.vector.tensor_tensor(out=m_[:, :tn], in0=p_[:, :tn], in1=gtg[:, n * 8 + ob, :tn], op=ALU.mult),
                                     reads=[p_, gtg], writes=[m_])
                            else:
                                t_ = tmp.next()
                                S.op("dve", lambda: nc.vector.tensor_tensor(out=t_[:, :tn], in0=p_[:, :tn], in1=gtg[:, n * 8 + ob, :tn], op=ALU.mult),
                                     reads=[p_, gtg], writes=[t_])
                                if n < 3:
                                    S.op("pool", lambda: nc.gpsimd.tensor_tensor(out=m_[:, :tn], in0=m_[:, :tn], in1=t_[:, :tn], op=ALU.add),
                                         reads=[m_, t_], writes=[m_])
                                else:
                                    S.op("pool", lambda: nc.gpsimd.tensor_tensor(out=mg[:, ob, :tn], in0=m_[:, :tn], in1=t_[:, :tn], op=ALU.add),
                                         reads=[m_, t_], writes=[mg])
                    for ob in range(8):
                        p_ = pp.next()
                        for kc in range(8):
                            S.op("pe", lambda: nc.tensor.matmul(p_[:, :tn], lhsT=wo[:, kc, ob * 128:(ob + 1) * 128], rhs=mg[:, kc, :tn],
                                                                start=(kc == 0), stop=(kc == 7)), reads=[wo, mg], writes=[p_])
                        S.op("dve", lambda: nc.vector.scalar_tensor_tensor(out=x1g[:, ob, :tn], in0=p_[:, :tn], scalar=modT[:, 16 + ob, jcol:jcol + 1],
                                                                           in1=xg[:, ob, :tn], op0=ALU.mult, op1=ALU.add),
                             reads=[p_, modT, xg], writes=[x1g])
                    S.dma("sp", x1_d[:, t0:t0 + tn].rearrange("(c p) t -> p c t", p=128), x1g[:, :, :tn], reads=[x1g], writes=[x1_b], par=True)
                    emit_norm_mod(X, nc, S, x1g, tn, jcol, Gm2, _Shift(modT, 24), ones, epsc,
                                  sqr, pms_r, tmp, h2g)
                    S.dma("sp", h2_d[:, t0:t0 + tn].rearrange("(c p) t -> p c t", p=128), h2g[:, :, :tn], reads=[h2g], writes=[h2_b], par=True)
            S.barrier()
        with ExitStack() as es:
            wfi = X.sb(es, "wfi", [128, 8, 2 * FFN], BF16)
            wfo = X.sb(es, "wfo", [128, 22, D], BF16)
            for c in range(8):
                S.dma("pool", wfi[:, c, :], w_fi[c * 128:(c + 1) * 128, :], writes=[wfi], par=True)
            S.dma("pool", wfo[:], w_fo.rearrange("(c p) n -> p c n", p=128), writes=[wfo])
            TG = 256
            h2r = X.ring(es, "h2r", [128, 8, TG], BF16, 2)
            x1r = X.ring(es, "x1r", [128, 8, TG], F32, 2)
            x2r = X.ring(es, "x2r", [128, 8, TG], F32, 2)
            act = X.sb(es, "actT", [128, 22, TG], BF16)
            sgr = X.ring(es, "sgr", [128, TG], F32, 3)
            pg_r = X.ring(es, "pg", [128, 512], F32, 3, psum=True)
            pu_r = X.ring(es, "pu", [128, 512], F32, 3, psum=True)
            po_r = X.ring(es, "po4", [128, 512], F32, 2, psum=True)
            for v in vs:
                need_ctx = (l < DEPTH - 1 and v == 0)
                x1_d, h2_d = T.x1T_s[v], T.h2T_s[v]
                x1_b, h2_b = Buf(x1_d, "x1"), Buf(h2_d, "h2")
                ntok = NT if need_ctx else NLAT
                for t0 in range(0, ntok, TG):
                    jcol = 1 if t0 >= NLAT else 0
                    h2 = h2r.next()
                    x1 = x1r.next()
                    x2 = x2r.next()
                    S.dma("sp", h2[:], h2_d[:, t0:t0 + TG].rearrange("(c p) t -> p c t", p=128), reads=[h2_b], writes=[h2])
                    S.dma("sp", x1[:], x1_d[:, t0:t0 + TG].rearrange("(c p) t -> p c t", p=128), reads=[x1_b], writes=[x1])
                    for fb in range(22):
                        pg, pu = pg_r.next(), pu_r.next()
                        for kc in range(8):
                            S.op("pe", lambda: nc.tensor.matmul(pg[:, :TG], lhsT=wfi[:, kc, fb * 128:(fb + 1) * 128], rhs=h2[:, kc, :],
                                                                start=(kc == 0), stop=(kc == 7)), reads=[wfi, h2], writes=[pg])
                        for kc in range(8):
                            S.op("pe", lambda: nc.tensor.matmul(pu[:, :TG], lhsT=wfi[:, kc, FFN + fb * 128:FFN + (fb + 1) * 128], rhs=h2[:, kc, :],
                                                                start=(kc == 0), stop=(kc == 7)), reads=[wfi, h2], writes=[pu])
                        sg = sgr.next()
                        S.op("act", lambda: nc.scalar.activation(out=sg[:], in_=pg[:, :TG], func=AF.Silu), reads=[pg], writes=[sg])
                        S.op("dve", lambda: nc.vector.tensor_tensor(out=act[:, fb, :], in0=pu[:, :TG], in1=sg[:], op=ALU.mult),
                             reads=[pu, sg], writes=[act])
                    for ob in range(8):
                        po = po_r.next()
                        for fb in range(22):
                            S.op("pe", lambda: nc.tensor.matmul(po[:, :TG], lhsT=wfo[:, fb, ob * 128:(ob + 1) * 128], rhs=act[:, fb, :],
                                                                start=(fb == 0), stop=(fb == 21)), reads=[wfo, act], writes=[po])
                        S.op("dve", lambda: nc.vector.scalar_tensor_tensor(out=x2[:, ob, :], in0=po[:, :TG], scalar=modT[:, 40 + ob, jcol:jcol + 1],
                                                                           in1=x1[:, ob, :], op0=ALU.mult, op1=ALU.add),
                             reads=[po, modT, x1], writes=[x2])
                    S.dma("sp", xco(t0, TG).rearrange("(c p) t -> p c t", p=128), x2[:], reads=[x2])
        S.barrier()


def make_consts():
    c = np.zeros((128, NCONST), np.float32)
    c[:, C_ONES:C_ONES + 128] = 1.0
    for b in range(2):
        c[b * 64:(b + 1) * 64, C_BD64 + b * 64:C_BD64 + (b + 1) * 64] = 1.0
    c[0:64, C_BD96:C_BD96 + 64] = 1.0
    c[64:96, C_BD96 + 64:C_BD96 + 96] = 1.0
    for p in range(128):
        c[p, C_PSW + (p ^ 1)] = 1.0
        c[p, C_ID + p] = 1.0
    return c


def make_rope_tables(core):
    s = core * NLAT + np.arange(NLAT)
    row = (s // 64).astype(np.float64)
    col = (s % 64).astype(np.float64)

    def ang(dim):
        q = dim // 4
        f = 10000.0 ** (-(np.arange(q, dtype=np.float32) / np.float32(q))).astype(np.float32)
        f = f.astype(np.float32)
        a = np.concatenate([row[:, None].astype(np.float32) * f[None, :], col[:, None].astype(np.float32) * f[None, :]], -1)
        return a.astype(np.float32)
    out = np.zeros((6, 128, NT), np.float32)
    out[0::2, :, :] = 1.0
    a64 = ang(64)
    a32 = ang(32)
    sign = np.where(np.arange(128) % 2 == 0, -1.0, 1.0).astype(np.float32)
    p = np.arange(128)
    idx64 = (p % 64) // 2
    out[0, :, :NLAT] = np.cos(a64)[:, idx64].T
    out[1, :, :NLAT] = (np.sin(a64)[:, idx64] * sign[None, :]).T
    p32 = np.arange(32)
    idx32 = p32 // 2
    out[2, 64:96, :NLAT] = np.cos(a32)[:, idx32].T
    out[3, 64:96, :NLAT] = (np.sin(a32)[:, idx32] * sign[None, :32]).T
    out[4, 0:32, :NLAT] = np.cos(a32)[:, idx32].T
    out[5, 0:32, :NLAT] = (np.sin(a32)[:, idx32] * sign[None, :32]).T
    return out


def tile_col(v, n=128):
    v = np.asarray(v, np.float32).reshape(-1)
    reps = -(-n // v.size)
    return np.tile(v, reps)[:n] if v.size <= n and n % v.size == 0 else np.pad(v, (0, n - v.size))


def make_gvec(inp, l):
    g = np.zeros((128, NGV), np.float32)
    g[:, GV_AQ] = tile_col(inp["gqa_qk_gain"][l, 0])
    g[:, GV_AK] = tile_col(inp["gqa_qk_gain"][l, 1])
    g[:, GV_DQ] = tile_col(inp["diff_qk_gain"][l, 0])
    g[:, GV_DK] = tile_col(inp["diff_qk_gain"][l, 1])
    g[:, GV_CQ:GV_CQ + 3] = inp["mla_cq_gain"][l].reshape(3, 128).T
    g[:, GV_CKV:GV_CKV + 2] = inp["mla_ckv_gain"][l].reshape(2, 128).T
    g[:96, GV_MQ] = inp["mla_qk_gain"][l, 0]
    g[:, GV_MKN] = tile_col(inp["mla_qk_gain"][l, 1, :64])
    g[:32, GV_MKR] = inp["mla_qk_gain"][l, 1, 64:]
    g[:64, GV_SC96] = 1.0 / 64
    g[64:96, GV_SC96] = 1.0 / 32
    g[:, GV_EPS] = EPS
    g[:, GV_SUBLN] = inp["diff_subln_gain"][l]
    return g


class _Shift:
    def __init__(self, buf, base):
        self.buf = buf
        self.base = base
        self.lw = buf.lw
        self.rd = buf.rd

    def __getitem__(self, idx):
        p, c, j = idx
        return self.buf.t[p, self.base + c, j]


def make_rt():
    r = np.zeros((128, NRT), np.float32)
    j = np.arange(128)[:, None].astype(np.float32)
    i = np.arange(128)[None, :].astype(np.float32)
    r[:, RT_RELF:RT_RELF + 128] = np.maximum(i - j, 0)
    r[:, RT_MSKF:RT_MSKF + 128] = (i >= j)
    r[:, RT_RELB:RT_RELB + 128] = np.maximum(j - i, 0)
    r[:, RT_MSKB:RT_MSKB + 128] = (j >= i)
    r[:, RT_QDF:RT_QDF + 128] = i + 1.0
    r[:, RT_QDB:RT_QDB + 128] = 128.0 - i
    r[:, RT_KDF] = 127.0 - np.arange(128)
    r[:, RT_KDB] = np.arange(128)
    for k in range(64):
        r[64 + k, RT_SEL + k] = 1.0
    return r


def make_retE(core, rot=0):
    Eg = _make_retE_global(core)
    perm = [((rot + u // 16) % NCORES) * 16 + u % 16 for u in range(128)] + [128, 129]
    return np.ascontiguousarray(Eg[:, :, perm])


def _make_retE_global(core):
    E = np.zeros((128, 4, NKT), np.float32)
    j = np.arange(128).astype(np.float32)
    T0 = core * 16
    pos = np.zeros(NKT)
    pos[128], pos[129] = 0, 1
    pos[:128] = 2 + np.arange(128)
    P0 = 2 + T0
    for u in range(NKT):
        if pos[u] < P0:
            E[:, 0, u] = 127.0 - j + 128.0 * (P0 - 1 - pos[u])
            E[:, 1, u] = 1.0
    pos[129], pos[128] = 0, 1
    pos[:128] = 2 + 127 - np.arange(128)
    P0 = 2 + 127 - (T0 + 15)
    for u in range(NKT):
        if pos[u] < P0:
            E[:, 2, u] = j + 128.0 * (P0 - 1 - pos[u])
            E[:, 3, u] = 1.0
    return E


class _NS:
    pass


def build_fused():
    nc = bass.Bass("TRN2", target_bir_lowering=False)
    di = lambda name, shape, dt=F32: nc.dram_tensor(name, shape, dt, kind="ExternalInput").ap()
    scr = lambda name, shape, dt=BF16: nc.dram_tensor(name, shape, dt, kind="Internal").ap()
    T = _NS()
    xT_all = di("xT_all", [D, NKEY])
    cT = di("cT", [D, 2])
    w_mod = di("w_mod", [DEPTH, D, 6 * D])
    bmodT = di("bmodT", [DEPTH, 128, 48])
    T.g0T = di("g0T", [DEPTH, 128, 8])
    T.g1T = di("g1T", [DEPTH, 128, 8])
    T.w_in = di("w_in", [DEPTH, D, IN_W])
    T.w_uq = di("w_uq", [DEPTH, 384, 768])
    T.w_ukv = di("w_ukv", [DEPTH, 256, 1024])
    T.gvec = di("gvec", [DEPTH, 128, NGV])
    T.consts = di("consts", [128, NCONST])
    T.rope = di("rope", [NCORES, 6, 128, NT])
    T.w_branch = di("w_branch", [DEPTH, 2048, D])
    T.w_out = di("w_out", [DEPTH, D, D])
    T.w_fi = di("w_fi", [DEPTH, D, 2 * FFN])
    T.w_fo = di("w_fo", [DEPTH, FFN, D])
    T.rt = di("rt", [128, NRT])
    T.retE = di("retE", [NCORES, 128, 4, NKT])
    T.retd = di("retd", [DEPTH, 128, 8])
    T.dlam = di("dlam", [DEPTH, 128, 256])
    out = nc.dram_tensor("xTo", [D, NLAT], F32, kind="ExternalOutput").ap()
    x1_all = scr("x1_all", [D, NKEY], F32)
    T.xin = [xT_all, x1_all]
    T.xout = [x1_all, out]
    T.modT_s = [scr("modT_s%d" % l, [128, 96], F32) for l in range(DEPTH)]
    T.qT_s = [scr("qT_s%d" % v, [NQ, NT]) for v in range(NCORES)]
    T.kT_s = [scr("kT_s%d" % v, [NK, NT]) for v in range(NCORES)]
    T.v_s = [scr("v_s%d" % v, [NT, NV]) for v in range(NCORES)]
    T.gT_s = [scr("gT_s%d" % v, [NG, NT]) for v in range(NCORES)]
    T.hT_s = [scr("hT_s%d" % v, [D, NT]) for v in range(NCORES)]
    T.kTg = scr("kTg", [NK, NKEY])
    T.vg = scr("vg", [NKEY, NV])
    T.brT_s = [scr("brT_s%d" % v, [2048, NT]) for v in range(NCORES)]
    T.x1T_s = [scr("x1T_s%d" % v, [D, NT], F32) for v in range(NCORES)]
    T.h2T_s = [scr("h2T_s%d" % v, [D, NT]) for v in range(NCORES)]
    with ExitStack() as esr:
        X = Ctx(nc, esr)
        S = X.S
        for l in range(DEPTH):
            with ExitStack() as esm:
                dummy = X.sb(esm, "modkeep", [128, 1], F32)
                with ExitStack() as est:
                    emit_modvec(X, esm, est, cT, w_mod[l], bmodT[l], T.modT_s[l])
                    S.barrier()
            emit_A(X, T, l, list(range(NCORES)))
            for v in range(NCORES):
                S.dma("sp", T.kTg[:, v * NLAT:(v + 1) * NLAT], T.kT_s[v][:, 0:NLAT])
                S.dma("sp", T.vg[v * NLAT:(v + 1) * NLAT, :], T.v_s[v][0:NLAT, :])
                if v == 0:
                    S.dma("sp", T.kTg[:, 8 * NLAT:], T.kT_s[v][:, NLAT:])
                    S.dma("sp", T.vg[8 * NLAT:, :], T.v_s[v][NLAT:, :])
            S.barrier()
            emit_B(X, T, l, list(range(NCORES)) if l < DEPTH - 1 else [0])
        S.finish("sp")
        print("fused: instructions", S.nins, "waits", S.nwaits)
    return nc


def inputs_fused(inp):
    consts = make_consts()
    rtab = make_rt()
    x = inp["x"][0]
    ctx = inp["ctx"][0]
    c_cols = np.ascontiguousarray(np.stack([inp["c"][0], inp["c_ctx"]], 1))
    shared = {
        "cT": c_cols,
        "w_mod": np.ascontiguousarray(inp["w_mod"]),
        "bmodT": np.ascontiguousarray(inp["b_mod"].reshape(DEPTH, 48, 128).transpose(0, 2, 1)),
        "g0T": np.ascontiguousarray(inp["norm_gain"][:, 0].reshape(DEPTH, 8, 128).transpose(0, 2, 1)),
        "g1T": np.ascontiguousarray(inp["norm_gain"][:, 1].reshape(DEPTH, 8, 128).transpose(0, 2, 1)),
        "w_in": np.ascontiguousarray(inp["w_in"]),
        "w_uq": np.ascontiguousarray(inp["mla_w_uq"]),
        "w_ukv": np.ascontiguousarray(inp["mla_w_ukv"]),
        "gvec": np.stack([make_gvec(inp, l) for l in range(DEPTH)]),
        "consts": consts,
        "w_branch": np.ascontiguousarray(inp["w_branch"].reshape(DEPTH, 2048, D)),
        "w_out": np.ascontiguousarray(inp["w_out"]),
        "w_fi": np.ascontiguousarray(inp["w_ffn_in"]),
        "w_fo": np.ascontiguousarray(inp["w_ffn_out"]),
        "rt": rtab,
        "retd": np.ascontiguousarray(np.tile(inp["ret_decay"].reshape(DEPTH, 1, 8), (1, 128, 1))),
        "dlam": np.ascontiguousarray(np.tile(inp["diff_lambda"].reshape(DEPTH, 1, 256), (1, 128, 1))),
    }
    maps = []
    for r in range(NCORES):
        order = [(r + v) % NCORES for v in range(NCORES)]
        xrot = np.concatenate([x[g * NLAT:(g + 1) * NLAT] for g in order] + [ctx], 0)
        m = dict(shared)
        m["xT_all"] = np.ascontiguousarray(xrot.T)
        m["rope"] = np.stack([make_rope_tables(g) for g in order])
        m["retE"] = np.stack([make_retE(g, r) for g in order])
        maps.append(m)
    return maps


_PROG = {}


def kernel(**inp):
    inp = {k: np.asarray(v) for k, v in inp.items()}
    if "f" not in _PROG:
        _PROG["f"] = build_fused()
    res = run_bass_kernel_spmd(_PROG["f"], inputs_fused(inp), core_ids=list(range(NCORES)))
    out = np.concatenate([np.asarray(res.results[r]["xTo"]).T for r in range(NCORES)], 0)[None]
    return np.ascontiguousarray(out.astype(np.float32))
```

```python
import math
import numpy as np
import ml_dtypes
import concourse.bass as bass
import concourse.mybir as mybir
from concourse.bass_utils import run_bass_kernel_spmd
from contextlib import ExitStack

F32 = mybir.dt.float32
BF16 = mybir.dt.bfloat16
AF = mybir.ActivationFunctionType
ALU = mybir.AluOpType

NCORES = 8
D = 1024
SEQ = 16384
NLAT = SEQ // NCORES
NCTX = 256
NT = NLAT + NCTX
GROUPS = [(0, 512), (512, 512), (1024, 512), (1536, 512), (2048, 256)]
DEPTH = 2
FFN = 2816
EPS = 1e-6

O_AQ, O_AK, O_AV = 0, 512, 640
O_BQ, O_BK, O_BV, O_BG = 768, 1024, 1280, 1792
O_CQ, O_CKV, O_CKR = 2304, 2688, 2944
O_DQ, O_DK, O_DV = 2976, 3488, 4000
O_G = 4512
IN_W = 8608
QA, QB, QC, QD = 0, 512, 768, 1536
NQ = 2048
KA, KCN, KCR, KD, KB = 0, 128, 640, 672, 1184
NK = 1440
VA, VC, VD, VB, VBK = 0, 128, 640, 1152, 1664
NV = 1920
NG = 4608
GV_AQ, GV_AK, GV_DQ, GV_DK, GV_CQ, GV_CKV, GV_MQ, GV_MKN, GV_MKR, GV_SC96, GV_EPS, GV_SUBLN = 0, 1, 2, 3, 4, 7, 9, 10, 11, 12, 13, 14
NGV = 16
C_ONES, C_BD64, C_BD96, C_PSW, C_ID = 0, 128, 256, 384, 512
NCONST = 640


class Buf:
    __slots__ = ("t", "lw", "rd", "name")

    def __init__(self, t, name=""):
        self.t = t
        self.lw = {}
        self.rd = {}
        self.name = name

    def __getitem__(self, idx):
        return self.t[idx]


class Ring:
    def __init__(self, bufs):
        self.bufs = bufs
        self.i = 0

    def next(self):
        b = self.bufs[self.i % len(self.bufs)]
        self.i += 1
        return b


class Sync:
    def __init__(self, nc, es, n_dma_sems=24):
        self.nc = nc
        self.engs = {"pe": nc.tensor, "act": nc.scalar, "dve": nc.vector,
                     "pool": nc.gpsimd, "sp": nc.sync}
        self.sems = {}
        self.cnt = {}
        for k in self.engs:
            self.sems[k] = es.enter_context(nc.semaphore("s_" + k))
            self.cnt[k] = 0
        self.dsems = []
        for i in range(n_dma_sems):
            key = "d%d" % i
            self.sems[key] = es.enter_context(nc.semaphore("s_" + key))
            self.cnt[key] = 0
            self.dsems.append(key)
        self.dnext = 0
        self.seen = {k: {} for k in self.engs}
        self.nwaits = 0
        self.nins = 0

    def _wait(self, e, tok):
        if tok is None:
            return
        key, val = tok
        if key == e:
            return
        if self.seen[e].get(key, 0) >= val:
            return
        self.engs[e].wait_ge(self.sems[key], val)
        self.seen[e][key] = val
        self.nwaits += 1

    def _deps(self, e, reads, writes, par=False):
        for b in reads:
            for k, v in b.lw.items():
                self._wait(e, (k, v))
        for b in writes:
            for k, v in b.lw.items():
                if par and k[0] == "d":
                    continue
                self._wait(e, (k, v))
            for k, v in b.rd.items():
                self._wait(e, (k, v))

    def _mark(self, tok, reads, writes, par=False):
        for b in reads:
            b.rd[tok[0]] = tok[1]
        for b in writes:
            if par:
                b.lw[tok[0]] = tok[1]
            else:
                b.lw = {tok[0]: tok[1]}
                b.rd = {}

    def op(self, e, fn, reads=(), writes=()):
        self._deps(e, reads, writes)
        ins = fn()
        self.cnt[e] += 1
        ins.then_inc(self.sems[e], 1)
        tok = (e, self.cnt[e])
        self._mark(tok, reads, writes)
        self.nins += 1
        return tok

    def dma(self, q, out, in_, reads=(), writes=(), par=False, **kw):
        key = self.dsems[self.dnext]
        self.dnext = (self.dnext + 1) % len(self.dsems)
        self._wait(q, (key, self.cnt[key]))
        self._deps(q, reads, writes, par)
        ins = self.engs[q].dma_start(out=out, in_=in_, **kw)
        self.cnt[key] += 16
        ins.then_inc(self.sems[key], 16)
        tok = (key, self.cnt[key])
        self._mark(tok, reads, writes, par)
        self.nins += 1
        return tok

    def barrier(self):
        for e in self.engs:
            for k, v in self.cnt.items():
                if k != e and v > 0:
                    self._wait(e, (k, v))

    def finish(self, e="sp"):
        for k, v in self.cnt.items():
            if k != e and v > 0:
                self._wait(e, (k, v))


class Ctx:
    def __init__(self, nc, es):
        self.nc = nc
        self.es = es
        self.S = Sync(nc, es)
        self.uid = 0

    def sb(self, es, name, shape, dt):
        self.uid += 1
        return Buf(es.enter_context(self.nc.sbuf_tensor("sb%d_%s" % (self.uid, name), shape, dt)), name)

    def ps(self, es, name, shape, dt=F32):
        self.uid += 1
        return Buf(es.enter_context(self.nc.psum_tensor("ps%d_%s" % (self.uid, name), shape, dt)), name)

    def ring(self, es, name, shape, dt, n, psum=False):
        mk = self.ps if psum else self.sb
        return Ring([mk(es, "%s%d" % (name, i), shape, dt) for i in range(n)])


def emit_modvec(X, esp, es, cT, w_mod, bmodT, modT_d):
    nc, S = X.nc, X.S
    modT = X.sb(esp, "modT_sb", [128, 48, 2], F32)
    csb = X.sb(es, "csb", [128, 8, 2], F32)
    sg = X.sb(es, "csg", [128, 8, 2], F32)
    bm = X.sb(es, "bmod", [128, 48], F32)
    pm = X.ps(es, "pmod", [128, 48, 2], F32)
    wr = X.ring(es, "wmod", [128, 8, 512], F32, 2)
    S.dma("sp", csb[:], cT.rearrange("(c p) j -> p c j", p=128), writes=[csb])
    S.dma("sp", bm[:], bmodT, writes=[bm])
    S.op("act", lambda: nc.scalar.activation(out=sg[:], in_=csb[:], func=AF.Sigmoid), reads=[csb], writes=[sg])
    S.op("dve", lambda: nc.vector.tensor_tensor(out=sg[:], in0=sg[:], in1=csb[:], op=ALU.mult),
         reads=[sg, csb], writes=[sg])
    for ch in range(12):
        w = wr.next()
        S.dma("sp", w[:], w_mod[:, ch * 512:(ch + 1) * 512].rearrange("(c p) n -> p c n", p=128), writes=[w])
        for nb in range(4):
            for kc in range(8):
                S.op("pe", lambda: nc.tensor.matmul(pm[:, ch * 4 + nb, :], lhsT=w[:, kc, nb * 128:(nb + 1) * 128],
                                                    rhs=sg[:, kc, :], start=(kc == 0), stop=(kc == 7)),
                     reads=[w, sg], writes=[pm])
    for j in range(2):
        S.op("dve", lambda: nc.vector.tensor_tensor(out=modT[:, :, j], in0=pm[:, :, j], in1=bm[:], op=ALU.add),
             reads=[pm, bm], writes=[modT])
    S.dma("sp", modT_d.rearrange("p (a j) -> p a j", j=2), modT[:], reads=[modT])
    return modT


def emit_norm_mod(X, nc, S, xg, tn, jcol, Gm, Sft, ones, epsc, sqr, pms_r, tmp_r, hT_out):
    pms = pms_r.next()
    for c in range(8):
        sq = sqr.next()
        S.op("act", lambda: nc.scalar.activation(out=sq[:, :tn], in_=xg[:, c, :tn], func=AF.Square),
             reads=[xg], writes=[sq])
        S.op("pe", lambda: nc.tensor.matmul(pms[:, :tn], lhsT=ones, rhs=sq[:, :tn], start=(c == 0), stop=(c == 7)),
             reads=[sq], writes=[pms])
    rstd = getattr(tmp_r, "rstd", None) or tmp_r.next()
    S.op("act", lambda: nc.scalar.activation(out=rstd[:, :tn], in_=pms[:, :tn], func=AF.Ln, scale=1.0 / D, bias=epsc),
         reads=[pms], writes=[rstd])
    S.op("act", lambda: nc.scalar.activation(out=rstd[:, :tn], in_=rstd[:, :tn], func=AF.Exp, scale=-0.5),
         reads=[rstd], writes=[rstd])
    for c in range(8):
        t = tmp_r.next()
        S.op("dve", lambda: nc.vector.scalar_tensor_tensor(out=t[:, :tn], in0=xg[:, c, :tn], scalar=Gm[:, c, jcol:jcol + 1],
                                                           in1=rstd[:, :tn], op0=ALU.mult, op1=ALU.mult),
             reads=[xg, Gm, rstd], writes=[t])
        S.op("act", lambda: nc.scalar.activation(out=hT_out[:, c, :tn], in_=t[:, :tn], func=AF.Identity,
                                                 bias=Sft[:, c, jcol:jcol + 1]),
             reads=[t, Sft], writes=[hT_out])


def emit_A(X, T, l, vs):
    nc, S = X.nc, X.S
    xin = T.xin[l]
    v = vs[0]

    def xc(t0, tn):
        c0 = (XCTX + t0 - NLAT) if t0 >= NLAT else (v * NLAT + t0)
        return xin[:, c0:c0 + tn]
    w_in, w_uq, w_ukv, g0T, gvec, consts = T.w_in[l], T.w_uq[l], T.w_ukv[l], T.g0T[l], T.gvec[l], T.consts
    modT_i = T.modT_s[l]
    with ExitStack() as es0:
        S.barrier()
        cst = X.sb(es0, "cst", [128, NCONST], F32)
        gv = X.sb(es0, "gv", [128, NGV], F32)
        S.dma("sp", cst[:], consts, writes=[cst])
        S.dma("sp", gv[:], gvec, writes=[gv])
        identb = X.sb(es0, "identb", [128, 128], BF16)
        S.op("dve", lambda: nc.vector.tensor_copy(out=identb[:], in_=cst[:, C_ID:C_ID + 128]), reads=[cst], writes=[identb])
        ones = cst[:, C_ONES:C_ONES + 128]
        epsc = gv[:, GV_EPS:GV_EPS + 1]
        modT = X.sb(es0, "modTa", [128, 48, 2], F32)
        S.dma("sp", modT[:], modT_i.rearrange("p (a j) -> p a j", j=2), writes=[modT])
        g0 = X.sb(es0, "g0", [128, 8], F32)
        S.dma("sp", g0[:], g0T, writes=[g0])
        Gm = X.sb(es0, "Gm", [128, 8, 2], F32)
        for j in range(2):
            S.op("dve", lambda: nc.vector.scalar_tensor_tensor(out=Gm[:, :, j], in0=modT[:, 8:16, j], scalar=1.0,
                                                               in1=g0[:], op0=ALU.add, op1=ALU.mult),
                 reads=[modT, g0], writes=[Gm])
        Sft = modT
        S.barrier()

        with ExitStack() as es:
            NW1 = O_G
            w1 = X.sb(es, "w1", [128, 8, NW1], BF16)
            for c in range(8):
                S.dma("pool", w1[:, c, :], w_in[c * 128:(c + 1) * 128, 0:NW1], writes=[w1], par=True)
            wuq = X.sb(es, "wuq", [128, 3, 768], BF16)
            S.dma("pool", wuq[:], w_uq.rearrange("(c p) n -> p c n", p=128), writes=[wuq])
            wukv = X.sb(es, "wukv", [128, 2, 2, 8, 64], BF16)
            for c in range(2):
                for t in range(2):
                    S.dma("pool", wukv[:, c, t], w_ukv[c * 128:(c + 1) * 128, :].rearrange("p (h t e) -> p t h e", h=8, t=2)[:, t],
                          writes=[wukv], par=True)
            xg = X.sb(es, "xg", [128, 8, 512], F32)
            hTg = X.sb(es, "hTg", [128, 8, 512], BF16)
            tab = X.sb(es, "tab", [128, 6, 512], F32)
            sqr = X.ring(es, "sq", [128, 512], F32, 2)
            tmp = X.ring(es, "tmp", [128, 512], F32, 8)
            tmp.rstd = X.sb(es, "rstd1", [128, 512], F32)
            xsr = X.ring(es, "xs", [128, 512], F32, 4)
            outr = X.ring(es, "ob", [128, 512], BF16, 4)
            cqn = X.sb(es, "cqn", [128, 3, 512], BF16)
            ckvn = X.sb(es, "ckvn", [128, 2, 512], BF16)
            pmm = X.ring(es, "pmm", [128, 512], F32, 3, psum=True)
            pms_r = X.ring(es, "pms", [128, 512], F32, 2, psum=True)
            prx = X.ring(es, "prx", [128, 512], F32, 2, psum=True)
            ptr = X.ps(es, "ptr", [128, 4, 128], BF16)

            def mm_fm(pm, M, tn, lhs_fn, rhs_fn, nk, rd):
                for c in range(nk):
                    S.op("pe", lambda: nc.tensor.matmul(pm[0:M, :tn], lhsT=lhs_fn(c), rhs=rhs_fn(c),
                                                        start=(c == 0), stop=(c == nk - 1)),
                         reads=rd, writes=[pm])

            def rstd_from(xsq_list, M, tn, bd, scale):
                pst = pms_r.next()
                n = len(xsq_list)
                for i, sq in enumerate(xsq_list):
                    S.op("pe", lambda: nc.tensor.matmul(pst[0:M, :tn], lhsT=bd, rhs=sq[0:M, :tn],
                                                        start=(i == 0), stop=(i == n - 1)),
                         reads=[sq, cst], writes=[pst])
                r = tmp.next()
                S.op("act", lambda: nc.scalar.activation(out=r[0:M, :tn], in_=pst[0:M, :tn], func=AF.Ln, scale=scale,
                                                         bias=epsc[0:M, :]),
                     reads=[pst, gv], writes=[r])
                S.op("act", lambda: nc.scalar.activation(out=r[0:M, :tn], in_=r[0:M, :tn], func=AF.Exp, scale=-0.5),
                     reads=[r], writes=[r])
                return r

            def rope_store(xn, M, tn, ti, dst_ap, dst_buf=None):
                pr = prx.next()
                S.op("pe", lambda: nc.tensor.matmul(pr[0:M, :tn], lhsT=cst[0:M, C_PSW:C_PSW + M], rhs=xn[0:M, :tn],
                                                    start=True, stop=True), reads=[xn, cst], writes=[pr])
                t1 = tmp.next()
                S.op("pool", lambda: nc.gpsimd.tensor_tensor(out=t1[0:M, :tn], in0=xn[0:M, :tn], in1=tab[0:M, 2 * ti, :tn],
                                                             op=ALU.mult), reads=[xn, tab], writes=[t1])
                t2 = tmp.next()
                S.op("dve", lambda: nc.vector.tensor_tensor(out=t2[0:M, :tn], in0=pr[0:M, :tn], in1=tab[0:M, 2 * ti + 1, :tn],
                                                            op=ALU.mult), reads=[pr, tab], writes=[t2])
                ob = outr.next()
                S.op("dve", lambda: nc.vector.tensor_tensor(out=ob[0:M, :tn], in0=t1[0:M, :tn], in1=t2[0:M, :tn], op=ALU.add),
                     reads=[t1, t2], writes=[ob])
                S.dma("sp", dst_ap, ob[0:M, :tn], reads=[ob])
                return ob

            def job_norm(pm, M, tn, bd, scale, gcol, ti, dst_ap):
                xs = xsr.next()
                S.op("act", lambda: nc.scalar.copy(out=xs[0:M, :tn], in_=pm[0:M, :tn]), reads=[pm], writes=[xs])
                sq = sqr.next()
                S.op("act", lambda: nc.scalar.activation(out=sq[0:M, :tn], in_=pm[0:M, :tn], func=AF.Square),
                     reads=[pm], writes=[sq])
                r = rstd_from([sq], M, tn, bd, scale)
                if ti is None:
                    ob = outr.next()
                    S.op("dve", lambda: nc.vector.scalar_tensor_tensor(out=ob[0:M, :tn], in0=xs[0:M, :tn],
                                                                       scalar=gv[0:M, gcol:gcol + 1], in1=r[0:M, :tn],
                                                                       op0=ALU.mult, op1=ALU.mult),
                         reads=[xs, gv, r], writes=[ob])
                    S.dma("sp", dst_ap, ob[0:M, :tn], reads=[ob])
                    return ob
                xn = tmp.next()
                S.op("dve", lambda: nc.vector.scalar_tensor_tensor(out=xn[0:M, :tn], in0=xs[0:M, :tn],
                                                                   scalar=gv[0:M, gcol:gcol + 1], in1=r[0:M, :tn],
                                                                   op0=ALU.mult, op1=ALU.mult),
                     reads=[xs, gv, r], writes=[xn])
                return rope_store(xn, M, tn, ti, dst_ap)

            bd64 = cst[:, C_BD64:C_BD64 + 128]
            for v in vs:
                kv_only = (l == DEPTH - 1 and v > 0)
                groups = GROUPS if v == 0 else GROUPS[:4]
                rope = T.rope[v]
                qT_d, kT_d, v_d, gT_d = T.qT_s[v], T.kT_s[v], T.v_s[v], T.gT_s[v]
                hT_d = T.hT_s[v]
                hT_db = Buf(hT_d, "hT_scr")
                for gi, (t0, tn) in enumerate(groups):
                    jcol = 1 if gi == 4 else 0
                    S.dma("sp", xg[:, :, :tn], xc(t0, tn).rearrange("(c p) t -> p c t", p=128), writes=[xg])
                    S.dma("sp", tab[:, :, :tn], rope[:, :, t0:t0 + tn].rearrange("s p t -> p s t"), writes=[tab])
                    emit_norm_mod(X, nc, S, xg, tn, jcol, Gm, Sft, ones, epsc, sqr, pms_r, tmp, hTg)
                    S.dma("sp", hT_d[:, t0:t0 + tn].rearrange("(c p) t -> p c t", p=128), hTg[:, :, :tn], reads=[hTg],
                          writes=[hT_db], par=True)
                    hrhs = lambda c: hTg[:, c, :tn]
                    fam = [(O_AQ, 4, GV_AQ, qT_d, QA), (O_AK, 1, GV_AK, kT_d, KA),
                           (O_DQ, 4, GV_DQ, qT_d, QD), (O_DK, 4, GV_DK, kT_d, KD)]
                    if kv_only:
                        fam = [f_ for f_ in fam if f_[3] is kT_d]
                    for (co, nb, gcol, dst, r0) in fam:
                        for b in range(nb):
                            pm = pmm.next()
                            mm_fm(pm, 128, tn, lambda c: w1[:, c, co + b * 128:co + (b + 1) * 128], hrhs, 8, [w1, hTg])
                            job_norm(pm, 128, tn, bd64, 1.0 / 64, gcol, 0, dst[r0 + b * 128:r0 + (b + 1) * 128, t0:t0 + tn])
                    for (co, nb, scl, dst, r0, tokmaj) in ([] if kv_only else [(O_BQ, 2, 1.0, qT_d, QB, False)]) + [(O_BK, 2, 0.125, kT_d, KB, True)]:
                        for b in range(nb):
                            pm = pmm.next()
                            mm_fm(pm, 128, tn, lambda c: w1[:, c, co + b * 128:co + (b + 1) * 128], hrhs, 8, [w1, hTg])
                            xn = tmp.next()
                            S.op("act", lambda: nc.scalar.mul(out=xn[:, :tn], in_=pm[:, :tn], mul=scl),
                                 reads=[pm], writes=[xn])
                            ob = rope_store(xn, 128, tn, 0, dst[r0 + b * 128:r0 + (b + 1) * 128, t0:t0 + tn])
                            if tokmaj:
                                nsub = tn // 128
                                for s in range(nsub):
                                    S.op("pe", lambda: nc.tensor.transpose(out=ptr[:, s, :], in_=ob[:, s * 128:(s + 1) * 128],
                                                                           identity=identb[:]),
                                         reads=[ob, identb], writes=[ptr])
                                kb = outr.next()
                                S.op("dve", lambda: nc.vector.tensor_copy(out=kb[:, :tn], in_=ptr[:, 0:nsub, :].rearrange("p s e -> p (s e)")),
                                     reads=[ptr], writes=[kb])
                                S.dma("sp", v_d[t0:t0 + tn, VBK + b * 128:VBK + (b + 1) * 128].rearrange("(s p) e -> p s e", p=128),
                                      kb[:, :tn].rearrange("p (s e) -> p s e", e=128), reads=[kb])
                    for b in range(0 if kv_only else 4):
                        pm = pmm.next()
                        mm_fm(pm, 128, tn, lambda c: w1[:, c, O_BG + b * 128:O_BG + (b + 1) * 128], hrhs, 8, [w1, hTg])
                        ob = outr.next()
                        S.op("act", lambda: nc.scalar.activation(out=ob[:, :tn], in_=pm[:, :tn], func=AF.Silu),
                             reads=[pm], writes=[ob])
                        S.dma("sp", gT_d[b * 128:(b + 1) * 128, t0:t0 + tn], ob[:, :tn], reads=[ob])
                    for (co, nb, gc0, dstt, nfeat) in ([] if kv_only else [(O_CQ, 3, GV_CQ, cqn, 384.0)]) + [(O_CKV, 2, GV_CKV, ckvn, 256.0)]:
                        xss, sqs = [], []
                        for b in range(nb):
                            pm = pmm.next()
                            mm_fm(pm, 128, tn, lambda c: w1[:, c, co + b * 128:co + (b + 1) * 128], hrhs, 8, [w1, hTg])
                            xs = xsr.next()
                            S.op("act", lambda: nc.scalar.copy(out=xs[:, :tn], in_=pm[:, :tn]), reads=[pm], writes=[xs])
                            sq = tmp.next()
                            S.op("act", lambda: nc.scalar.activation(out=sq[:, :tn], in_=pm[:, :tn], func=AF.Square),
                                 reads=[pm], writes=[sq])
                            xss.append(xs)
                            sqs.append(sq)
                        r = rstd_from(sqs, 128, tn, ones, 1.0 / nfeat)
                        for b in range(nb):
                            S.op("dve", lambda: nc.vector.scalar_tensor_tensor(out=dstt[:, b, :tn], in0=xss[b][:, :tn],
                                                                               scalar=gv[:, gc0 + b:gc0 + b + 1], in1=r[:, :tn],
                                                                               op0=ALU.mult, op1=ALU.mult),
                                 reads=[xss[b], gv, r], writes=[dstt])
                    pm = pmm.next()
                    mm_fm(pm, 32, tn, lambda c: w1[:, c, O_CKR:O_CKR + 32], hrhs, 8, [w1, hTg])
                    job_norm(pm, 32, tn, cst[0:32, C_BD64:C_BD64 + 32], 1.0 / 32, GV_MKR, 2, kT_d[KCR:KCR + 32, t0:t0 + tn])
                    for h in range(0 if kv_only else 8):
                        pm = pmm.next()
                        mm_fm(pm, 96, tn, lambda c: wuq[:, c, h * 96:(h + 1) * 96], lambda c: cqn[:, c, :tn], 3, [wuq, cqn])
                        job_norm(pm, 96, tn, cst[0:96, C_BD96:C_BD96 + 96], gv[0:96, GV_SC96:GV_SC96 + 1], GV_MQ, 1,
                                 qT_d[QC + h * 96:QC + (h + 1) * 96, t0:t0 + tn])
                    for hp in range(4):
                        pm = pmm.next()
                        mm_fm(pm, 128, tn, lambda c: wukv[:, c, 0, 2 * hp:2 * hp + 2, :].rearrange("p h e -> p (h e)"),
                              lambda c: ckvn[:, c, :tn], 2, [wukv, ckvn])
                        job_norm(pm, 128, tn, bd64, 1.0 / 64, GV_MKN, None, kT_d[KCN + hp * 128:KCN + (hp + 1) * 128, t0:t0 + tn])
                    for s in range(tn // 128):
                        ts_ = slice(s * 128, (s + 1) * 128)
                        vjobs = [(lambda c: hTg[:, c, ts_], lambda c: w1[:, c, O_AV:O_AV + 128], 8, 128, VA, [hTg, w1]),
                                 (lambda c: hTg[:, c, ts_], lambda c: w1[:, c, O_BV:O_BV + 512], 8, 512, VB, [hTg, w1]),
                                 (lambda c: hTg[:, c, ts_], lambda c: w1[:, c, O_DV:O_DV + 512], 8, 512, VD, [hTg, w1]),
                                 (lambda c: ckvn[:, c, ts_], lambda c: wukv[:, c, 1].rearrange("p h e -> p (h e)"), 2, 512, VC,
                                  [ckvn, wukv])]
                        for (lf, rf, nk, ncol, vo, rd) in vjobs:
                            pm = pmm.next()
                            for c in range(nk):
                                S.op("pe", lambda: nc.tensor.matmul(pm[:, :ncol], lhsT=lf(c), rhs=rf(c), start=(c == 0),
                                                                    stop=(c == nk - 1)), reads=rd, writes=[pm])
                            ob = outr.next()
                            S.op("act", lambda: nc.scalar.copy(out=ob[:, :ncol], in_=pm[:, :ncol]), reads=[pm], writes=[ob])
                            S.dma("sp", v_d[t0 + s * 128:t0 + (s + 1) * 128, vo:vo + ncol], ob[:, :ncol], reads=[ob])
            S.barrier()
        with ExitStack() as es:
          if True:
              w2 = X.sb(es, "w2", [128, 8, 4096], BF16)
              for c in range(8):
                  S.dma("pool", w2[:, c, :], w_in[c * 128:(c + 1) * 128, O_G:O_G + 4096], writes=[w2], par=True)
              hr = X.ring(es, "hT2", [128, 8, 512], BF16, 2)
              outr = X.ring(es, "ob2", [128, 512], BF16, 4)
              pmm = X.ring(es, "pmm2", [128, 512], F32, 4, psum=True)
              for v in vs:
                  if l == DEPTH - 1 and v > 0:
                      continue
                  groups = GROUPS if v == 0 else GROUPS[:4]
                  gT_d = T.gT_s[v]
                  hT_d = T.hT_s[v]
                  hT_db = Buf(hT_d, "hT_scr")
                  for gi, (t0, tn) in enumerate(groups):
                      hTg = hr.next()
                      S.dma("sp", hTg[:, :, :tn], hT_d[:, t0:t0 + tn].rearrange("(c p) t -> p c t", p=128), reads=[hT_db],
                            writes=[hTg])
                      for b in range(32):
                          pm = pmm.next()
                          for c in range(8):
                              S.op("pe", lambda: nc.tensor.matmul(pm[:, :tn], lhsT=w2[:, c, b * 128:(b + 1) * 128], rhs=hTg[:, c, :tn],
                                                                  start=(c == 0), stop=(c == 7)), reads=[w2, hTg], writes=[pm])
                          ob = outr.next()
                          S.op("act", lambda: nc.scalar.activation(out=ob[:, :tn], in_=pm[:, :tn], func=AF.Sigmoid),
                               reads=[pm], writes=[ob])
                          S.dma("sp", gT_d[512 + b * 128:512 + (b + 1) * 128, t0:t0 + tn], ob[:, :tn], reads=[ob])
        S.barrier()


XCTX = NLAT
NKT = 130
NKEY = NKT * 128
BR_A, BR_B, BR_C, BR_D = 0, 512, 1024, 1536
RT_RELF, RT_MSKF, RT_RELB, RT_MSKB, RT_QDF, RT_QDB, RT_KDF, RT_KDB, RT_SEL = 0, 128, 256, 384, 512, 640, 768, 769, 770
NRT = 770 + 64
AX = mybir.AxisListType


def emit_B(X, T, l, vs):
    nc, S = X.nc, X.S
    layer = l
    lam_init = 0.8 - 0.6 * math.exp(-0.3 * layer)
    xin = T.xin[l]
    xout = T.xout[l]
    v = vs[0]

    def xci(t0, tn):
        c0 = (XCTX + t0 - NLAT) if t0 >= NLAT else (v * NLAT + t0)
        return xin[:, c0:c0 + tn]

    def xco(t0, tn):
        if l == DEPTH - 1:
            return xout[:, t0:t0 + tn]
        c0 = (XCTX + t0 - NLAT) if t0 >= NLAT else (v * NLAT + t0)
        return xout[:, c0:c0 + tn]
    modT_i, g1T = T.modT_s[l], T.g1T[l]
    w_branch, w_out, w_fi, w_fo = T.w_branch[l], T.w_out[l], T.w_fi[l], T.w_fo[l]
    gvec, consts, rt_i, retd_i, dlam_i = T.gvec[l], T.consts, T.rt, T.retd[l], T.dlam[l]
    with ExitStack() as es0:
        S.barrier()
        cst = X.sb(es0, "cst", [128, NCONST], F32)
        gv = X.sb(es0, "gv", [128, NGV], F32)
        modT = X.sb(es0, "modTs", [128, 48, 2], F32)
        rt = X.sb(es0, "rt", [128, NRT], F32)
        g1 = X.sb(es0, "g1", [128, 8], F32)
        Gm2 = X.sb(es0, "Gm2", [128, 8, 2], F32)
        small = X.sb(es0, "small", [128, 64], F32)
        onesb = X.sb(es0, "onesb", [128, 128], BF16)
        S.dma("sp", cst[:], consts, writes=[cst])
        S.dma("sp", gv[:], gvec, writes=[gv])
        S.dma("sp", modT[:], modT_i.rearrange("p (a j) -> p a j", j=2), writes=[modT])
        S.dma("sp", rt[:], rt_i, writes=[rt])
        S.dma("sp", g1[:], g1T, writes=[g1])
        S.barrier()
        S.op("dve", lambda: nc.vector.tensor_copy(out=onesb[:], in_=cst[:, C_ONES:C_ONES + 128]), reads=[cst], writes=[onesb])
        S.op("pool", lambda: nc.gpsimd.memset(small[:], 0.0), writes=[small])
        for j in range(2):
            S.op("dve", lambda: nc.vector.scalar_tensor_tensor(out=Gm2[:, :, j], in0=modT[:, 32:40, j], scalar=1.0,
                                                               in1=g1[:], op0=ALU.add, op1=ALU.mult),
                 reads=[modT, g1], writes=[Gm2])
        ones = cst[:, C_ONES:C_ONES + 128]
        one_col = cst[:, C_ONES:C_ONES + 1]
        epsc = gv[:, GV_EPS:GV_EPS + 1]
        sel = rt[:, RT_SEL:RT_SEL + 64]
        with ExitStack() as es:
            dl = X.sb(es, "dl", [128, 4, 64], F32)
            rd = X.sb(es, "rd", [128, 8], F32)
            pr = X.sb(es, "prd", [128, 2, 64], F32)
            S.dma("sp", dl[:], dlam_i.rearrange("p (a e) -> p a e", e=64), writes=[dl])
            S.dma("sp", rd[:], retd_i, writes=[rd])
            S.op("dve", lambda: nc.vector.tensor_tensor(out=pr[:, 0, :], in0=dl[:, 0, :], in1=dl[:, 1, :], op=ALU.mult),
                 reads=[dl], writes=[pr])
            S.op("dve", lambda: nc.vector.tensor_tensor(out=pr[:, 1, :], in0=dl[:, 2, :], in1=dl[:, 3, :], op=ALU.mult),
                 reads=[dl], writes=[pr])
            for m in range(2):
                S.op("act", lambda: nc.scalar.activation(out=dl[:, m, :], in_=pr[:, m, :], func=AF.Identity,
                                                         accum_out=small[:, 2 + m:3 + m]), reads=[pr], writes=[small, dl])
            S.op("act", lambda: nc.scalar.activation(out=small[:, 2:4], in_=small[:, 2:4], func=AF.Exp), reads=[small], writes=[small])
            S.op("dve", lambda: nc.vector.tensor_tensor(out=small[:, 0:1], in0=small[:, 3:4], in1=small[:, 2:3], op=ALU.subtract),
                 reads=[small], writes=[small])
            S.op("dve", lambda: nc.vector.tensor_scalar(out=small[:, 0:1], in0=small[:, 0:1], scalar1=-lam_init, scalar2=None,
                                                        op0=ALU.add), reads=[small], writes=[small])
            S.op("dve", lambda: nc.vector.tensor_scalar(out=small[:, 1:2], in0=gv[:, GV_SUBLN:GV_SUBLN + 1],
                                                        scalar1=1.0 - lam_init, scalar2=None, op0=ALU.mult),
                 reads=[gv], writes=[small])
            S.op("act", lambda: nc.scalar.activation(out=small[:, 8:16], in_=rd[:], func=AF.Exp, scale=-1.0), reads=[rd], writes=[small])
            S.op("act", lambda: nc.scalar.activation(out=small[:, 8:16], in_=small[:, 8:16], func=AF.Ln, bias=one_col),
                 reads=[small], writes=[small])
            S.op("dve", lambda: nc.vector.tensor_scalar(out=small[:, 8:16], in0=small[:, 8:16], scalar1=-1.0, scalar2=None,
                                                        op0=ALU.mult), reads=[small], writes=[small])
            S.op("act", lambda: nc.scalar.activation(out=small[:, 16:24], in_=small[:, 8:16], func=AF.Exp, scale=128.0),
                 reads=[small], writes=[small])
            for dr in range(2):
                for h in range(4):
                    cc = dr * 4 + h
                    S.op("act", lambda: nc.scalar.activation(out=small[:, 24 + cc:25 + cc], in_=rt[:, RT_KDF + dr:RT_KDF + dr + 1],
                                                             func=AF.Exp, scale=small[:, 8 + cc:9 + cc]),
                         reads=[small, rt], writes=[small])
            S.barrier()
        neg_lam = small[:, 0:1]
        gsub = small[:, 1:2]

        with ExitStack() as es:
            pst = X.ring(es, "pst", [128, 2, 512], F32, 2, psum=True)
            pacc = X.ring(es, "pacc", [128, 512], F32, 4, psum=True)
            kbuf = X.ring(es, "kTu", [128, NKEY], BF16, 2)
            vbuf = X.ring(es, "vau", [128, NKT, 128], BF16, 2)
            qr = X.ring(es, "qh", [128, NT], BF16, 3)
            ptr_ = X.ring(es, "pt", [128, 2, 512], BF16, 3)
            fin = X.ring(es, "fin", [128, 512], F32, 6)
            obr = X.ring(es, "obf", [128, 512], BF16, 3)
            for vb_ in vbuf.bufs:
                S.op("pool", lambda: nc.gpsimd.memset(vb_[:, :, 64:128], 1.0), writes=[vb_])

            def pair_run(steps, tn, scale, rd_qk):
                n = len(steps)
                LA = 2
                pts = [None] * n
                for i in range(n + LA):
                    if i < n:
                        qk = steps[i][0]
                        ps_ = pst.next()
                        for slot, (lh, rh) in enumerate(qk):
                            S.op("pe", lambda: nc.tensor.matmul(ps_[:, slot, :tn], lhsT=lh, rhs=rh, start=True, stop=True),
                                 reads=rd_qk, writes=[ps_])
                        pt = ptr_.next()
                        ns = len(qk)
                        S.op("act", lambda: nc.scalar.activation(out=pt[:, 0:ns, :tn], in_=ps_[:, 0:ns, :tn], func=AF.Exp, scale=scale),
                             reads=[ps_], writes=[pt])
                        pts[i] = pt
                    if i >= LA:
                        j = i - LA
                        for (acc, lf, slot, st_, sp_, rdl) in steps[j][1]:
                            S.op("pe", lambda: nc.tensor.matmul(acc[:, :tn], lhsT=lf, rhs=pts[j][:, slot, :tn], start=st_, stop=sp_),
                                 reads=[pts[j]] + rdl, writes=[acc])

            def finalize_AC(acc, tn, dst_ap):
                a = fin.next()
                S.op("dve", lambda: nc.vector.tensor_copy(out=a[:, :tn], in_=acc[:, :tn]), reads=[acc], writes=[a])
                pn = pst.next()
                S.op("pe", lambda: nc.tensor.matmul(pn[0:64, 0, :tn], lhsT=sel, rhs=a[:, :tn], start=True, stop=True),
                     reads=[a, rt], writes=[pn])
                rc = fin.next()
                S.op("dve", lambda: nc.vector.reciprocal(out=rc[0:64, :tn], in_=pn[0:64, 0, :tn]), reads=[pn], writes=[rc])
                ob = obr.next()
                S.op("dve", lambda: nc.vector.tensor_tensor(out=ob[0:64, :tn], in0=a[0:64, :tn], in1=rc[0:64, :tn], op=ALU.mult),
                     reads=[a, rc], writes=[ob])
                S.dma("sp", dst_ap, ob[0:64, :tn], reads=[ob], writes=[brT_b], par=True)

            def load_k(kb_, p0, r0, n):
                S.dma("sp", kb_[p0:p0 + n, 0:8 * NLAT].rearrange("k (r t) -> k r t", r=NCORES),
                      T.kTgath[:, r0:r0 + n, :].rearrange("r k t -> k r t"), writes=[kb_], par=True)
                S.dma("sp", kb_[p0:p0 + n, 8 * NLAT:], T.kT_s[0][r0:r0 + n, NLAT:NT], writes=[kb_], par=True)

            def load_v(vb_, c0, w):
                for r_ in range(NCORES):
                    S.dma("sp", vb_[:, r_ * 16:(r_ + 1) * 16, 0:w],
                          T.vg3[r_, :, c0:c0 + w].rearrange("(t p) e -> p t e", p=128), writes=[vb_], par=True)
                S.dma("sp", vb_[:, 128:130, 0:w], T.v_s[0][NLAT:NT, c0:c0 + w].rearrange("(t p) e -> p t e", p=128),
                      writes=[vb_], par=True)

            units = [("A", g) for g in range(2)] + [("C", h) for h in range(8)] + [("D", h) for h in range(4)]
            all_kt = list(range(NKT))
            ctx_kt = [128, 129]
            for (kind, u) in units:
                kb_ = kbuf.next()
                vb_ = vbuf.next()
                if kind == "A":
                    load_k(kb_, 0, KA + u * 64, 64)
                    load_k(kb_, 64, KA + u * 64, 64)
                    vc0 = VA + u * 64
                    qtiles = [(QA + (4 * u + 2 * p) * 64, 128) for p in range(2)]
                    vw, scale = 64, 0.125
                elif kind == "C":
                    load_k(kb_, 0, KCN + u * 64, 64)
                    load_k(kb_, 64, KCR, 32)
                    vc0 = VC + u * 64
                    qtiles = [(QC + u * 96, 96)]
                    vw, scale = 64, 96.0 ** -0.5
                else:
                    load_k(kb_, 0, KD + u * 128, 128)
                    vc0 = VD + u * 128
                    qtiles = [(QD + u * 128, 128)]
                    vw, scale = 128, 0.125
                load_v(vb_, vc0, vw)
                for v in vs:
                    need_ctx = (l < DEPTH - 1 and v == 0)
                    groups = GROUPS if need_ctx else GROUPS[:4]
                    qT = T.qT_s[v]
                    brT_d = T.brT_s[v]
                    brT_b = Buf(brT_d, "brT")
                    for qi, (qrow, qrows) in enumerate(qtiles):
                        qh = qr.next()
                        S.dma("sp", qh[0:qrows, :], qT[qrow:qrow + qrows, :], writes=[qh])
                        rd_qk = [kb_, qh]
                        for gi, (t0, tn) in enumerate(groups):
                            kts = ctx_kt if gi == 4 else all_kt
                            nk = len(kts)
                            qs = slice(t0, t0 + tn)
                            if kind == "A":
                                a0, a1 = pacc.next(), pacc.next()
                                steps = []
                                for ii, kt in enumerate(kts):
                                    ks = slice(kt * 128, (kt + 1) * 128)
                                    st_, sp_ = (ii == 0), (ii == nk - 1)
                                    steps.append(([(kb_[0:64, ks], qh[0:64, qs]), (kb_[64:128, ks], qh[64:128, qs])],
                                                  [(a0, vb_[:, kt, :], 0, st_, sp_, [vb_]), (a1, vb_[:, kt, :], 1, st_, sp_, [vb_])]))
                                pair_run(steps, tn, scale, rd_qk)
                                hd = 4 * u + 2 * qi
                                finalize_AC(a0, tn, brT_d[BR_A + hd * 64:BR_A + (hd + 1) * 64, qs])
                                finalize_AC(a1, tn, brT_d[BR_A + (hd + 1) * 64:BR_A + (hd + 2) * 64, qs])
                            elif kind == "C":
                                acc = pacc.next()
                                steps = []
                                for ii in range(0, nk, 2):
                                    k0, k1 = kts[ii], kts[ii + 1]
                                    steps.append(([(kb_[0:96, k0 * 128:(k0 + 1) * 128], qh[0:96, qs]), (kb_[0:96, k1 * 128:(k1 + 1) * 128], qh[0:96, qs])],
                                                  [(acc, vb_[:, k0, :], 0, ii == 0, False, [vb_]), (acc, vb_[:, k1, :], 1, False, ii + 2 >= nk, [vb_])]))
                                pair_run(steps, tn, scale, rd_qk)
                                finalize_AC(acc, tn, brT_d[BR_C + u * 64:BR_C + (u + 1) * 64, qs])
                            else:
                                ao0, as0, ao1, as1 = pacc.next(), pacc.next(), pacc.next(), pacc.next()
                                steps = []
                                for ii, kt in enumerate(kts):
                                    ks = slice(kt * 128, (kt + 1) * 128)
                                    st_, sp_ = (ii == 0), (ii == nk - 1)
                                    steps.append(([(kb_[0:64, ks], qh[0:64, qs]), (kb_[64:128, ks], qh[64:128, qs])],
                                                  [(ao0, vb_[:, kt, :], 0, st_, sp_, [vb_]), (as0, onesb[:], 0, st_, sp_, [onesb]),
                                                   (ao1, vb_[:, kt, :], 1, st_, sp_, [vb_]), (as1, onesb[:], 1, st_, sp_, [onesb])]))
                                pair_run(steps, tn, scale, rd_qk)
                                accs = [(ao0, as0), (ao1, as1)]
                                brow = BR_D + u * 128
                                r0, r1, t0_, t1_ = fin.next(), fin.next(), fin.next(), fin.next()
                                S.op("dve", lambda: nc.vector.reciprocal(out=r0[:, :tn], in_=accs[0][1][:, :tn]), reads=[accs[0][1]], writes=[r0])
                                S.op("dve", lambda: nc.vector.reciprocal(out=r1[:, :tn], in_=accs[1][1][:, :tn]), reads=[accs[1][1]], writes=[r1])
                                S.op("dve", lambda: nc.vector.tensor_tensor(out=t0_[:, :tn], in0=accs[0][0][:, :tn], in1=r0[:, :tn], op=ALU.mult),
                                     reads=[accs[0][0], r0], writes=[t0_])
                                S.op("dve", lambda: nc.vector.scalar_tensor_tensor(out=t1_[:, :tn], in0=accs[1][0][:, :tn], scalar=neg_lam,
                                                                                   in1=r1[:, :tn], op0=ALU.mult, op1=ALU.mult),
                                     reads=[accs[1][0], r1, small], writes=[t1_])
                                S.op("dve", lambda: nc.vector.tensor_tensor(out=t0_[:, :tn], in0=t0_[:, :tn], in1=t1_[:, :tn], op=ALU.add),
                                     reads=[t0_, t1_], writes=[t0_])
                                S.op("act", lambda: nc.scalar.activation(out=r0[:, :tn], in_=t0_[:, :tn], func=AF.Square), reads=[t0_], writes=[r0])
                                pn = pst.next()
                                S.op("pe", lambda: nc.tensor.matmul(pn[:, 0, :tn], lhsT=ones, rhs=r0[:, :tn], start=True, stop=True),
                                     reads=[r0], writes=[pn])
                                S.op("act", lambda: nc.scalar.activation(out=r1[:, :tn], in_=pn[:, 0, :tn], func=AF.Ln, scale=1.0 / 128, bias=epsc),
                                     reads=[pn], writes=[r1])
                                S.op("act", lambda: nc.scalar.activation(out=r1[:, :tn], in_=r1[:, :tn], func=AF.Exp, scale=-0.5),
                                     reads=[r1], writes=[r1])
                                ob = obr.next()
                                S.op("dve", lambda: nc.vector.scalar_tensor_tensor(out=ob[:, :tn], in0=t0_[:, :tn], scalar=gsub, in1=r1[:, :tn],
                                                                                   op0=ALU.mult, op1=ALU.mult),
                                     reads=[t0_, r1, small], writes=[ob])
                                S.dma("sp", brT_d[brow:brow + 128, qs], ob[:, :tn], reads=[ob], writes=[brT_b], par=True)
            S.barrier()

        with ExitStack() as es:
            mask = X.sb(es, "rmask", [128, 4, 128], F32)
            qdec = X.sb(es, "qdec", [64, 8, 128], F32)
            coef = X.sb(es, "coef", [128, 2, NKT], F32)
            Eb = X.sb(es, "retEs", [128, 4, NKT], F32)
            kbg = X.sb(es, "kbg", [128, NKT, 64], BF16)
            vbg = X.sb(es, "vbg", [128, NKT, 128], BF16)
            kw = X.ring(es, "kw", [128, NKT, 64], BF16, 2)
            kbl = X.sb(es, "kbl", [128, 18, 64], BF16)
            vbl = X.sb(es, "vbl", [128, 18, 128], BF16)
            kdl = X.ring(es, "kdl", [128, 18, 64], BF16, 2)
            qh = X.sb(es, "rqh", [64, NT], BF16)
            kh = X.sb(es, "rkh", [64, NT], BF16)
            gh = X.sb(es, "rgh", [128, NT], BF16)
            qd = X.ring(es, "rqd", [64, 2, 128], BF16, 3)
            st = X.ring(es, "rst", [64, 128], F32, 2)
            snap = X.sb(es, "snap", [64, 2, 18, 128], BF16)
            sm = X.ring(es, "rsm", [128, 128], BF16, 3)
            tmpf = X.ring(es, "rtmp", [128, 512], F32, 4)
            obr = X.ring(es, "rob", [128, 512], BF16, 2)
            pU = X.ring(es, "pU", [64, 128], F32, 2, psum=True)
            pS = X.ring(es, "pS", [128, 128], F32, 2, psum=True)
            pO = X.ring(es, "pO", [128, 512], F32, 2, psum=True)
            pN = X.ring(es, "pN2", [128, 512], F32, 1, psum=True)
            for h in range(4):
                for dr in range(2):
                    cc = dr * 4 + h
                    t = tmpf.next()
                    rel = rt[:, RT_RELF + 256 * dr:RT_RELF + 256 * dr + 128]
                    msk = rt[:, RT_MSKF + 256 * dr:RT_MSKF + 256 * dr + 128]
                    S.op("act", lambda: nc.scalar.activation(out=t[:, 0:128], in_=rel, func=AF.Exp, scale=small[:, 8 + cc:9 + cc]),
                         reads=[small], writes=[t])
                    if dr == 0:
                        S.op("dve", lambda: nc.vector.tensor_tensor(out=mask[:, h, :], in0=t[:, 0:128], in1=msk, op=ALU.mult),
                             reads=[t], writes=[mask])
                    else:
                        S.op("dve", lambda: nc.vector.tensor_tensor(out=t[:, 0:128], in0=t[:, 0:128], in1=msk, op=ALU.mult),
                             reads=[t], writes=[t])
                        S.op("dve", lambda: nc.vector.tensor_tensor(out=mask[:, h, :], in0=mask[:, h, :], in1=t[:, 0:128], op=ALU.add),
                             reads=[t, mask], writes=[mask])
                    S.op("act", lambda: nc.scalar.activation(out=qdec[:, cc, :], in_=rt[0:64, RT_QDF + 128 * dr:RT_QDF + 128 * dr + 128],
                                                             func=AF.Exp, scale=small[0:64, 8 + cc:9 + cc]),
                         reads=[small], writes=[qdec])
            for h in range(4):
                for (dstb, c0, w) in ((kbg, VBK + h * 64, 64), (vbg, VB + h * 128, 128)):
                    for r_ in range(NCORES):
                        S.dma("sp", dstb[:, r_ * 16:(r_ + 1) * 16, :],
                              T.vg3[r_, :, c0:c0 + w].rearrange("(t p) e -> p t e", p=128), writes=[dstb], par=True)
                    S.dma("sp", dstb[:, 128:130, :], T.v_s[0][NLAT:NT, c0:c0 + w].rearrange("(t p) e -> p t e", p=128),
                          writes=[dstb], par=True)
                for v in vs:
                    need_ctx = (l < DEPTH - 1 and v == 0)
                    groups = GROUPS if need_ctx else GROUPS[:4]
                    qT, kTl, vl, gT = T.qT_s[v], T.kT_s[v], T.v_s[v], T.gT_s[v]
                    brT_d = T.brT_s[v]
                    brT_b = Buf(brT_d, "brT")
                    S.dma("sp", Eb[:], T.retE[v], writes=[Eb])
                    S.dma("sp", kbl[:], vl[:, VBK + h * 64:VBK + (h + 1) * 64].rearrange("(t p) e -> p t e", p=128), writes=[kbl])
                    S.dma("sp", vbl[:], vl[:, VB + h * 128:VB + (h + 1) * 128].rearrange("(t p) e -> p t e", p=128), writes=[vbl])
                    S.dma("sp", qh[:], qT[QB + h * 64:QB + (h + 1) * 64, :], writes=[qh])
                    S.dma("sp", kh[:], kTl[KB + h * 64:KB + (h + 1) * 64, :], writes=[kh])
                    S.dma("sp", gh[:], gT[h * 128:(h + 1) * 128, :], writes=[gh])
                    for dr in range(2):
                        cc = dr * 4 + h
                        S.op("act", lambda: nc.scalar.activation(out=coef[:, dr, :], in_=Eb[:, 2 * dr, :], func=AF.Exp,
                                                                 scale=small[:, 8 + cc:9 + cc]), reads=[Eb, small], writes=[coef])
                        S.op("dve", lambda: nc.vector.tensor_tensor(out=coef[:, dr, :], in0=coef[:, dr, :], in1=Eb[:, 2 * dr + 1, :], op=ALU.mult),
                             reads=[Eb, coef], writes=[coef])
                        kw_ = kw.next()
                        S.op("dve", lambda: nc.vector.tensor_tensor(out=kw_[:], in0=kbg[:], in1=coef[:, dr, :].unsqueeze(2).to_broadcast([128, NKT, 64]),
                                                                    op=ALU.mult), reads=[kbg, coef], writes=[kw_])
                        pu = pU.next()
                        for t in range(NKT):
                            S.op("pe", lambda: nc.tensor.matmul(pu[:], lhsT=kw_[:, t, :], rhs=vbg[:, t, :], start=(t == 0), stop=(t == NKT - 1)),
                                 reads=[kw_, vbg], writes=[pu])
                        s_ = st.next()
                        S.op("dve", lambda: nc.vector.tensor_copy(out=s_[:], in_=pu[:]), reads=[pu], writes=[s_])
                        kd_ = kdl.next()
                        S.op("dve", lambda: nc.vector.tensor_scalar(out=kd_[:], in0=kbl[:], scalar1=small[:, 24 + cc:25 + cc], scalar2=None,
                                                                    op0=ALU.mult), reads=[kbl, small], writes=[kd_])
                        order = list(range(16)) if dr == 0 else list(range(15, -1, -1))
                        cdc = small[0:64, 16 + cc:17 + cc]
                        for i in order:
                            S.op("act", lambda: nc.scalar.copy(out=snap[:, dr, i, :], in_=s_[:]), reads=[s_], writes=[snap])
                            pu = pU.next()
                            S.op("pe", lambda: nc.tensor.matmul(pu[:], lhsT=kd_[:, i, :], rhs=vbl[:, i, :], start=True, stop=True),
                                 reads=[kd_, vbl], writes=[pu])
                            S.op("dve", lambda: nc.vector.scalar_tensor_tensor(out=s_[:], in0=s_[:], scalar=cdc, in1=pu[:],
                                                                               op0=ALU.mult, op1=ALU.add), reads=[s_, pu, small], writes=[s_])
                        if need_ctx:
                            first, second = (16, 17) if dr == 0 else (17, 16)
                            S.op("pool", lambda: nc.gpsimd.memset(snap[:, dr, first, :], 0.0), writes=[snap])
                            pu = pU.next()
                            S.op("pe", lambda: nc.tensor.matmul(pu[:], lhsT=kd_[:, first, :], rhs=vbl[:, first, :], start=True, stop=True),
                                 reads=[kd_, vbl], writes=[pu])
                            S.op("act", lambda: nc.scalar.copy(out=snap[:, dr, second, :], in_=pu[:]), reads=[pu], writes=[snap])
                    for gi, (t0, tn) in enumerate(groups):
                        po = pO.next()
                        for s in range(tn // 128):
                            i = t0 // 128 + s
                            ts_ = slice(i * 128, (i + 1) * 128)
                            psc = pS.next()
                            S.op("pe", lambda: nc.tensor.matmul(psc[:], lhsT=kh[:, ts_], rhs=qh[:, ts_], start=True, stop=True),
                                 reads=[kh, qh], writes=[psc])
                            sm_ = sm.next()
                            S.op("dve", lambda: nc.vector.tensor_tensor(out=sm_[:], in0=psc[:], in1=mask[:, h, :], op=ALU.mult),
                                 reads=[psc, mask], writes=[sm_])
                            qd_ = qd.next()
                            for dr in range(2):
                                S.op("pool", lambda: nc.gpsimd.tensor_tensor(out=qd_[:, dr, :], in0=qh[:, ts_], in1=qdec[:, dr * 4 + h, :], op=ALU.mult),
                                     reads=[qh, qdec], writes=[qd_])
                            S.op("pe", lambda: nc.tensor.matmul(po[:, s * 128:(s + 1) * 128], lhsT=vbl[:, i, :], rhs=sm_[:], start=True, stop=False),
                                 reads=[vbl, sm_], writes=[po])
                            for dr in range(2):
                                S.op("pe", lambda: nc.tensor.matmul(po[:, s * 128:(s + 1) * 128], lhsT=snap[:, dr, i, :], rhs=qd_[:, dr, :],
                                                                    start=False, stop=(dr == 1)), reads=[snap, qd_], writes=[po])
                        o_ = tmpf.next()
                        S.op("act", lambda: nc.scalar.copy(out=o_[:, :tn], in_=po[:, :tn]), reads=[po], writes=[o_])
                        q_ = tmpf.next()
                        S.op("act", lambda: nc.scalar.activation(out=q_[:, :tn], in_=po[:, :tn], func=AF.Square), reads=[po], writes=[q_])
                        pn = pN.next()
                        S.op("pe", lambda: nc.tensor.matmul(pn[:, :tn], lhsT=ones, rhs=q_[:, :tn], start=True, stop=True), reads=[q_], writes=[pn])
                        S.op("act", lambda: nc.scalar.activation(out=q_[:, :tn], in_=pn[:, :tn], func=AF.Ln, scale=1.0 / 128, bias=epsc),
                             reads=[pn], writes=[q_])
                        S.op("act", lambda: nc.scalar.activation(out=q_[:, :tn], in_=q_[:, :tn], func=AF.Exp, scale=-0.5), reads=[q_], writes=[q_])
                        S.op("dve", lambda: nc.vector.tensor_tensor(out=o_[:, :tn], in0=o_[:, :tn], in1=q_[:, :tn], op=ALU.mult),
                             reads=[o_, q_], writes=[o_])
                        ob = obr.next()
                        S.op("dve", lambda: nc.vector.tensor_tensor(out=ob[:, :tn], in0=o_[:, :tn], in1=gh[:, t0:t0 + tn], op=ALU.mult),
                             reads=[o_, gh], writes=[ob])
                        S.dma("sp", brT_d[BR_B + h * 128:BR_B + (h + 1) * 128, t0:t0 + tn], ob[:, :tn], reads=[ob], writes=[brT_b], par=True)
            S.barrier()

        with ExitStack() as es:
            wb = X.sb(es, "wbr", [128, 16, D], BF16)
            wo = X.sb(es, "wo", [128, 8, D], BF16)
            S.dma("pool", wb[:], w_branch.rearrange("(c p) n -> p c n", p=128), writes=[wb])
            S.dma("pool", wo[:], w_out.rearrange("(c p) n -> p c n", p=128), writes=[wo])
            brg = X.sb(es, "brg", [128, 16, 512], BF16)
            gtg = X.sb(es, "gtg", [128, 32, 512], BF16)
            xg = X.sb(es, "xg3", [128, 8, 512], F32)
            x1g = X.sb(es, "x1g", [128, 8, 512], F32)
            mg = X.sb(es, "mg", [128, 8, 512], BF16)
            h2g = X.sb(es, "h2g", [128, 8, 512], BF16)
            macc = X.ring(es, "macc", [128, 512], F32, 2)
            tmp = X.ring(es, "tmp3", [128, 512], F32, 4)
            tmp.rstd = X.sb(es, "rstd3", [128, 512], F32)
            sqr = X.ring(es, "sq3", [128, 512], F32, 2)
            pp = X.ring(es, "pp3", [128, 512], F32, 4, psum=True)
            pms_r = X.ring(es, "pms3", [128, 512], F32, 2, psum=True)
            for v in vs:
                need_ctx = (l < DEPTH - 1 and v == 0)
                groups = GROUPS if need_ctx else GROUPS[:4]
                gT = T.gT_s[v]
                brT_d, x1_d, h2_d = T.brT_s[v], T.x1T_s[v], T.h2T_s[v]
                brT_b, x1_b, h2_b = Buf(brT_d, "brT"), Buf(x1_d, "x1"), Buf(h2_d, "h2")
                for gi, (t0, tn) in enumerate(groups):
                    jcol = 1 if gi == 4 else 0
                    S.dma("sp", brg[:, :, :tn], brT_d[:, t0:t0 + tn].rearrange("(c p) t -> p c t", p=128), reads=[brT_b], writes=[brg])
                    S.dma("sp", gtg[:, :, :tn], gT[512:, t0:t0 + tn].rearrange("(c p) t -> p c t", p=128), writes=[gtg])
                    S.dma("sp", xg[:, :, :tn], xci(t0, tn).rearrange("(c p) t -> p c t", p=128), writes=[xg])
                    for ob in range(8):
                        m_ = macc.next()
                        for n in range(4):
                            p_ = pp.next()
                            for kc in range(4):
                                S.op("pe", lambda: nc.tensor.matmul(p_[:, :tn], lhsT=wb[:, n * 4 + kc, ob * 128:(ob + 1) * 128],
                                                                    rhs=brg[:, n * 4 + kc, :tn], start=(kc == 0), stop=(kc == 3)),
                                     reads=[wb, brg], writes=[p_])
                            if n == 0:
                                S.op("dve", lambda: nc.vector.tensor_tensor(out=m_[:, :tn], in0=p_[:, :tn], in1=gtg[:, n * 8 + ob, :tn], op=ALU.mult),
                                     reads=[p_, gtg], writes=[m_])
                            else:
                                t_ = tmp.next()
                                S.op("dve", lambda: nc.vector.tensor_tensor(out=t_[:, :tn], in0=p_[:, :tn], in1=gtg[:, n * 8 + ob, :tn], op=ALU.mult),
                                     reads=[p_, gtg], writes=[t_])
                                if n < 3:
                                    S.op("pool", lambda: nc.gpsimd.tensor_tensor(out=m_[:, :tn], in0=m_[:, :tn], in1=t_[:, :tn], op=ALU.add),
                                         reads=[m_, t_], writes=[m_])
                                else:
                                    S.op("pool", lambda: nc.gpsimd.tensor_tensor(out=mg[:, ob, :tn], in0=m_[:, :tn], in1=t_[:, :tn], op=ALU.add),
                                         reads=[m_, t_], writes=[mg])
                    for ob in range(8):
                        p_ = pp.next()
                        for kc in range(8):
                            S.op("pe", lambda: nc.tensor.matmul(p_[:, :tn], lhsT=wo[:, kc, ob * 128:(ob + 1) * 128], rhs=mg[:, kc, :tn],
                                                                start=(kc == 0), stop=(kc == 7)), reads=[wo, mg], writes=[p_])
                        S.op("dve", lambda: nc.vector.scalar_tensor_tensor(out=x1g[:, ob, :tn], in0=p_[:, :tn], scalar=modT[:, 16 + ob, jcol:jcol + 1],
                                                                           in1=xg[:, ob, :tn], op0=ALU.mult, op1=ALU.add),
                             reads=[p_, modT, xg], writes=[x1g])
                    S.dma("sp", x1_d[:, t0:t0 + tn].rearrange("(c p) t -> p c t", p=128), x1g[:, :, :tn], reads=[x1g], writes=[x1_b], par=True)
                    emit_norm_mod(X, nc, S, x1g, tn, jcol, Gm2, _Shift(modT, 24), ones, epsc,
                                  sqr, pms_r, tmp, h2g)
                    S.dma("sp", h2_d[:, t0:t0 + tn].rearrange("(c p) t -> p c t", p=128), h2g[:, :, :tn], reads=[h2g], writes=[h2_b], par=True)
            S.barrier()
        with ExitStack() as es:
            wfi = X.sb(es, "wfi", [128, 8, 2 * FFN], BF16)
            wfo = X.sb(es, "wfo", [128, 22, D], BF16)
            for c in range(8):
                S.dma("pool", wfi[:, c, :], w_fi[c * 128:(c + 1) * 128, :], writes=[wfi], par=True)
            S.dma("pool", wfo[:], w_fo.rearrange("(c p) n -> p c n", p=128), writes=[wfo])
            TG = 256
            h2r = X.ring(es, "h2r", [128, 8, TG], BF16, 2)
            x1r = X.ring(es, "x1r", [128, 8, TG], F32, 2)
            x2r = X.ring(es, "x2r", [128, 8, TG], F32, 2)
            act = X.sb(es, "actT", [128, 22, TG], BF16)
            sgr = X.ring(es, "sgr", [128, TG], F32, 3)
            pg_r = X.ring(es, "pg", [128, 512], F32, 3, psum=True)
            pu_r = X.ring(es, "pu", [128, 512], F32, 3, psum=True)
            po_r = X.ring(es, "po4", [128, 512], F32, 2, psum=True)
            for v in vs:
                need_ctx = (l < DEPTH - 1 and v == 0)
                x1_d, h2_d = T.x1T_s[v], T.h2T_s[v]
                x1_b, h2_b = Buf(x1_d, "x1"), Buf(h2_d, "h2")
                ntok = NT if need_ctx else NLAT
                for t0 in range(0, ntok, TG):
                    jcol = 1 if t0 >= NLAT else 0
                    h2 = h2r.next()
                    x1 = x1r.next()
                    x2 = x2r.next()
                    S.dma("sp", h2[:], h2_d[:, t0:t0 + TG].rearrange("(c p) t -> p c t", p=128), reads=[h2_b], writes=[h2])
                    S.dma("sp", x1[:], x1_d[:, t0:t0 + TG].rearrange("(c p) t -> p c t", p=128), reads=[x1_b], writes=[x1])
                    for fb in range(22):
                        pg, pu = pg_r.next(), pu_r.next()
                        for kc in range(8):
                            S.op("pe", lambda: nc.tensor.matmul(pg[:, :TG], lhsT=wfi[:, kc, fb * 128:(fb + 1) * 128], rhs=h2[:, kc, :],
                                                                start=(kc == 0), stop=(kc == 7)), reads=[wfi, h2], writes=[pg])
                        for kc in range(8):
                            S.op("pe", lambda: nc.tensor.matmul(pu[:, :TG], lhsT=wfi[:, kc, FFN + fb * 128:FFN + (fb + 1) * 128], rhs=h2[:, kc, :],
                                                                start=(kc == 0), stop=(kc == 7)), reads=[wfi, h2], writes=[pu])
                        sg = sgr.next()
                        S.op("act", lambda: nc.scalar.activation(out=sg[:], in_=pg[:, :TG], func=AF.Silu), reads=[pg], writes=[sg])
                        S.op("dve", lambda: nc.vector.tensor_tensor(out=act[:, fb, :], in0=pu[:, :TG], in1=sg[:], op=ALU.mult),
                             reads=[pu, sg], writes=[act])
                    for ob in range(8):
                        po = po_r.next()
                        for fb in range(22):
                            S.op("pe", lambda: nc.tensor.matmul(po[:, :TG], lhsT=wfo[:, fb, ob * 128:(ob + 1) * 128], rhs=act[:, fb, :],
                                                                start=(fb == 0), stop=(fb == 21)), reads=[wfo, act], writes=[po])
                        S.op("dve", lambda: nc.vector.scalar_tensor_tensor(out=x2[:, ob, :], in0=po[:, :TG], scalar=modT[:, 40 + ob, jcol:jcol + 1],
                                                                           in1=x1[:, ob, :], op0=ALU.mult, op1=ALU.add),
                             reads=[po, modT, x1], writes=[x2])
                    S.dma("sp", xco(t0, TG).rearrange("(c p) t -> p c t", p=128), x2[:], reads=[x2])
        S.barrier()


def make_consts():
    c = np.zeros((128, NCONST), np.float32)
    c[:, C_ONES:C_ONES + 128] = 1.0
    for b in range(2):
        c[b * 64:(b + 1) * 64, C_BD64 + b * 64:C_BD64 + (b + 1) * 64] = 1.0
    c[0:64, C_BD96:C_BD96 + 64] = 1.0
    c[64:96, C_BD96 + 64:C_BD96 + 96] = 1.0
    for p in range(128):
        c[p, C_PSW + (p ^ 1)] = 1.0
        c[p, C_ID + p] = 1.0
    return c


def make_rope_tables(core):
    s = core * NLAT + np.arange(NLAT)
    row = (s // 64).astype(np.float64)
    col = (s % 64).astype(np.float64)

    def ang(dim):
        q = dim // 4
        f = 10000.0 ** (-(np.arange(q, dtype=np.float32) / np.float32(q))).astype(np.float32)
        f = f.astype(np.float32)
        a = np.concatenate([row[:, None].astype(np.float32) * f[None, :], col[:, None].astype(np.float32) * f[None, :]], -1)
        return a.astype(np.float32)
    out = np.zeros((6, 128, NT), np.float32)
    out[0::2, :, :] = 1.0
    a64 = ang(64)
    a32 = ang(32)
    sign = np.where(np.arange(128) % 2 == 0, -1.0, 1.0).astype(np.float32)
    p = np.arange(128)
    idx64 = (p % 64) // 2
    out[0, :, :NLAT] = np.cos(a64)[:, idx64].T
    out[1, :, :NLAT] = (np.sin(a64)[:, idx64] * sign[None, :]).T
    p32 = np.arange(32)
    idx32 = p32 // 2
    out[2, 64:96, :NLAT] = np.cos(a32)[:, idx32].T
    out[3, 64:96, :NLAT] = (np.sin(a32)[:, idx32] * sign[None, :32]).T
    out[4, 0:32, :NLAT] = np.cos(a32)[:, idx32].T
    out[5, 0:32, :NLAT] = (np.sin(a32)[:, idx32] * sign[None, :32]).T
    return out


def tile_col(v, n=128):
    v = np.asarray(v, np.float32).reshape(-1)
    reps = -(-n // v.size)
    return np.tile(v, reps)[:n] if v.size <= n and n % v.size == 0 else np.pad(v, (0, n - v.size))


def make_gvec(inp, l):
    g = np.zeros((128, NGV), np.float32)
    g[:, GV_AQ] = tile_col(inp["gqa_qk_gain"][l, 0])
    g[:, GV_AK] = tile_col(inp["gqa_qk_gain"][l, 1])
    g[:, GV_DQ] = tile_col(inp["diff_qk_gain"][l, 0])
    g[:, GV_DK] = tile_col(inp["diff_qk_gain"][l, 1])
    g[:, GV_CQ:GV_CQ + 3] = inp["mla_cq_gain"][l].reshape(3, 128).T
    g[:, GV_CKV:GV_CKV + 2] = inp["mla_ckv_gain"][l].reshape(2, 128).T
    g[:96, GV_MQ] = inp["mla_qk_gain"][l, 0]
    g[:, GV_MKN] = tile_col(inp["mla_qk_gain"][l, 1, :64])
    g[:32, GV_MKR] = inp["mla_qk_gain"][l, 1, 64:]
    g[:64, GV_SC96] = 1.0 / 64
    g[64:96, GV_SC96] = 1.0 / 32
    g[:, GV_EPS] = EPS
    g[:, GV_SUBLN] = inp["diff_subln_gain"][l]
    return g


class _Shift:
    def __init__(self, buf, base):
        self.buf = buf
        self.base = base
        self.lw = buf.lw
        self.rd = buf.rd

    def __getitem__(self, idx):
        p, c, j = idx
        return self.buf.t[p, self.base + c, j]


def make_rt():
    r = np.zeros((128, NRT), np.float32)
    j = np.arange(128)[:, None].astype(np.float32)
    i = np.arange(128)[None, :].astype(np.float32)
    r[:, RT_RELF:RT_RELF + 128] = np.maximum(i - j, 0)
    r[:, RT_MSKF:RT_MSKF + 128] = (i >= j)
    r[:, RT_RELB:RT_RELB + 128] = np.maximum(j - i, 0)
    r[:, RT_MSKB:RT_MSKB + 128] = (j >= i)
    r[:, RT_QDF:RT_QDF + 128] = i + 1.0
    r[:, RT_QDB:RT_QDB + 128] = 128.0 - i
    r[:, RT_KDF] = 127.0 - np.arange(128)
    r[:, RT_KDB] = np.arange(128)
    for k in range(64):
        r[64 + k, RT_SEL + k] = 1.0
    return r


def make_retE(core, rot=0):
    Eg = _make_retE_global(core)
    perm = [((rot + u // 16) % NCORES) * 16 + u % 16 for u in range(128)] + [128, 129]
    return np.ascontiguousarray(Eg[:, :, perm])


def _make_retE_global(core):
    E = np.zeros((128, 4, NKT), np.float32)
    j = np.arange(128).astype(np.float32)
    T0 = core * 16
    pos = np.zeros(NKT)
    pos[128], pos[129] = 0, 1
    pos[:128] = 2 + np.arange(128)
    P0 = 2 + T0
    for u in range(NKT):
        if pos[u] < P0:
            E[:, 0, u] = 127.0 - j + 128.0 * (P0 - 1 - pos[u])
            E[:, 1, u] = 1.0
    pos[129], pos[128] = 0, 1
    pos[:128] = 2 + 127 - np.arange(128)
    P0 = 2 + 127 - (T0 + 15)
    for u in range(NKT):
        if pos[u] < P0:
            E[:, 2, u] = j + 128.0 * (P0 - 1 - pos[u])
            E[:, 3, u] = 1.0
    return E


class _NS:
    pass


def build_fused():
    nc = bass.Bass("TRN2", target_bir_lowering=False, num_devices=NCORES)
    di = lambda name, shape, dt=F32: nc.dram_tensor(name, shape, dt, kind="ExternalInput").ap()
    scr = lambda name, shape, dt=BF16: nc.dram_tensor(name, shape, dt, kind="Internal").ap()
    T = _NS()
    xT_all = di("xT_all", [D, NT])
    cT = di("cT", [D, 2])
    w_mod = di("w_mod", [DEPTH, D, 6 * D])
    bmodT = di("bmodT", [DEPTH, 128, 48])
    T.g0T = di("g0T", [DEPTH, 128, 8])
    T.g1T = di("g1T", [DEPTH, 128, 8])
    T.w_in = di("w_in", [DEPTH, D, IN_W])
    T.w_uq = di("w_uq", [DEPTH, 384, 768])
    T.w_ukv = di("w_ukv", [DEPTH, 256, 1024])
    T.gvec = di("gvec", [DEPTH, 128, NGV])
    T.consts = di("consts", [128, NCONST])
    T.rope = di("rope", [1, 6, 128, NT])
    T.w_branch = di("w_branch", [DEPTH, 2048, D])
    T.w_out = di("w_out", [DEPTH, D, D])
    T.w_fi = di("w_fi", [DEPTH, D, 2 * FFN])
    T.w_fo = di("w_fo", [DEPTH, FFN, D])
    T.rt = di("rt", [128, NRT])
    T.retE = di("retE", [1, 128, 4, NKT])
    T.retd = di("retd", [DEPTH, 128, 8])
    T.dlam = di("dlam", [DEPTH, 128, 256])
    out = nc.dram_tensor("xTo", [D, NLAT], F32, kind="ExternalOutput").ap()
    x1_all = scr("x1_all", [D, NT], F32)
    T.xin = [xT_all, x1_all]
    T.xout = [x1_all, out]
    T.modT_s = [scr("modT_s%d" % l, [128, 96], F32) for l in range(DEPTH)]
    T.qT_s = [scr("qT_s%d" % v, [NQ, NT]) for v in range(1)]
    T.kT_s = [scr("kT_s%d" % v, [NK, NT]) for v in range(1)]
    T.v_s = [scr("v_s%d" % v, [NT, NV]) for v in range(1)]
    T.gT_s = [scr("gT_s%d" % v, [NG, NT]) for v in range(1)]
    T.hT_s = [scr("hT_s%d" % v, [D, NT]) for v in range(1)]
    NKS, NTOT = NK * NLAT, NK * NLAT + NLAT * NV
    send = [scr("kvsend%d" % l, [1, NTOT]) for l in range(DEPTH)]
    gath = [scr("kvgath%d" % l, [NCORES, NTOT]) for l in range(DEPTH)]
    T.brT_s = [scr("brT_s%d" % v, [2048, NT]) for v in range(1)]
    T.x1T_s = [scr("x1T_s%d" % v, [D, NT], F32) for v in range(1)]
    T.h2T_s = [scr("h2T_s%d" % v, [D, NT]) for v in range(1)]
    with ExitStack() as esr:
        X = Ctx(nc, esr)
        S = X.S
        for l in range(DEPTH):
            with ExitStack() as esm:
                dummy = X.sb(esm, "modkeep", [128, 1], F32)
                with ExitStack() as est:
                    emit_modvec(X, esm, est, cT, w_mod[l], bmodT[l], T.modT_s[l])
                    S.barrier()
            emit_A(X, T, l, [0])
            S.dma("sp", send[l][:, 0:NKS].rearrange("o (k t) -> (o k) t", k=NK), T.kT_s[0][:, 0:NLAT])
            S.dma("sp", send[l][:, NKS:NTOT].rearrange("o (t e) -> (o t) e", e=NV), T.v_s[0][0:NLAT, :])
            S.barrier()
            S.op("pool", lambda: nc.gpsimd.collective_compute("AllGather", ALU.bypass, replica_groups=[list(range(NCORES))],
                                                              ins=[send[l]], outs=[gath[l]], unique_tensors="No"))
            S.barrier()
            T.kTgath = gath[l][:, 0:NKS].rearrange("r (k t) -> r k t", k=NK)
            T.vg3 = gath[l][:, NKS:NTOT].rearrange("r (t e) -> r t e", e=NV)
            emit_B(X, T, l, [0])
        S.finish("sp")
        print("fused: instructions", S.nins, "waits", S.nwaits)
    return nc


def inputs_fused(inp):
    consts = make_consts()
    rtab = make_rt()
    x = inp["x"][0]
    ctx = inp["ctx"][0]
    c_cols = np.ascontiguousarray(np.stack([inp["c"][0], inp["c_ctx"]], 1))
    shared = {
        "cT": c_cols,
        "w_mod": np.ascontiguousarray(inp["w_mod"]),
        "bmodT": np.ascontiguousarray(inp["b_mod"].reshape(DEPTH, 48, 128).transpose(0, 2, 1)),
        "g0T": np.ascontiguousarray(inp["norm_gain"][:, 0].reshape(DEPTH, 8, 128).transpose(0, 2, 1)),
        "g1T": np.ascontiguousarray(inp["norm_gain"][:, 1].reshape(DEPTH, 8, 128).transpose(0, 2, 1)),
        "w_in": np.ascontiguousarray(inp["w_in"]),
        "w_uq": np.ascontiguousarray(inp["mla_w_uq"]),
        "w_ukv": np.ascontiguousarray(inp["mla_w_ukv"]),
        "gvec": np.stack([make_gvec(inp, l) for l in range(DEPTH)]),
        "consts": consts,
        "w_branch": np.ascontiguousarray(inp["w_branch"].reshape(DEPTH, 2048, D)),
        "w_out": np.ascontiguousarray(inp["w_out"]),
        "w_fi": np.ascontiguousarray(inp["w_ffn_in"]),
        "w_fo": np.ascontiguousarray(inp["w_ffn_out"]),
        "rt": rtab,
        "retd": np.ascontiguousarray(np.tile(inp["ret_decay"].reshape(DEPTH, 1, 8), (1, 128, 1))),
        "dlam": np.ascontiguousarray(np.tile(inp["diff_lambda"].reshape(DEPTH, 1, 256), (1, 128, 1))),
    }
    maps = []
    for r in range(NCORES):
        m = dict(shared)
        m["xT_all"] = np.ascontiguousarray(np.concatenate([x[r * NLAT:(r + 1) * NLAT], ctx], 0).T)
        m["rope"] = make_rope_tables(r)[None]
        m["retE"] = make_retE(r)[None]
        maps.append(m)
    return maps


_PROG = {}


def kernel(**inp):
    inp = {k: np.asarray(v) for k, v in inp.items()}
    if "f" not in _PROG:
        _PROG["f"] = build_fused()
    res = run_bass_kernel_spmd(_PROG["f"], inputs_fused(inp), core_ids=list(range(NCORES)))
    out = np.concatenate([np.asarray(res.results[r]["xTo"]).T for r in range(NCORES)], 0)[None]
    return np.ascontiguousarray(out.astype(np.float32))
```
